# Optimizing a Trainium2 kernel written in Bass

```python
import math
import jax, jax.numpy as jnp
from jax import lax
import numpy as np

D_MODEL = 1024
BATCH = 8
SEQ = 4096
DEPTH = 2
DEC_BATCH = 128
DEC_SEQ = 4
PAST_LEN = 16384
PAGE_SIZE = 128

N_EVEN = (DEPTH + 1) // 2
N_ODD = DEPTH // 2
MLA_HEADS = 8
MLA_NOPE = 64
MLA_ROPE = 32
MLA_QK = MLA_NOPE + MLA_ROPE
MLA_V = 64
MLA_Q_LORA = 768
MLA_KV_LORA = 256
ROPE_THETA = 10000.0
Q_BLOCK = 128
S5_WIDTH = 512
S5_GROUP = 16
S5_GROUPS = S5_WIDTH // S5_GROUP
S5_STATE = 64
EVEN_IN = MLA_Q_LORA + MLA_KV_LORA + MLA_ROPE + S5_WIDTH
EVEN_OUT = MLA_HEADS * MLA_V + S5_WIDTH
HGRN_WIDTH = D_MODEL
HGRN_HEADS = 8
HGRN_DK = HGRN_WIDTH // HGRN_HEADS
HGRN_DV = HGRN_DK
HGRN_CHUNK = 64
MEM_LEN = 256
MEM_HEADS = 4
MEM_HEAD_DIM = 128
MEM_WIDTH = MEM_HEADS * MEM_HEAD_DIM
D_FF = 4 * D_MODEL
EPS = 1e-6

kernel_name = 'hybrid_mla_s5_hgrn2_decoder_step'


def rms_norm(x, g):
    xf = x.astype(jnp.float32)
    y = xf * lax.rsqrt(jnp.mean(xf * xf, axis=-1, keepdims=True) + EPS)
    return (y * g.astype(jnp.float32)).astype(x.dtype)


def rope(x, pos):
    half = x.shape[-1] // 2
    inv = ROPE_THETA ** (-jnp.arange(half, dtype=jnp.float32) / half)
    ang = pos.astype(jnp.float32)[:, None] * inv[None, :]
    cos = jnp.cos(ang)[None, :, None, :]
    sin = jnp.sin(ang)[None, :, None, :]
    xf = x.astype(jnp.float32)
    x1, x2 = xf[..., :half], xf[..., half:]
    return jnp.concatenate([x1 * cos - x2 * sin, x2 * cos + x1 * sin], axis=-1).astype(x.dtype)


def mla_keys(c_kv, k_rope, w_ukv, k_gain):
    b, t, _ = c_kv.shape
    kv = (c_kv @ w_ukv).reshape(b, t, MLA_HEADS, MLA_NOPE + MLA_V)
    kr = jnp.broadcast_to(k_rope[:, :, None, :], (b, t, MLA_HEADS, MLA_ROPE)).astype(kv.dtype)
    k = jnp.concatenate([kv[..., :MLA_NOPE], kr], axis=-1)
    return rms_norm(k, k_gain), kv[..., MLA_NOPE:]


def causal_block_attention(q, k, v):
    b, s, h, dq = q.shape
    qb = min(Q_BLOCK, s)
    nb = s // qb
    q_blocks = q.reshape(b, nb, qb, h, dq).swapaxes(0, 1)
    kpos = jnp.arange(s)

    def block(args):
        q_i, start = args
        qpos = start + jnp.arange(qb)
        sc = jnp.einsum('bqhd,bkhd->bhqk', q_i, k).astype(jnp.float32) * (MLA_QK ** -0.5)
        sc = jnp.where(qpos[:, None] >= kpos[None, :], sc, -jnp.inf)
        p = jax.nn.softmax(sc, axis=-1).astype(v.dtype)
        return jnp.einsum('bhqk,bkhd->bqhd', p, v)

    o = lax.map(block, (q_blocks, jnp.arange(nb) * qb))
    return o.swapaxes(0, 1).reshape(b, s, h, v.shape[-1])


def attn_stats(q, k, v, mask):
    sc = jnp.einsum('bqhd,bkhd->bhqk', q, k).astype(jnp.float32) * (MLA_QK ** -0.5)
    if mask is not None:
        sc = jnp.where(mask, sc, -jnp.inf)
    m = jnp.max(sc, axis=-1)
    p = jnp.exp(sc - m[..., None])
    return m, jnp.sum(p, axis=-1), jnp.einsum('bhqk,bkhd->bhqd', p, v.astype(jnp.float32))


def paged_mla_attention(q, ckv_new, kr_new, cache_lat, cache_kr, page_table, e, w_ukv, k_gain):
    def page_stats(phys):
        k, v = mla_keys(cache_lat[phys, e], cache_kr[phys, e], w_ukv, k_gain)
        return attn_stats(q, k, v, None)

    m_pg, l_pg, a_pg = lax.map(page_stats, page_table.T)
    k_n, v_n = mla_keys(ckv_new, kr_new, w_ukv, k_gain)
    sq = q.shape[1]
    m_n, l_n, a_n = attn_stats(q, k_n, v_n, jnp.tril(jnp.ones((sq, sq), bool)))
    m_all = jnp.concatenate([m_pg, m_n[None]], axis=0)
    l_all = jnp.concatenate([l_pg, l_n[None]], axis=0)
    a_all = jnp.concatenate([a_pg, a_n[None]], axis=0)
    m_max = jnp.max(m_all, axis=0)
    w = jnp.exp(m_all - m_max)
    o = jnp.sum(a_all * w[..., None], axis=0) / jnp.sum(l_all * w, axis=0)[..., None]
    return o.transpose(0, 2, 1, 3).astype(q.dtype)


def _cplx_affine_combine(e1, e2):
    a1r, a1i, b1r, b1i = e1
    a2r, a2i, b2r, b2i = e2
    return (a2r * a1r - a2i * a1i, a2r * a1i + a2i * a1r,
            a2r * b1r - a2i * b1i + b2r, a2r * b1i + a2i * b1r + b2i)


def s5_ssm(u, h0_re, h0_im, p):
    f32 = jnp.float32
    bsz, L, _ = u.shape
    lr = jnp.minimum(p['lam_re'].astype(f32), -1e-4)
    li = p['lam_im'].astype(f32)
    dt = jnp.exp(p['log_step'].astype(f32))[:, None]
    mag = jnp.exp(lr * dt)
    ab_re, ab_im = mag * jnp.cos(li * dt), mag * jnp.sin(li * dt)
    den = lr * lr + li * li
    co_re = ((ab_re - 1.0) * lr + ab_im * li) / den
    co_im = (ab_im * lr - (ab_re - 1.0) * li) / den
    br, bi = p['b_re'].astype(f32), p['b_im'].astype(f32)
    bb_re = co_re[..., None] * br - co_im[..., None] * bi
    bb_im = co_re[..., None] * bi + co_im[..., None] * br
    ug = u.astype(f32).reshape(bsz, L, S5_GROUPS, S5_GROUP)
    x_re = jnp.einsum('blgc,gnc->blgn', ug, bb_re)
    x_im = jnp.einsum('blgc,gnc->blgn', ug, bb_im)
    h0r, h0i = h0_re.astype(f32), h0_im.astype(f32)
    x_re = x_re.at[:, 0].add(ab_re * h0r - ab_im * h0i)
    x_im = x_im.at[:, 0].add(ab_re * h0i + ab_im * h0r)
    a_re = jnp.broadcast_to(ab_re, x_re.shape)
    a_im = jnp.broadcast_to(ab_im, x_im.shape)
    _, _, h_re, h_im = lax.associative_scan(_cplx_affine_combine, (a_re, a_im, x_re, x_im), axis=1)
    y = (jnp.einsum('blgn,gcn->blgc', h_re, p['c_re'].astype(f32))
         - jnp.einsum('blgn,gcn->blgc', h_im, p['c_im'].astype(f32)))
    y = y.reshape(bsz, L, S5_WIDTH) + p['d'].astype(f32) * u.astype(f32)
    z = jax.nn.gelu(y)
    out = z * jax.nn.sigmoid(z @ p['w_glu'].astype(f32) + p['b_glu'].astype(f32))
    return out.astype(u.dtype), h_re[:, -1], h_im[:, -1]


def even_mixer(hn, pos, h0_re, h0_im, attend, p):
    b, L, _ = hn.shape
    proj = hn @ p['w_in']
    o1 = MLA_Q_LORA
    o2 = o1 + MLA_KV_LORA
    o3 = o2 + MLA_ROPE
    c_q, c_kv, k_r, u = proj[..., :o1], proj[..., o1:o2], proj[..., o2:o3], proj[..., o3:]
    cq = rms_norm(c_q, p['cq_norm'])
    qf = rms_norm((cq @ p['w_uq']).reshape(b, L, MLA_HEADS, MLA_QK), p['q_gain'])
    q = jnp.concatenate([qf[..., :MLA_NOPE], rope(qf[..., MLA_NOPE:], pos)], axis=-1)
    ckv = rms_norm(c_kv, p['ckv_norm'])
    kr = rope(k_r[:, :, None, :], pos)[:, :, 0]
    o_att = attend(q, ckv, kr).reshape(b, L, MLA_HEADS * MLA_V)
    o_s5, hr, hi = s5_ssm(u, h0_re, h0_im, p)
    out = jnp.concatenate([o_att, o_s5.astype(o_att.dtype)], axis=-1) @ p['w_out']
    return out, ckv, kr, hr, hi


def hgrn_recurrence(q, k, v, logf, s0):
    bsz, L, H, _ = q.shape
    C = min(HGRN_CHUNK, L)
    nc = -(-L // C)
    pad = nc * C - L

    def prep(t):
        t = jnp.pad(t, ((0, 0), (0, pad), (0, 0), (0, 0)))
        return t.reshape(bsz, nc, C, H, t.shape[-1]).swapaxes(0, 1)

    qs, ks, vs, gs = prep(q), prep(k), prep(v), prep(logf)
    causal = jnp.tril(jnp.ones((C, C), bool))[None, :, :, None, None]

    def step(S, xs):
        qc, kc, vc, gc = xs
        bcum = jnp.cumsum(gc, axis=1)
        decay = jnp.exp(jnp.where(causal, bcum[:, :, None] - bcum[:, None, :], -jnp.inf))
        att = jnp.einsum('bthd,bshd,btshd->bhts', qc, kc, decay)
        o = (jnp.einsum('bhts,bshe->bthe', att, vc)
             + jnp.einsum('bthd,bhde->bthe', qc * jnp.exp(bcum), S))
        b_last = bcum[:, -1]
        S = (jnp.exp(b_last)[..., None] * S
             + jnp.einsum('bshd,bshe->bhde', kc * jnp.exp(b_last[:, None] - bcum), vc))
        return S, o

    S, o = lax.scan(step, s0, (qs, ks, vs, gs))
    o = o.swapaxes(0, 1).reshape(bsz, nc * C, H, v.shape[-1])[:, :L]
    return o, S


def odd_mixer(hn, s0, lb, p):
    b, L, _ = hn.shape
    q, f, i, g = jnp.split(hn @ p['w_in'], 4, axis=-1)
    forget = lb + (1.0 - lb) * jax.nn.sigmoid(f.astype(jnp.float32))
    logf = jnp.log(forget)
    k = 1.0 - forget

    def heads(t):
        return t.reshape(b, L, HGRN_HEADS, HGRN_DK)

    o, S = hgrn_recurrence(heads(jax.nn.silu(q.astype(jnp.float32))), heads(k),
                           heads(i.astype(jnp.float32)), heads(logf), s0.astype(jnp.float32))
    o = rms_norm(o.reshape(b, L, HGRN_WIDTH), p['out_norm']) * jax.nn.silu(g.astype(jnp.float32))
    return o.astype(hn.dtype) @ p['w_out'], S


def mem_kv(mem, g_src, w_k, w_v, k_gain):
    b, m, _ = mem.shape
    mn = rms_norm(mem, g_src)
    k = rms_norm((mn @ w_k).reshape(b, m, MEM_HEADS, MEM_HEAD_DIM), k_gain)
    v = (mn @ w_v).reshape(b, m, MEM_HEADS, MEM_HEAD_DIM)
    return k, v


def mem_attend(hn, k, v, w_q, q_gain, w_o):
    b, L, _ = hn.shape
    q = rms_norm((hn @ w_q).reshape(b, L, MEM_HEADS, MEM_HEAD_DIM), q_gain)
    sc = jnp.einsum('bqhd,bkhd->bhqk', q, k).astype(jnp.float32) * (MEM_HEAD_DIM ** -0.5)
    pr = jax.nn.softmax(sc, axis=-1).astype(v.dtype)
    o = jnp.einsum('bhqk,bkhd->bqhd', pr, v).reshape(b, L, MEM_WIDTH)
    return o @ w_o


def sq_relu_mlp(hn, w_up, w_down):
    return jnp.square(jax.nn.relu(hn @ w_up)) @ w_down


def setup_inputs(seed: int = 0) -> dict:
    key = jax.random.key(seed)
    keys = iter(jax.random.split(key, 64))

    def nrm(shape, scale=1.0):
        return jax.random.normal(next(keys), shape, jnp.float32) * scale

    def gain(shape):
        return 1.0 + 0.02 * nrm(shape)

    n_pages = PAST_LEN // PAGE_SIZE
    n_used = DEC_BATCH * n_pages
    n_phys = (5 * n_used + 3) // 4
    page_table = jax.random.permutation(next(keys), n_phys)[:n_used].reshape(DEC_BATCH, n_pages).astype(jnp.int32)
    return {
        'x_prompt': nrm((BATCH, SEQ, D_MODEL)),
        'x_sample': nrm((DEC_BATCH, DEC_SEQ, D_MODEL)),
        'cache_mla_latent': nrm((n_phys, N_EVEN, PAGE_SIZE, MLA_KV_LORA)),
        'cache_mla_krope': nrm((n_phys, N_EVEN, PAGE_SIZE, MLA_ROPE)),
        'state_s5_re': nrm((N_EVEN, DEC_BATCH, S5_GROUPS, S5_STATE), 0.5),
        'state_s5_im': nrm((N_EVEN, DEC_BATCH, S5_GROUPS, S5_STATE), 0.5),
        'state_hgrn': nrm((N_ODD, DEC_BATCH, HGRN_HEADS, HGRN_DK, HGRN_DV), 0.5),
        'cache_mem_k': nrm((DEPTH, DEC_BATCH, MEM_LEN, MEM_HEADS, MEM_HEAD_DIM)),
        'cache_mem_v': nrm((DEPTH, DEC_BATCH, MEM_LEN, MEM_HEADS, MEM_HEAD_DIM)),
        'page_table': page_table,
        'mem_prompt': nrm((BATCH, MEM_LEN, D_MODEL)),
        'norm_mix': gain((DEPTH, D_MODEL)),
        'norm_mem': gain((DEPTH, D_MODEL)),
        'norm_memsrc': gain((DEPTH, D_MODEL)),
        'norm_mlp': gain((DEPTH, D_MODEL)),
        'w_mem_q': nrm((DEPTH, D_MODEL, MEM_WIDTH), D_MODEL ** -0.5),
        'w_mem_k': nrm((DEPTH, D_MODEL, MEM_WIDTH), D_MODEL ** -0.5),
        'w_mem_v': nrm((DEPTH, D_MODEL, MEM_WIDTH), D_MODEL ** -0.5),
        'w_mem_o': nrm((DEPTH, MEM_WIDTH, D_MODEL), MEM_WIDTH ** -0.5),
        'mem_q_gain': gain((DEPTH, MEM_HEAD_DIM)),
        'mem_k_gain': gain((DEPTH, MEM_HEAD_DIM)),
        'w_mlp_up': nrm((DEPTH, D_MODEL, D_FF), D_MODEL ** -0.5),
        'w_mlp_down': nrm((DEPTH, D_FF, D_MODEL), D_FF ** -0.5),
        'w_in_even': nrm((N_EVEN, D_MODEL, EVEN_IN), D_MODEL ** -0.5),
        'mla_cq_norm': gain((N_EVEN, MLA_Q_LORA)),
        'mla_ckv_norm': gain((N_EVEN, MLA_KV_LORA)),
        'w_mla_uq': nrm((N_EVEN, MLA_Q_LORA, MLA_HEADS * MLA_QK), MLA_Q_LORA ** -0.5),
        'w_mla_ukv': nrm((N_EVEN, MLA_KV_LORA, MLA_HEADS * (MLA_NOPE + MLA_V)), MLA_KV_LORA ** -0.5),
        'mla_qn_nope': gain((N_EVEN, MLA_NOPE)),
        'mla_qn_rope': gain((N_EVEN, MLA_ROPE // 2)),
        'mla_kn_nope': gain((N_EVEN, MLA_NOPE)),
        'mla_kn_rope': gain((N_EVEN, MLA_ROPE // 2)),
        's5_lambda_re': -0.5 + 0.01 * nrm((N_EVEN, S5_GROUPS, S5_STATE)),
        's5_lambda_im': math.pi * jnp.arange(S5_STATE, dtype=jnp.float32) + 0.01 * nrm((N_EVEN, S5_GROUPS, S5_STATE)),
        's5_log_step': jax.random.uniform(next(keys), (N_EVEN, S5_GROUPS), jnp.float32, math.log(1e-3), math.log(1e-1)),
        's5_b_re': nrm((N_EVEN, S5_GROUPS, S5_STATE, S5_GROUP), (2 * S5_GROUP) ** -0.5),
        's5_b_im': nrm((N_EVEN, S5_GROUPS, S5_STATE, S5_GROUP), (2 * S5_GROUP) ** -0.5),
        's5_c_re': nrm((N_EVEN, S5_GROUPS, S5_GROUP, S5_STATE), S5_STATE ** -0.5),
        's5_c_im': nrm((N_EVEN, S5_GROUPS, S5_GROUP, S5_STATE), S5_STATE ** -0.5),
        's5_d': nrm((N_EVEN, S5_WIDTH)),
        's5_w_glu': nrm((N_EVEN, S5_WIDTH, S5_WIDTH), S5_WIDTH ** -0.5),
        's5_b_glu': nrm((N_EVEN, S5_WIDTH), 0.01),
        'w_out_even': nrm((N_EVEN, EVEN_OUT, D_MODEL), EVEN_OUT ** -0.5),
        'w_in_odd': nrm((N_ODD, D_MODEL, 4 * HGRN_WIDTH), D_MODEL ** -0.5),
        'hgrn_lower_bounds': nrm((DEPTH, HGRN_WIDTH), 0.1),
        'hgrn_out_norm': gain((N_ODD, HGRN_WIDTH)),
        'w_out_odd': nrm((N_ODD, HGRN_WIDTH, D_MODEL), HGRN_WIDTH ** -0.5),
    }


def reference(x_prompt, x_sample, cache_mla_latent, cache_mla_krope, state_s5_re, state_s5_im, state_hgrn,
              cache_mem_k, cache_mem_v, page_table, mem_prompt,
              norm_mix, norm_mem, norm_memsrc, norm_mlp, w_mem_q, w_mem_k, w_mem_v, w_mem_o, mem_q_gain, mem_k_gain,
              w_mlp_up, w_mlp_down, w_in_even, mla_cq_norm, mla_ckv_norm, w_mla_uq, w_mla_ukv,
              mla_qn_nope, mla_qn_rope, mla_kn_nope, mla_kn_rope,
              s5_lambda_re, s5_lambda_im, s5_log_step, s5_b_re, s5_b_im, s5_c_re, s5_c_im, s5_d, s5_w_glu, s5_b_glu,
              w_out_even, w_in_odd, hgrn_lower_bounds, hgrn_out_norm, w_out_odd):
    pos_p = jnp.arange(SEQ, dtype=jnp.int32)
    pos_s = PAST_LEN + jnp.arange(DEC_SEQ, dtype=jnp.int32)
    lb_sm = jax.nn.softmax(hgrn_lower_bounds.astype(jnp.float32), axis=0)
    lb_all = jnp.cumsum(lb_sm, axis=0) - lb_sm[0]

    hp, hs = x_prompt, x_sample
    lat_p, kr_p, s5r_p, s5i_p, hg_p, mk_p, mv_p = [], [], [], [], [], [], []
    lat_s, kr_s, s5r_s, s5i_s, hg_s = [], [], [], [], []
    for l in range(DEPTH):
        if l % 2 == 0:
            e = l // 2
            pe = {
                'w_in': w_in_even[e], 'cq_norm': mla_cq_norm[e], 'ckv_norm': mla_ckv_norm[e],
                'w_uq': w_mla_uq[e],
                'q_gain': jnp.concatenate([mla_qn_nope[e], mla_qn_rope[e], mla_qn_rope[e]]),
                'lam_re': s5_lambda_re[e], 'lam_im': s5_lambda_im[e], 'log_step': s5_log_step[e],
                'b_re': s5_b_re[e], 'b_im': s5_b_im[e], 'c_re': s5_c_re[e], 'c_im': s5_c_im[e],
                'd': s5_d[e], 'w_glu': s5_w_glu[e], 'b_glu': s5_b_glu[e], 'w_out': w_out_even[e],
            }
            w_ukv_e = w_mla_ukv[e]
            k_gain_e = jnp.concatenate([mla_kn_nope[e], mla_kn_rope[e], mla_kn_rope[e]])

            def attend_prompt(q, ckv, kr, w=w_ukv_e, g=k_gain_e):
                k, v = mla_keys(ckv, kr, w, g)
                return causal_block_attention(q, k, v)

            def attend_sample(q, ckv, kr, w=w_ukv_e, g=k_gain_e, e=e):
                return paged_mla_attention(q, ckv, kr, cache_mla_latent, cache_mla_krope, page_table, e, w, g)

            z0 = jnp.zeros((BATCH, S5_GROUPS, S5_STATE), jnp.float32)
            out, ckv, kr, hr, hi = even_mixer(rms_norm(hp, norm_mix[l]), pos_p, z0, z0, attend_prompt, pe)
            hp = hp + out
            lat_p.append(ckv); kr_p.append(kr); s5r_p.append(hr); s5i_p.append(hi)
            out, ckv, kr, hr, hi = even_mixer(rms_norm(hs, norm_mix[l]), pos_s, state_s5_re[e], state_s5_im[e],
                                              attend_sample, pe)
            hs = hs + out
            lat_s.append(ckv); kr_s.append(kr); s5r_s.append(hr); s5i_s.append(hi)
        else:
            o = l // 2
            po = {'w_in': w_in_odd[o], 'out_norm': hgrn_out_norm[o], 'w_out': w_out_odd[o]}
            s_zero = jnp.zeros((BATCH, HGRN_HEADS, HGRN_DK, HGRN_DV), jnp.float32)
            out, S = odd_mixer(rms_norm(hp, norm_mix[l]), s_zero, lb_all[l], po)
            hp = hp + out
            hg_p.append(S)
            out, S = odd_mixer(rms_norm(hs, norm_mix[l]), state_hgrn[o], lb_all[l], po)
            hs = hs + out
            hg_s.append(S)
        mk, mv = mem_kv(mem_prompt, norm_memsrc[l], w_mem_k[l], w_mem_v[l], mem_k_gain[l])
        mk_p.append(mk); mv_p.append(mv)
        hp = hp + mem_attend(rms_norm(hp, norm_mem[l]), mk, mv, w_mem_q[l], mem_q_gain[l], w_mem_o[l])
        hs = hs + mem_attend(rms_norm(hs, norm_mem[l]), cache_mem_k[l], cache_mem_v[l],
                             w_mem_q[l], mem_q_gain[l], w_mem_o[l])
        hp = hp + sq_relu_mlp(rms_norm(hp, norm_mlp[l]), w_mlp_up[l], w_mlp_down[l])
        hs = hs + sq_relu_mlp(rms_norm(hs, norm_mlp[l]), w_mlp_up[l], w_mlp_down[l])

    return (hp, hs,
            jnp.stack(lat_p, axis=1), jnp.stack(kr_p, axis=1),
            jnp.stack(s5r_p), jnp.stack(s5i_p), jnp.stack(hg_p),
            jnp.stack(mk_p), jnp.stack(mv_p),
            jnp.stack(lat_s, axis=1), jnp.stack(kr_s, axis=1),
            jnp.stack(s5r_s), jnp.stack(s5i_s), jnp.stack(hg_s))
```

```python
import concourse.bass as bass
import concourse.mybir as mybir

F32 = mybir.dt.float32
BF16 = mybir.dt.bfloat16
I32 = mybir.dt.int32
U32 = mybir.dt.uint32
ALU = mybir.AluOpType
AF = mybir.ActivationFunctionType
AX = mybir.AxisListType

_AP_T = None


class _Buf:
    __slots__ = ("name", "w", "r", "dsem", "dcnt", "space")

    def __init__(self, name, space):
        self.name = name
        self.w = None
        self.r = []
        self.dsem = None
        self.space = space


class Sched:
    EPOCH = 30000

    def __init__(self, nc, stack):
        self.nc = nc
        self.stack = stack
        self.engs = {"pe": nc.tensor, "act": nc.scalar, "dve": nc.vector,
                     "pool": nc.gpsimd, "sp": nc.sync}
        self.streams = {e: [] for e in self.engs}
        self.sems = {}
        self.semcnt = {}
        self.cur = {}
        self.waited = {e: {} for e in self.engs}
        self.bufs = {}
        self.nsem = 0
        self.uid = 0
        self.free_dma = []
        self.pstack = None
        for e in self.engs:
            self._new_eng_sem(e)

    def _alloc_sem(self, key):
        h = self.stack.enter_context(self.nc.semaphore("s_%s_%d" % (key if isinstance(key, str) else "x", self.nsem)))
        self.nsem += 1
        self.sems[key] = h
        self.semcnt[key] = 0
        return key

    def _new_eng_sem(self, e):
        key = ("eng", e, self.nsem)
        self._alloc_sem(key)
        self.cur[e] = key

    def _buf(self, ap):
        t = ap.tensor
        name = t.name
        b = self.bufs.get(name)
        if b is None:
            tn = type(t).__name__
            space = "dram" if "DRam" in tn else ("psum" if "PSum" in tn else "sbuf")
            b = _Buf(name, space)
            self.bufs[name] = b
        return b

    def _ring(self, kind, name, shape, dtype, bufs):
        key = (kind, name)
        r = self.rings.get(key)
        if r is None:
            r = {"tiles": [], "i": 0, "shape": list(shape), "dtype": dtype}
            self.rings[key] = r
        assert r["shape"] == list(shape) and r["dtype"] == dtype, (name, shape, r["shape"])
        if len(r["tiles"]) < bufs:
            self.uid += 1
            mk = self.nc.sbuf_tensor if kind == "sb" else self.nc.psum_tensor
            t = self.pstack.enter_context(mk("%s_%d" % (name, self.uid), list(shape), dtype))
            r["tiles"].append(t)
            return t
        t = r["tiles"][r["i"] % len(r["tiles"])]
        r["i"] += 1
        return t

    def sb(self, name, shape, dtype, bufs=2):
        return self._ring("sb", name, shape, dtype, bufs)

    def gsb(self, name, shape, dtype):
        self.uid += 1
        return self.stack.enter_context(self.nc.sbuf_tensor("%s_%d" % (name, self.uid), list(shape), dtype))

    def ps(self, name, shape, dtype=F32, bufs=2):
        return self._ring("ps", name, shape, dtype, bufs)

    def begin_phase(self):
        from contextlib import ExitStack
        self.pstack = ExitStack()
        self.pstack.__enter__()
        self.rings = {}

    def end_phase(self):
        allv = [(kk, vv) for kk, vv in self.semcnt.items() if vv > 0]
        self.phase_final = allv
        self.emit()
        for e in self.engs:
            self.streams[e] = []
            for kk, vv in allv:
                self.waited[e][kk] = vv
        for b in self.bufs.values():
            if b.dsem is not None:
                self.free_dma.append(b.dsem)
        self.bufs = {}
        self.pstack.__exit__(None, None, None)
        self.pstack = None

    def dram(self, name, shape, dtype, kind="Internal"):
        return self.nc.dram_tensor(name, list(shape), dtype, kind=kind)

    def _deps(self, eng, reads, writes):
        deps = {}
        def add(tok):
            if tok is None:
                return
            k, v = tok
            if k[0] == "dma":
                v = self.semcnt[k]
            if deps.get(k, 0) < v:
                deps[k] = v
        for b in reads:
            add(b.w)
        for b in writes:
            add(b.w)
            for t in b.r:
                add(t)
        out = []
        wd = self.waited[eng]
        for k, v in deps.items():
            if eng == "pe" and k == self.cur["pe"]:
                continue
            if wd.get(k, 0) >= v:
                continue
            wd[k] = v
            out.append((k, v))
        return out

    def _classify(self, kw, extra_r, extra_w):
        global _AP_T
        reads, writes = [], []
        for name, v in kw.items():
            if name in ("identity",):
                continue
            if hasattr(v, "tensor") and hasattr(v, "partition_size"):
                b = self._buf(v)
                if name in ("out", "accum_out"):
                    writes.append(b)
                else:
                    reads.append(b)
        for v in extra_r:
            reads.append(self._buf(v))
        for v in extra_w:
            writes.append(self._buf(v))
        return reads, writes

    def op(self, eng, method, _r=(), _w=(), _rw=(), **kw):
        reads, writes = self._classify(kw, list(_r) + list(_rw), list(_w) + list(_rw))
        waits = self._deps(eng, reads, writes)
        if self.semcnt[self.cur[eng]] >= self.EPOCH:
            self._new_eng_sem(eng)
        key = self.cur[eng]
        self.semcnt[key] += 1
        tok = (key, self.semcnt[key])
        self.streams[eng].append((waits, method, kw, key, 1))
        for b in reads:
            b.r.append(tok)
        for b in writes:
            b.w = tok
            b.r = []
        return tok

    def dma(self, queue, out, in_, **kw):
        bo, bi = self._buf(out), self._buf(in_)
        sbside = bo if bo.space != "dram" else bi
        assert sbside.space == "sbuf", "dma needs an sbuf side: %s %s" % (bo.name, bi.name)
        if sbside.dsem is None:
            if self.free_dma:
                sbside.dsem = self.free_dma.pop()
            else:
                sbside.dsem = self._alloc_sem(("dma", sbside.name))
        key = sbside.dsem
        waits = self._deps(queue, [bi] if bi.space != "dram" else [], [bo] if bo.space != "dram" else [])
        self.semcnt[key] += 16
        tok = (key, self.semcnt[key])
        meth = kw.pop("_method", "dma_start")
        for xr in kw.pop("_r", ()):
            xb = self._buf(xr)
            if xb.space != "dram":
                for kk, vv in self._deps(queue, [xb], []):
                    waits.append((kk, vv))
                xb.r.append((key, self.semcnt[key]))
        d = dict(out=out, in_=in_)
        d.update(kw)
        self.streams[queue].append((waits, meth, d, key, 16))
        if bi.space != "dram":
            bi.r.append(tok)
        if bo.space != "dram":
            bo.w = tok
            bo.r = []
        return tok

    def finish(self, final_bufs_aps=()):
        deps = {}
        for b in self.bufs.values():
            if b.space == "dram" and b.w is not None:
                k, v = b.w
                if k[0] == "dma":
                    v = self.semcnt[k]
                deps[k] = max(deps.get(k, 0), v)
        self.final_waits = list(deps.items())

    def emit(self):
        nc = self.nc
        blk = nc.Block()
        blk.__enter__()
        names = {"pe": "tensor", "act": "scalar", "dve": "vector", "pool": "gpsimd", "sp": "sync"}
        for e, attr in names.items():
            stream = self.streams[e]
            final = list(self.phase_final)

            def body(engine, stream=stream, final=final):
                for waits, meth, kw, key, inc in stream:
                    for k, v in waits:
                        engine.wait_ge(self.sems[k], v)
                    if isinstance(meth, tuple):
                        ins = meth[1](engine, kw["out"], kw["in_"])
                    else:
                        ins = getattr(engine, meth)(**kw)
                    ins.then_inc(self.sems[key], inc)
                for k, v in final:
                    engine.wait_ge(self.sems[k], v)
            getattr(blk, attr)(body)
        blk.__exit__(None, None, None)

import numpy as np
from contextlib import ExitStack
from concourse.bass_utils import run_bass_kernel_spmd

NCORES = 8
D = 1024
SEQ = 4096
NPT = 32
NSB = 16
NST = 64
NTOK = SEQ + NST
EPS = 1e-6
NPHYS = 20480
NPAGES = 128


def tiles():
    return [(i, i * 128, 128) for i in range(NPT)] + [(NPT, SEQ, NST)]


class KB:
    def __init__(self, nc, k):
        self.nc = nc
        self.k = k

    def bcast_load(self, name, vec_ap, n, glob=False):
        t = (self.k.gsb if glob else self.k.sb)(name, [128, n], F32)
        self.k.dma("sp", out=t[:], in_=vec_ap.partition_broadcast(128))
        return t

    def load_w(self, name, w_ap, K, N, stage_cols=2048):
        k = self.k
        kc = K // 128
        wb = k.sb(name, [128, kc, N], BF16)
        src = w_ap.rearrange("(c p) n -> p c n", p=128)
        step = max(1, stage_cols // N) if N <= stage_cols else 1
        idx = 0
        for c0 in range(0, kc, step):
            c1 = min(kc, c0 + step)
            for n0 in range(0, N, stage_cols):
                n1 = min(N, n0 + stage_cols)
                st = self.stage[idx % 2]
                idx += 1
                sv = st[:, 0:(c1 - c0) * (n1 - n0)].rearrange("p (c n) -> p c n", c=c1 - c0)
                k.dma("sp", out=sv, in_=src[:, c0:c1, n0:n1])
                eng = "pool" if idx % 2 else "dve"
                k.op(eng, "tensor_copy", out=wb[:, c0:c1, n0:n1], in_=sv)
        return wb

    def mk_stage(self, cols=2048):
        self.stage = [self.k.sb("wstage", [128, cols], F32) for _ in range(2)]

    def rstd_from_ss(self, ss_ap, n_feat):
        k = self.k
        k.op("dve", "tensor_scalar", out=ss_ap, in0=ss_ap, scalar1=1.0 / n_feat, scalar2=EPS,
             op0=ALU.mult, op1=ALU.add)
        k.op("act", "activation", out=ss_ap, in_=ss_ap, func=AF.Ln)
        k.op("act", "activation", out=ss_ap, in_=ss_ap, func=AF.Exp, scale=-0.5)

    def rmsnorm(self, out_ap, x_ap, g_ap, n_feat, R, scr_ap):
        k = self.k
        ss = k.sb("ss", [128, 1], F32)
        k.op("act", "activation", out=scr_ap, in_=x_ap, func=AF.Square, accum_out=ss[:R])
        self.rstd_from_ss(ss[:R], n_feat)
        k.op("dve", "scalar_tensor_tensor", out=out_ap, in0=x_ap, scalar=ss[:R, 0:1], in1=g_ap,
             op0=ALU.mult, op1=ALU.mult)

    def transpose_chunks(self, dst, src, nch, R, width=128, dt=BF16):
        k = self.k
        per = 8 if dt == BF16 else 4
        for c0 in range(0, nch, per):
            c1 = min(nch, c0 + per)
            pt = k.ps("pt" + ("b" if dt == BF16 else "f"), [128, per, 128], dt, bufs=2)
            for c in range(c0, c1):
                k.op("pe", "transpose", out=pt[:width, c - c0, :R], in_=src[:R, c * width:(c + 1) * width],
                     identity=(self.idb if dt == BF16 else self.idf)[:R, :R])
            k.op("act", "activation", out=dst[:width, c0:c1, :R], in_=pt[:width, 0:c1 - c0, :R], func=AF.Copy)

    def proj(self, ps_ap, xT, w, kc, R, n0, n1, kw=128):
        for c in range(kc):
            self.k.op("pe", "matmul", out=ps_ap, lhsT=xT[:kw, c, :R], rhs=w[:kw, c, n0:n1],
                      start=(c == 0), stop=(c == kc - 1))

    def head_norm(self, x3, nh, hd, gain_t, R):
        k = self.k
        sq = k.sb("hn_sq%d_%d" % (nh, hd), [128, nh, hd], F32)
        s8 = k.sb("hn_s8%d" % nh, [128, nh], F32)
        k.op("act", "activation", out=sq[:R], in_=x3, func=AF.Square)
        k.op("dve", "tensor_reduce", out=s8[:R], in_=sq[:R], axis=AX.X, op=ALU.add)
        self.rstd_from_ss(s8[:R], hd)
        k.op("dve", "tensor_tensor", out=x3, in0=x3, in1=s8[:R].unsqueeze(2).to_broadcast([R, nh, hd]), op=ALU.mult)
        k.op("pool", "tensor_tensor", out=x3, in0=x3, in1=gain_t[:R].unsqueeze(1).to_broadcast([R, nh, hd]), op=ALU.mult)

    def rope(self, out1, out2, x1, x2, cs, sn, R, shape):
        k = self.k
        sfx = "_".join(str(x) for x in shape)
        t1 = k.sb("rp1" + sfx, [128] + shape, F32)
        t2 = k.sb("rp2" + sfx, [128] + shape, F32)
        t3 = k.sb("rp3" + sfx, [128] + shape, F32)
        t4 = k.sb("rp4" + sfx, [128] + shape, F32)
        k.op("dve", "tensor_tensor", out=t1[:R], in0=x1, in1=cs, op=ALU.mult)
        k.op("pool", "tensor_tensor", out=t2[:R], in0=x2, in1=sn, op=ALU.mult)
        k.op("dve", "tensor_tensor", out=t3[:R], in0=x2, in1=cs, op=ALU.mult)
        k.op("pool", "tensor_tensor", out=t4[:R], in0=x1, in1=sn, op=ALU.mult)
        k.op("dve", "tensor_tensor", out=out1, in0=t1[:R], in1=t2[:R], op=ALU.subtract)
        k.op("dve", "tensor_tensor", out=out2, in0=t3[:R], in1=t4[:R], op=ALU.add)


CUT = 99
DBG = {}
NOP = {"v": False}
DECLARED = set()


TWO_PI = 6.283185307179586


def phase_s5(nc, k, kb, I, O, S):
    kb.mk_stage()
    NJ = 16
    def par(name, src):
        t = k.sb(name, [128, NJ], F32, bufs=1)
        for g2 in range(2):
            k.dma("sp", out=t[g2 * 64:(g2 + 1) * 64, :], in_=src.rearrange("(j g) n -> g n j", g=2)[g2], allow_slow_non_contiguous=True)
        return t
    lre = par("lre", I["s5_lre"]); lim = par("lim", I["s5_lim"])
    dt = k.sb("dt", [128, NJ], F32, bufs=1)
    ls2 = I["s5_ls"].rearrange("(j g) -> g j", g=2)
    for g2 in range(2):
        k.dma("sp", out=dt[g2 * 64:(g2 + 1) * 64, :], in_=ls2[g2].partition_broadcast(64), allow_slow_non_contiguous=True)
    k.op("act", "activation", out=dt[:], in_=dt[:], func=AF.Exp)
    k.op("dve", "tensor_scalar", out=lre[:], in0=lre[:], scalar1=-1e-4, scalar2=None, op0=ALU.min)
    mag = k.sb("mag", [128, NJ], F32, bufs=1); th = k.sb("th", [128, NJ], F32, bufs=1)
    k.op("dve", "tensor_tensor", out=mag[:], in0=lre[:], in1=dt[:], op=ALU.mult)
    k.op("act", "activation", out=mag[:], in_=mag[:], func=AF.Exp)
    k.op("dve", "tensor_tensor", out=th[:], in0=lim[:], in1=dt[:], op=ALU.mult)
    def sincos(dst, ang_ap, shape, shift):
        sfx = "%d" % shape[1]
        tmp = k.sb("sc_tmp" + sfx, shape, F32); ki = k.sb("sc_ki" + sfx, shape, I32); kf = k.sb("sc_kf" + sfx, shape, F32)
        m = k.sb("sc_m" + sfx, shape, F32)
        k.op("dve", "tensor_scalar", out=tmp[:], in0=ang_ap, scalar1=1.0 / TWO_PI, scalar2=shift / TWO_PI, op0=ALU.mult, op1=ALU.add)
        k.op("dve", "tensor_copy", out=ki[:], in_=tmp[:])
        k.op("dve", "tensor_copy", out=kf[:], in_=ki[:])
        k.op("dve", "tensor_tensor", out=tmp[:], in0=tmp[:], in1=kf[:], op=ALU.subtract)
        k.op("dve", "tensor_scalar", out=m[:], in0=tmp[:], scalar1=0.5, scalar2=-1.0, op0=ALU.is_gt, op1=ALU.mult)
        k.op("dve", "tensor_tensor", out=tmp[:], in0=tmp[:], in1=m[:], op=ALU.add)
        k.op("dve", "tensor_scalar", out=m[:], in0=tmp[:], scalar1=-0.5, scalar2=1.0, op0=ALU.is_lt, op1=ALU.mult)
        k.op("dve", "tensor_tensor", out=tmp[:], in0=tmp[:], in1=m[:], op=ALU.add)
        k.op("act", "activation", out=dst, in_=tmp[:], func=AF.Sin, scale=TWO_PI)
    abre = k.sb("abre", [128, NJ], F32, bufs=1); abim = k.sb("abim", [128, NJ], F32, bufs=1)
    sincos(abre[:], th[:], [128, NJ], np.pi / 2); sincos(abim[:], th[:], [128, NJ], 0.0)
    k.op("dve", "tensor_tensor", out=abre[:], in0=abre[:], in1=mag[:], op=ALU.mult)
    k.op("dve", "tensor_tensor", out=abim[:], in0=abim[:], in1=mag[:], op=ALU.mult)
    den = k.sb("den", [128, NJ], F32, bufs=1); t1 = k.sb("t1", [128, NJ], F32, bufs=1); t2 = k.sb("t2", [128, NJ], F32, bufs=1)
    core_ = k.sb("core", [128, NJ], F32, bufs=1); coim = k.sb("coim", [128, NJ], F32, bufs=1); am1 = k.sb("am1", [128, NJ], F32, bufs=1)
    k.op("dve", "tensor_tensor", out=den[:], in0=lre[:], in1=lre[:], op=ALU.mult)
    k.op("dve", "tensor_tensor", out=t1[:], in0=lim[:], in1=lim[:], op=ALU.mult)
    k.op("dve", "tensor_tensor", out=den[:], in0=den[:], in1=t1[:], op=ALU.add)
    k.op("dve", "reciprocal", out=den[:], in_=den[:])
    k.op("dve", "tensor_scalar", out=am1[:], in0=abre[:], scalar1=-1.0, scalar2=None, op0=ALU.add)
    k.op("dve", "tensor_tensor", out=t1[:], in0=am1[:], in1=lre[:], op=ALU.mult)
    k.op("dve", "tensor_tensor", out=t2[:], in0=abim[:], in1=lim[:], op=ALU.mult)
    k.op("dve", "tensor_tensor", out=t1[:], in0=t1[:], in1=t2[:], op=ALU.add)
    k.op("dve", "tensor_tensor", out=core_[:], in0=t1[:], in1=den[:], op=ALU.mult)
    k.op("dve", "tensor_tensor", out=t1[:], in0=abim[:], in1=lre[:], op=ALU.mult)
    k.op("dve", "tensor_tensor", out=t2[:], in0=am1[:], in1=lim[:], op=ALU.mult)
    k.op("dve", "tensor_tensor", out=t1[:], in0=t1[:], in1=t2[:], op=ALU.subtract)
    k.op("dve", "tensor_tensor", out=coim[:], in0=t1[:], in1=den[:], op=ALU.mult)
    bre = k.sb("bre", [128, NJ, 16], F32, bufs=1); bim = k.sb("bim", [128, NJ, 16], F32, bufs=1)
    for g2 in range(2):
        k.dma("sp", out=bre[g2 * 64:(g2 + 1) * 64], in_=I["s5_bre"].rearrange("(j g) n c -> g n j c", g=2)[g2])
        k.dma("sp", out=bim[g2 * 64:(g2 + 1) * 64], in_=I["s5_bim"].rearrange("(j g) n c -> g n j c", g=2)[g2])
    bbre = k.sb("bbre", [128, NJ, 16], F32, bufs=1); bbim = k.sb("bbim", [128, NJ, 16], F32, bufs=1); tb = k.sb("tb", [128, NJ, 16], F32, bufs=1)
    cr3 = core_[:].unsqueeze(2).to_broadcast([128, NJ, 16]); ci3 = coim[:].unsqueeze(2).to_broadcast([128, NJ, 16])
    k.op("dve", "tensor_tensor", out=bbre[:], in0=bre[:], in1=cr3, op=ALU.mult)
    k.op("dve", "tensor_tensor", out=tb[:], in0=bim[:], in1=ci3, op=ALU.mult)
    k.op("dve", "tensor_tensor", out=bbre[:], in0=bbre[:], in1=tb[:], op=ALU.subtract)
    k.op("dve", "tensor_tensor", out=bbim[:], in0=bim[:], in1=cr3, op=ALU.mult)
    k.op("dve", "tensor_tensor", out=tb[:], in0=bre[:], in1=ci3, op=ALU.mult)
    k.op("dve", "tensor_tensor", out=bbim[:], in0=bbim[:], in1=tb[:], op=ALU.add)
    BBT = [k.sb("BBTre", [128, NJ, 128], BF16, bufs=1), k.sb("BBTim", [128, NJ, 128], BF16, bufs=1)]
    for ri, bbx in enumerate((bbre, bbim)):
        for j in range(NJ):
            blk = k.sb("bblk", [128, 128], F32)
            k.op("dve", "memset", ap=blk[:], constant=0.0, _w=[blk[:]])
            for g2 in range(2):
                c0 = (j % 4) * 32 + g2 * 16
                k.op("dve", "tensor_copy", out=blk[g2 * 64:(g2 + 1) * 64, c0:c0 + 16], in_=bbx[g2 * 64:(g2 + 1) * 64, j, :])
            pt = k.ps("px4", [128, 4, 2, 128], F32, bufs=2)[:, :, 0, :]
            k.op("pe", "transpose", out=pt[:, 0, :], in_=blk[:], identity=kb.idf[:])
            k.op("act", "activation", out=BBT[ri][:, j, :], in_=pt[:, 0, :], func=AF.Copy)
    CM = []
    for ri, nm in enumerate(("s5_cre", "s5_cim")):
        cf = k.sb("cmf%d" % ri, [128, NJ, 32], F32, bufs=1)
        k.op("dve", "memset", ap=cf[:], constant=0.0, _w=[cf[:]])
        for j in range(NJ):
            for g2 in range(2):
                k.dma("sp", out=cf[g2 * 64:(g2 + 1) * 64, j, g2 * 16:(g2 + 1) * 16], in_=I[nm][2 * j + g2].rearrange("c n -> n c"), allow_slow_non_contiguous=True)
        cb = k.sb("cmb%d" % ri, [128, NJ, 32], BF16, bufs=1)
        k.op("dve", "tensor_scalar", out=cb[:], in0=cf[:], scalar1=(1.0 if ri == 0 else -1.0), scalar2=None, op0=ALU.mult)
        CM.append(cb)
    ramp = k.sb("ramp", [128, 128], F32, bufs=1)
    k.dma("sp", out=ramp[:], in_=I["ramp"])
    TC = k.sb("TC", [128, NJ, 128], F32, bufs=1); TS = k.sb("TS", [128, NJ, 128], F32, bufs=1)
    for j in range(NJ):
        ph = k.sb("ph", [128, 128], F32)
        k.op("dve", "tensor_scalar", out=ph[:], in0=ramp[:], scalar1=th[:, j:j + 1], scalar2=None, op0=ALU.mult)
        sincos(TC[:, j, :], ph[:], [128, 128], np.pi / 2); sincos(TS[:, j, :], ph[:], [128, 128], 0.0)
    dvec = kb.bcast_load("s5d", I["s5_d"], 512); bglu = kb.bcast_load("s5bg", I["s5_bglu"], 512)
    wglu = kb.load_w("wglu", I["s5_wglu"], 512, 512)
    car = [k.sb("car_re", [128, NJ], F32, bufs=1), k.sb("car_im", [128, NJ], F32, bufs=1)]
    k.op("dve", "memset", ap=car[0][:], constant=0.0, _w=[car[0][:]])
    k.op("dve", "memset", ap=car[1][:], constant=0.0, _w=[car[1][:]])
    H0 = [k.sb("h0re", [128, NJ, NSB], F32, bufs=1), k.sb("h0im", [128, NJ, NSB], F32, bufs=1)]
    for ri, nm in enumerate(("s5_h0re", "s5_h0im")):
        hn_ = k.sb("h0nat", [NSB, 2048], F32)
        k.dma("sp", out=hn_[:], in_=I[nm])
        for j0 in range(0, NJ, 4):
            pt = k.ps("px4", [128, 4, 2, 128], F32, bufs=2)[:, :, 0, :]
            for j in range(j0, j0 + 4):
                k.op("pe", "transpose", out=pt[:, j - j0, :NSB], in_=hn_[:NSB, j * 128:(j + 1) * 128], identity=kb.idf[:NSB, :NSB])
            k.op("act", "activation", out=H0[ri][:, j0:j0 + 4, :], in_=pt[:, :, :NSB], func=AF.Copy)
    HS = [k.sb("hsre", [128, NJ, NSB], F32, bufs=1), k.sb("hsim", [128, NJ, NSB], F32, bufs=1)]

    for (i, t0, R) in tiles():
        T = R
        uT = k.sb("s5uT", [128, 4, 128], BF16)
        k.dma("sp", out=uT[:, :, :T], in_=S["UT"][:, :, t0:t0 + T].rearrange("c p t -> p c t"))
        uf = k.sb("s5uf", [128, 512], F32)
        k.dma("sp", out=uf[:T], in_=S["U"][t0:t0 + T, :])
        py = k.ps("py", [128, 512], F32, bufs=2)
        for j0 in (range(0, NJ, 4) if i < NPT else []):
            px4 = k.ps("px4", [128, 4, 2, 128], F32, bufs=2)
            for jj in range(4):
                j = j0 + jj
                k.op("pe", "matmul", out=px4[:, jj, 0, :T], lhsT=BBT[0][:, j, :], rhs=uT[:, j // 4, :T], start=True, stop=True)
                k.op("pe", "matmul", out=px4[:, jj, 1, :T], lhsT=BBT[1][:, j, :], rhs=uT[:, j // 4, :T], start=True, stop=True)
            xs4 = k.sb("s5x4", [128, 4, 2, 128], F32)
            k.op("act", "activation", out=xs4[:], in_=px4[:], func=AF.Copy)
            c4 = TC[:, j0:j0 + 4, :]; s4 = TS[:, j0:j0 + 4, :]
            xre = xs4[:, :, 0, :]; xim = xs4[:, :, 1, :]
            b1 = k.sb("s5b1", [128, 4, 128], F32); b2 = k.sb("s5b2", [128, 4, 128], F32); b3 = k.sb("s5b3", [128, 4, 128], F32); b4 = k.sb("s5b4", [128, 4, 128], F32)
            k.op("dve", "tensor_tensor", out=b1[:], in0=xre, in1=c4, op=ALU.mult)
            k.op("pool", "tensor_tensor", out=b2[:], in0=xim, in1=s4, op=ALU.mult)
            k.op("dve", "tensor_tensor", out=b3[:], in0=xim, in1=c4, op=ALU.mult)
            k.op("pool", "tensor_tensor", out=b4[:], in0=xre, in1=s4, op=ALU.mult)
            k.op("dve", "tensor_tensor", out=b1[:], in0=b1[:], in1=b2[:], op=ALU.add)
            k.op("pool", "tensor_tensor", out=b3[:], in0=b3[:], in1=b4[:], op=ALU.subtract)
            g4 = k.sb("s5g4", [128, 4, 2, 128], F32)
            for jj in range(4):
                j = j0 + jj
                rb = mag[:, j:j + 1].to_broadcast([128, T])
                k.op("dve", "tensor_tensor_scan", out=g4[:, jj, 0, :], data0=rb, data1=b1[:, jj, :], initial=car[0][:, j:j + 1], op0=ALU.mult, op1=ALU.add)
                k.op("dve", "tensor_tensor_scan", out=g4[:, jj, 1, :], data0=rb, data1=b3[:, jj, :], initial=car[1][:, j:j + 1], op0=ALU.mult, op1=ALU.add)
            gre = g4[:, :, 0, :]; gim = g4[:, :, 1, :]
            h4 = k.sb("s5h4", [128, 4, 2, 128], F32)
            k.op("dve", "tensor_tensor", out=b1[:], in0=gre, in1=c4, op=ALU.mult)
            k.op("pool", "tensor_tensor", out=b2[:], in0=gim, in1=s4, op=ALU.mult)
            k.op("dve", "tensor_tensor", out=b3[:], in0=gre, in1=s4, op=ALU.mult)
            k.op("pool", "tensor_tensor", out=b4[:], in0=gim, in1=c4, op=ALU.mult)
            k.op("dve", "tensor_tensor", out=h4[:, :, 0, :], in0=b1[:], in1=b2[:], op=ALU.subtract)
            k.op("pool", "tensor_tensor", out=h4[:, :, 1, :], in0=b3[:], in1=b4[:], op=ALU.add)
            k.op("act", "activation", out=car[0][:, j0:j0 + 4], in_=h4[:, :, 0, T - 1], func=AF.Copy)
            k.op("act", "activation", out=car[1][:, j0:j0 + 4], in_=h4[:, :, 1, T - 1], func=AF.Copy)
            hb4 = k.sb("s5hb4", [128, 4, 2, 128], BF16)
            k.op("act", "activation", out=hb4[:], in_=h4[:], func=AF.Copy)
            for jj in range(4):
                j = j0 + jj
                k.op("pe", "matmul", out=py[:T, j * 32:(j + 1) * 32], lhsT=hb4[:, jj, 0, :T], rhs=CM[0][:, j, :], start=True, stop=False)
                k.op("pe", "matmul", out=py[:T, j * 32:(j + 1) * 32], lhsT=hb4[:, jj, 1, :T], rhs=CM[1][:, j, :], start=False, stop=True)
        for j in (range(NJ) if i >= NPT else []):
            px = k.ps("px4", [128, 4, 2, 128], F32, bufs=2)[:, 0, :, :]
            k.op("pe", "matmul", out=px[:, 0, :T], lhsT=BBT[0][:, j, :], rhs=uT[:, j // 4, :T], start=True, stop=True)
            k.op("pe", "matmul", out=px[:, 1, :T], lhsT=BBT[1][:, j, :], rhs=uT[:, j // 4, :T], start=True, stop=True)
            xs_ = k.sb("s5x", [128, 2, 128], F32)
            k.op("act", "activation", out=xs_[:, :, :T], in_=px[:, :, :T], func=AF.Copy)
            if i < NPT:
                c_ = TC[:, j, :T]; s_ = TS[:, j, :T]
                shp = [128, T]
                v = lambda a: a
            else:
                c_ = TC[:, j, 0:4].unsqueeze(1).to_broadcast([128, NSB, 4]); s_ = TS[:, j, 0:4].unsqueeze(1).to_broadcast([128, NSB, 4])
                v = lambda a: a.rearrange("p (b t) -> p b t", t=4)
            xre = v(xs_[:, 0, :T]); xim = v(xs_[:, 1, :T])
            a1 = k.sb("s5a1", [128, 128], F32); a2 = k.sb("s5a2", [128, 128], F32); a3 = k.sb("s5a3", [128, 128], F32); a4 = k.sb("s5a4", [128, 128], F32)
            k.op("dve", "tensor_tensor", out=v(a1[:, :T]), in0=xre, in1=c_, op=ALU.mult)
            k.op("pool", "tensor_tensor", out=v(a2[:, :T]), in0=xim, in1=s_, op=ALU.mult)
            k.op("dve", "tensor_tensor", out=v(a3[:, :T]), in0=xim, in1=c_, op=ALU.mult)
            k.op("pool", "tensor_tensor", out=v(a4[:, :T]), in0=xre, in1=s_, op=ALU.mult)
            k.op("dve", "tensor_tensor", out=a1[:, :T], in0=a1[:, :T], in1=a2[:, :T], op=ALU.add)
            k.op("pool", "tensor_tensor", out=a3[:, :T], in0=a3[:, :T], in1=a4[:, :T], op=ALU.subtract)
            gre = k.sb("s5gre", [128, 128], F32); gim = k.sb("s5gim", [128, 128], F32)
            if i < NPT:
                rb = mag[:, j:j + 1].to_broadcast([128, T])
                k.op("dve", "tensor_tensor_scan", out=gre[:, :T], data0=rb, data1=a1[:, :T], initial=car[0][:, j:j + 1], op0=ALU.mult, op1=ALU.add)
                k.op("dve", "tensor_tensor_scan", out=gim[:, :T], data0=rb, data1=a3[:, :T], initial=car[1][:, j:j + 1], op0=ALU.mult, op1=ALU.add)
            else:
                rb = mag[:, j:j + 1].to_broadcast([128, 4])
                for bb_ in range(NSB):
                    sl = slice(bb_ * 4, bb_ * 4 + 4)
                    k.op("dve", "tensor_tensor_scan", out=gre[:, sl], data0=rb, data1=a1[:, sl], initial=H0[0][:, j, bb_:bb_ + 1], op0=ALU.mult, op1=ALU.add)
                    k.op("dve", "tensor_tensor_scan", out=gim[:, sl], data0=rb, data1=a3[:, sl], initial=H0[1][:, j, bb_:bb_ + 1], op0=ALU.mult, op1=ALU.add)
            hre = k.sb("s5hre", [128, 128], F32); him = k.sb("s5him", [128, 128], F32)
            k.op("dve", "tensor_tensor", out=v(a1[:, :T]), in0=v(gre[:, :T]), in1=c_, op=ALU.mult)
            k.op("pool", "tensor_tensor", out=v(a2[:, :T]), in0=v(gim[:, :T]), in1=s_, op=ALU.mult)
            k.op("dve", "tensor_tensor", out=v(a3[:, :T]), in0=v(gre[:, :T]), in1=s_, op=ALU.mult)
            k.op("pool", "tensor_tensor", out=v(a4[:, :T]), in0=v(gim[:, :T]), in1=c_, op=ALU.mult)
            k.op("dve", "tensor_tensor", out=hre[:, :T], in0=a1[:, :T], in1=a2[:, :T], op=ALU.subtract)
            k.op("pool", "tensor_tensor", out=him[:, :T], in0=a3[:, :T], in1=a4[:, :T], op=ALU.add)
            if i < NPT:
                k.op("act", "activation", out=car[0][:, j:j + 1], in_=hre[:, T - 1:T], func=AF.Copy)
                k.op("act", "activation", out=car[1][:, j:j + 1], in_=him[:, T - 1:T], func=AF.Copy)
            else:
                k.op("act", "activation", out=HS[0][:, j, :], in_=hre[:, 3:T:4], func=AF.Copy)
                k.op("act", "activation", out=HS[1][:, j, :], in_=him[:, 3:T:4], func=AF.Copy)
            hb = k.sb("s5hb", [128, 2, 128], BF16)
            k.op("act", "activation", out=hb[:, 0, :T], in_=hre[:, :T], func=AF.Copy)
            k.op("act", "activation", out=hb[:, 1, :T], in_=him[:, :T], func=AF.Copy)
            k.op("pe", "matmul", out=py[:T, j * 32:(j + 1) * 32], lhsT=hb[:, 0, :T], rhs=CM[0][:, j, :], start=True, stop=False)
            k.op("pe", "matmul", out=py[:T, j * 32:(j + 1) * 32], lhsT=hb[:, 1, :T], rhs=CM[1][:, j, :], start=False, stop=True)
        y = k.sb("s5y", [128, 512], F32); y2 = k.sb("s5y2", [128, 512], F32); z = k.sb("s5z", [128, 512], F32)
        k.op("dve", "tensor_tensor", out=y[:T], in0=uf[:T], in1=dvec[:T], op=ALU.mult)
        k.op("dve", "tensor_tensor", out=y[:T], in0=y[:T], in1=py[:T, :], op=ALU.add)
        k.op("act", "activation", out=y2[:T], in_=y[:T], func=AF.Square)
        k.op("dve", "tensor_scalar", out=y2[:T], in0=y2[:T], scalar1=0.044715, scalar2=1.0, op0=ALU.mult, op1=ALU.add)
        k.op("dve", "tensor_tensor", out=y2[:T], in0=y2[:T], in1=y[:T], op=ALU.mult)
        k.op("act", "activation", out=y2[:T], in_=y2[:T], func=AF.Sigmoid, scale=1.5957691216057308)
        k.op("dve", "tensor_tensor", out=z[:T], in0=y2[:T], in1=y[:T], op=ALU.mult)
        zb = k.sb("s5zb", [128, 512], BF16); zT = k.sb("s5zT", [128, 4, 128], BF16)
        k.op("act", "activation", out=zb[:T], in_=z[:T], func=AF.Copy)
        kb.transpose_chunks(zT, zb, 4, T)
        pg = k.ps("py", [128, 512], F32, bufs=2)
        kb.proj(pg[:T, :], zT, wglu, 4, T, 0, 512)
        gt_ = k.sb("s5g", [128, 512], F32)
        k.op("dve", "tensor_tensor", out=gt_[:T], in0=pg[:T, :], in1=bglu[:T], op=ALU.add)
        k.op("act", "activation", out=gt_[:T], in_=gt_[:T], func=AF.Sigmoid)
        ob = k.sb("s5ob", [128, 512], BF16)
        k.op("dve", "tensor_tensor", out=ob[:T], in0=gt_[:T], in1=z[:T], op=ALU.mult)
        k.dma("sp", out=S["OS5"][t0:t0 + T, :], in_=ob[:T])
    for ri, nm in enumerate(("s5p_re", "s5p_im")):
        pt = k.ps("px4", [128, 4, 2, 128], F32, bufs=2)[:, :, 0, :]
        k.op("pe", "transpose", out=pt[:NJ, 0, :], in_=car[ri][:], identity=kb.idf[:])
        o_ = k.sb("s5po", [NJ, 128], F32)
        k.op("act", "activation", out=o_[:], in_=pt[:NJ, 0, :], func=AF.Copy)
        k.dma("sp", out=O[nm], in_=o_[:])
    for ri, nm in enumerate(("s5s_re", "s5s_im")):
        o_ = k.sb("s5so", [NSB, 2048], F32)
        for j0 in range(0, NJ, 4):
            pt = k.ps("px4", [128, 4, 2, 128], F32, bufs=2)[:, :, 0, :]
            for j in range(j0, j0 + 4):
                k.op("pe", "transpose", out=pt[:NSB, j - j0, :], in_=HS[ri][:, j, :], identity=kb.idf[:])
            k.op("act", "activation", out=o_[:, j0 * 128:(j0 + 4) * 128].rearrange("p (a b) -> p a b", a=4), in_=pt[:NSB, :, :], func=AF.Copy)
        k.dma("sp", out=O[nm], in_=o_[:])


def phase_attn_prompt(nc, k, kb, I, O, S):
    trif = k.sb("trif", [128, 128], F32, bufs=1); trib = k.sb("trib", [128, 128], BF16, bufs=1)
    k.dma("sp", out=trif[:], in_=I["tri"])
    k.op("dve", "tensor_copy", out=trib[:], in_=trif[:])
    OATT = k.sb("OATT", [128, NPT, 512], BF16, bufs=1)
    sc = 96 ** -0.5
    for h in range(8):
        qT = k.sb("aqT", [128, SEQ], BF16); kT = k.sb("akT", [128, SEQ], BF16); V = k.sb("aV", [128, NPT, 66], BF16)
        k.dma("sp", out=qT[:96, :], in_=S["QT"][h, :, 0:SEQ])
        k.dma("sp", out=kT[:96, :], in_=S["KT"][h, :, 0:SEQ])
        k.dma("sp", out=V[:], in_=S["V"][0:SEQ, h, :].rearrange("(j p) e -> p j e", p=128))
        for g in range(8):
            po = [k.ps("po", [128, 512], F32, bufs=4) for _ in range(4)]
            for j in range(4 * g + 4):
                t_lo = max(0, j - 4 * g)
                q0 = (4 * g + t_lo) * 128; nq = (4 - t_lo) * 128
                ps = k.ps("psc", [128, 512], F32, bufs=2)
                k.op("pe", "matmul", out=ps[:, :nq], lhsT=kT[:96, j * 128:(j + 1) * 128], rhs=qT[:96, q0:q0 + nq], start=True, stop=True)
                pT = k.sb("apT", [128, 512], BF16, bufs=3)
                k.op("act", "activation", out=pT[:, :nq], in_=ps[:, :nq], func=AF.Exp, scale=sc)
                if j >= 4 * g:
                    k.op("dve", "tensor_tensor", out=pT[:, 0:128], in0=pT[:, 0:128], in1=trib[:], op=ALU.mult)
                for t in range(t_lo, 4):
                    col = (t - t_lo) * 128
                    k.op("pe", "matmul", out=po[t][:, 0:66], lhsT=pT[:, col:col + 128], rhs=V[:, j, :], start=(j == 0), stop=(j == 4 * g + t))
            for t in range(4):
                rinv = k.sb("arinv", [128, 1], F32)
                k.op("dve", "reciprocal", out=rinv[:], in_=po[t][:, 64:65])
                k.op("dve", "tensor_scalar", out=OATT[:, 4 * g + t, h * 64:(h + 1) * 64], in0=po[t][:, 0:64], scalar1=rinv[:, 0:1], scalar2=None, op0=ALU.mult)
    for i in range(NPT):
        k.dma("sp", out=S["OATT"][i * 128:(i + 1) * 128, :], in_=OATT[:, i, :])


def phase_attn_sample(nc, k, kb, I, O, S):
    wf = k.sb("pwf", [128, 2, 1024], F32, bufs=1)
    k.dma("sp", out=wf[:], in_=I["w_mla_ukv"].rearrange("(c p) n -> p c n", p=128))
    WukC = k.sb("WukC", [128, 2, 512], BF16, bufs=1); WuvC = k.sb("WuvC", [128, 2, 512], BF16, bufs=1)
    for c in range(2):
        w3 = wf[:, c, :].rearrange("p (h x) -> p h x", x=128)
        k.op("dve", "tensor_copy", out=WukC[:, c, :].rearrange("p (h d) -> p h d", d=64), in_=w3[:, :, 0:64])
        k.op("dve", "tensor_copy", out=WuvC[:, c, :].rearrange("p (h d) -> p h d", d=64), in_=w3[:, :, 64:128])
    WukT = k.sb("WukT", [64, 8, 256], BF16, bufs=1)
    for h in range(8):
        ptw = k.ps("pk", [128, 2, 512], F32, bufs=2)
        pt = ptw[:, 0, :].rearrange("p (a n) -> p a n", a=4)
        for c in range(2):
            k.op("pe", "transpose", out=pt[:64, c, :], in_=wf[:, c, h * 128:h * 128 + 64], identity=kb.idf[:])
        k.op("act", "activation", out=WukT[:, h, :].rearrange("p (c n) -> p c n", c=2), in_=pt[:64, 0:2, :], func=AF.Copy)
    gkn = k.sb("gkn", [64, 1], F32, bufs=1); gkr = k.sb("gkr", [32, 1], F32, bufs=1)
    k.dma("sp", out=gkn[:], in_=I["k_gain"][0:64].rearrange("(d o) -> d o", o=1))
    k.dma("sp", out=gkr[:], in_=I["k_gain"][64:96].rearrange("(d o) -> d o", o=1))
    qn = k.sb("pqn", [64, 8, NST], BF16, bufs=1); qr = k.sb("pqr", [32, 8, NST], BF16, bufs=1)
    k.dma("sp", out=qn[:], in_=S["QT"][:, 0:64, SEQ:NTOK].rearrange("h d t -> d h t"))
    k.dma("sp", out=qr[:], in_=S["QT"][:, 64:96, SEQ:NTOK].rearrange("h d t -> d h t"))
    qng = k.sb("pqng", [64, 8, NST], BF16, bufs=1); qrg = k.sb("pqrg", [32, 8, NST], BF16, bufs=1)
    k.op("dve", "tensor_scalar", out=qng[:], in0=qn[:], scalar1=gkn[:, 0:1], scalar2=None, op0=ALU.mult)
    k.op("dve", "tensor_scalar", out=qrg[:], in0=qr[:], scalar1=gkr[:, 0:1], scalar2=None, op0=ALU.mult)
    qabsT = k.sb("qabsT", [128, 2, 8, NST], BF16, bufs=1)
    for c in range(2):
        pqw = k.ps("pk", [128, 2, 512], F32, bufs=2)
        pq = pqw[:, 0, :]
        for h in range(8):
            k.op("pe", "matmul", out=pq[:, h * 64:(h + 1) * 64], lhsT=WukT[:64, h, c * 128:(c + 1) * 128], rhs=qng[:64, h, :], start=True, stop=True)
        k.op("act", "activation", out=qabsT[:, c, :, :].rearrange("p h t -> p (h t)"), in_=pq, func=AF.Copy)
    msf = k.sb("msf", [64, NSB * 32], F32, bufs=1); msb = k.sb("msb", [64, NSB, 32], BF16, bufs=1)
    k.dma("sp", out=msf[:], in_=I["maskS"])
    k.op("dve", "tensor_copy", out=msb[:].rearrange("p b x -> p (b x)"), in_=msf[:])
    OATTS = k.sb("OATTS", [64, 8, NST], BF16, bufs=1)
    ptab = k.sb("ptab", [128, NSB], I32, bufs=1)
    k.dma("sp", out=ptab[:], in_=I["ptab"].rearrange("b p -> p b"), allow_slow_non_contiguous=True)
    idx8 = k.sb("idx8", [128, NSB, 8], I32, bufs=1)
    for c in range(8):
        k.op("dve", "tensor_scalar", out=idx8[:, :, c], in0=ptab[:], scalar1=8.0, scalar2=float(c), op0=ALU.mult, op1=ALU.add)
    latv = I["cache_lat"].rearrange("n (a r) c -> (n a) (r c)", a=8)
    krv = I["cache_kr"].rearrange("n (a r) c -> (n a) (r c)", a=8)

    def block_prep(c_ap, kr_ap, R, tag=""):
        cb = k.sb("cb" + tag, [128, 258], BF16, bufs=3)
        k.op("pool", "tensor_copy", out=cb[:R, 0:256], in_=c_ap)
        k.op("dve", "memset", ap=cb[:R, 256:258], constant=1.0, _w=[cb[:]])
        krb = k.sb("krb", [128, 32], BF16)
        k.op("dve", "tensor_copy", out=krb[:R], in_=kr_ap)
        krss = k.sb("krss", [128, 1], F32); krsq = k.sb("krsq", [128, 32], F32)
        k.op("act", "activation", out=krsq[:R], in_=kr_ap, func=AF.Square, accum_out=krss[:R])
        pt = k.ps("ptb", [128, 8, 128], BF16, bufs=2)
        k.op("pe", "transpose", out=pt[:, 0, :R], in_=cb[:R, 0:128], identity=kb.idb[:R, :R])
        k.op("pe", "transpose", out=pt[:, 1, :R], in_=cb[:R, 128:256], identity=kb.idb[:R, :R])
        k.op("pe", "transpose", out=pt[:32, 2, :R], in_=krb[:R, :], identity=kb.idb[:R, :R])
        cT = k.sb("cT" + tag, [128, 3, 128], BF16, bufs=3)
        k.op("act", "activation", out=cT[:, 0:2, :R], in_=pt[:, 0:2, :R], func=AF.Copy)
        k.op("act", "activation", out=cT[:32, 2, :R], in_=pt[:32, 2, :R], func=AF.Copy)
        pkw = k.ps("pk", [128, 2, 512], F32, bufs=2)
        pk = pkw[:, 0, :]
        for c in range(2):
            k.op("pe", "matmul", out=pk[:R, :], lhsT=cT[:, c, :R], rhs=WukC[:, c, :], start=(c == 0), stop=(c == 1))
        sq = k.sb("psq1", [128, 512], F32, bufs=1)
        k.op("act", "activation", out=sq[:R], in_=pk[:R, :], func=AF.Square)
        ss = k.sb("pss" + tag, [128, 8], F32, bufs=3)
        k.op("dve", "tensor_reduce", out=ss[:R], in_=sq[:R].rearrange("p (h d) -> p h d", d=64), axis=AX.X, op=ALU.add)
        k.op("dve", "tensor_scalar", out=ss[:R], in0=ss[:R], scalar1=krss[:R, 0:1], scalar2=96 * EPS, op0=ALU.add, op1=ALU.add)
        k.op("act", "activation", out=ss[:R], in_=ss[:R], func=AF.Ln)
        k.op("act", "activation", out=ss[:R], in_=ss[:R], func=AF.Exp, scale=-0.5)
        return cb, cT, ss

    def score_acc(b, acc, cb, cT, rstd, R, first, last, mask=None):
        pscw = k.ps("psS", [128, 4, 32], F32, bufs=1)
        o3 = pscw[:R, 0, :].rearrange("p (h t) -> p h t", t=4)
        k.op("pe", "matmul", out=o3, lhsT=cT[:, 0, :R], rhs=qabsT[:, 0, :, b * 4:(b + 1) * 4], start=True, stop=False)
        k.op("pe", "matmul", out=o3, lhsT=cT[:, 1, :R], rhs=qabsT[:, 1, :, b * 4:(b + 1) * 4], start=False, stop=False)
        k.op("pe", "matmul", out=o3, lhsT=cT[:32, 2, :R], rhs=qrg[:32, :, b * 4:(b + 1) * 4], start=False, stop=True)
        sc_ = k.sb("psc_", [128, 8, 4], F32)
        k.op("dve", "tensor_tensor", out=sc_[:R], in0=o3, in1=rstd[:R].unsqueeze(2).to_broadcast([R, 8, 4]), op=ALU.mult)
        pT = k.sb("ppT", [128, 32], BF16)
        k.op("act", "activation", out=pT[:R], in_=sc_[:R].rearrange("p h t -> p (h t)"), func=AF.Exp)
        if mask is not None:
            k.op("dve", "tensor_tensor", out=pT[:R], in0=pT[:R], in1=mask, op=ALU.mult)
        k.op("pe", "matmul", out=acc[:32, 0:258], lhsT=pT[:R, :], rhs=cb[:R, :], start=first, stop=last)

    def group4(b, acc, Xg, XRg, first):
        cb4 = k.sb("cb4", [128, 4, 258], BF16, bufs=2)
        k.op("pool", "tensor_copy", out=cb4[:, :, 0:256], in_=Xg)
        k.op("dve", "memset", ap=cb4[:, :, 256:258], constant=1.0, _w=[cb4[:]])
        krb4 = k.sb("krb4", [128, 4, 32], BF16)
        k.op("dve", "tensor_copy", out=krb4[:], in_=XRg)
        krsq = k.sb("krsq4", [128, 4, 32], F32); krss4 = k.sb("krss4", [128, 4], F32)
        k.op("act", "activation", out=krsq[:], in_=XRg, func=AF.Square)
        k.op("dve", "tensor_reduce", out=krss4[:], in_=krsq[:], axis=AX.X, op=ALU.add)
        pt = k.ps("ptb", [128, 8, 128], BF16, bufs=2)
        for g in range(4):
            for c in range(2):
                k.op("pe", "transpose", out=pt[:, g * 2 + c, :], in_=cb4[:, g, c * 128:(c + 1) * 128], identity=kb.idb[:])
        cT4 = k.sb("cT4", [128, 8, 128], BF16, bufs=2)
        k.op("act", "activation", out=cT4[:], in_=pt[:], func=AF.Copy)
        ptk = k.ps("ptb", [128, 8, 128], BF16, bufs=2)
        for g in range(4):
            k.op("pe", "transpose", out=ptk[:32, g, :], in_=krb4[:, g, :], identity=kb.idb[:])
        krT4 = k.sb("krT4", [32, 4, 128], BF16, bufs=2)
        k.op("act", "activation", out=krT4[:], in_=ptk[:32, 0:4, :], func=AF.Copy)
        ss4 = k.sb("ss4", [128, 4, 8], F32, bufs=2)
        for pair in range(2):
            pk = k.ps("pk", [128, 2, 512], F32, bufs=2)
            for gg in range(2):
                g = pair * 2 + gg
                for c in range(2):
                    k.op("pe", "matmul", out=pk[:, gg, :], lhsT=cT4[:, g * 2 + c, :], rhs=WukC[:, c, :], start=(c == 0), stop=(c == 1))
            sq = k.sb("psq2", [128, 2, 512], F32)
            k.op("act", "activation", out=sq[:], in_=pk[:], func=AF.Square)
            k.op("dve", "tensor_reduce", out=ss4[:, pair * 2:(pair + 1) * 2, :], in_=sq[:].rearrange("p g (h d) -> p g h d", d=64), axis=AX.X, op=ALU.add)
        k.op("dve", "scalar_tensor_tensor", out=ss4[:], in0=ss4[:], scalar=96 * EPS, in1=krss4[:].unsqueeze(2).to_broadcast([128, 4, 8]), op0=ALU.add, op1=ALU.add)
        ssf = ss4[:].rearrange("p g h -> p (g h)")
        k.op("act", "activation", out=ssf, in_=ssf, func=AF.Ln)
        k.op("act", "activation", out=ssf, in_=ssf, func=AF.Exp, scale=-0.5)
        psc = k.ps("psS", [128, 4, 32], F32, bufs=1)
        for g in range(4):
            o3 = psc[:, g, :].rearrange("p (h t) -> p h t", t=4)
            k.op("pe", "matmul", out=o3, lhsT=cT4[:, g * 2, :], rhs=qabsT[:, 0, :, b * 4:(b + 1) * 4], start=True, stop=False)
            k.op("pe", "matmul", out=o3, lhsT=cT4[:, g * 2 + 1, :], rhs=qabsT[:, 1, :, b * 4:(b + 1) * 4], start=False, stop=False)
            k.op("pe", "matmul", out=o3, lhsT=krT4[:32, g, :], rhs=qrg[:32, :, b * 4:(b + 1) * 4], start=False, stop=True)
        sc4 = k.sb("sc4", [128, 32, 4], F32)
        k.op("dve", "tensor_tensor", out=sc4[:], in0=psc[:].rearrange("p g (h t) -> p (g h) t", t=4), in1=ssf.unsqueeze(2).to_broadcast([128, 32, 4]), op=ALU.mult)
        pT4 = k.sb("pT4", [128, 4, 32], BF16)
        k.op("act", "activation", out=pT4[:].rearrange("p g x -> p (g x)"), in_=sc4[:].rearrange("p a t -> p (a t)"), func=AF.Exp)
        for g in range(4):
            k.op("pe", "matmul", out=acc[:32, 0:258], lhsT=pT4[:, g, :], rhs=cb4[:, g, :], start=(first and g == 0), stop=False)

    latn = k.sb("latn", [64, 256], F32, bufs=1); krn = k.sb("krn", [64, 32], F32, bufs=1)
    k.dma("sp", out=latn[:], in_=O["lat"][SEQ:NTOK, :])
    k.dma("sp", out=krn[:], in_=O["kr"][SEQ:NTOK, :])
    cbn, cTn, rstdn = block_prep(latn[:], krn[:], 64, tag="n")

    def issue_gather(bq, c):
        X = k.sb("gX", [128, 16, 256], F32); XR = k.sb("gXR", [128, 16, 32], F32)
        k.dma("pool", out=X[:].rearrange("p r c -> p (r c)"), in_=latv, _method="indirect_dma_start", out_offset=None,
              in_offset=bass.IndirectOffsetOnAxis(ap=idx8[:, bq, c:c + 1], axis=0), _r=[idx8[:]])
        k.dma("pool", out=XR[:].rearrange("p r c -> p (r c)"), in_=krv, _method="indirect_dma_start", out_offset=None,
              in_offset=bass.IndirectOffsetOnAxis(ap=idx8[:, bq, c:c + 1], axis=0), _r=[idx8[:]])
        return X, XR
    chunks = [(bq, c) for bq in range(NSB) for c in range(8)]
    pending = issue_gather(*chunks[0])
    ci = 0
    for b in range(NSB):
        acc = k.ps("pacc", [128, 512], F32, bufs=1)
        for c in range(8):
            X, XR = pending
            ci += 1
            if ci < len(chunks):
                pending = issue_gather(*chunks[ci])
            for r0 in range(0, 16, 4):
                group4(b, acc, X[:, r0:r0 + 4, :], XR[:, r0:r0 + 4, :], first=(c == 0 and r0 == 0))
        score_acc(b, acc, cbn, cTn, rstdn, 64, first=False, last=True, mask=msb[:, b, :])
        rinv = k.sb("prinv", [128, 1], F32)
        k.op("dve", "reciprocal", out=rinv[:32], in_=acc[:32, 256:257])
        olat = k.sb("olat", [128, 256], BF16)
        k.op("dve", "tensor_scalar", out=olat[:32], in0=acc[:32, 0:256], scalar1=rinv[:32, 0:1], scalar2=None, op0=ALU.mult)
        pt = k.ps("ptb", [128, 8, 128], BF16, bufs=2)
        for c in range(2):
            k.op("pe", "transpose", out=pt[:, c, :32], in_=olat[:32, c * 128:(c + 1) * 128], identity=kb.idb[:32, :32])
        olT = k.sb("olT", [128, 2, 32], BF16)
        k.op("act", "activation", out=olT[:], in_=pt[:, 0:2, :32], func=AF.Copy)
        povw = k.ps("psS", [128, 4, 32], F32, bufs=1)
        pov = povw[:, 0, :]
        for h in range(8):
            for c in range(2):
                k.op("pe", "matmul", out=pov[:64, h * 4:(h + 1) * 4], lhsT=WuvC[:, c, h * 64:(h + 1) * 64], rhs=olT[:, c, h * 4:(h + 1) * 4], start=(c == 0), stop=(c == 1))
        k.op("act", "activation", out=OATTS[:, :, b * 4:(b + 1) * 4], in_=pov[:64, :].rearrange("p (h t) -> p h t", t=4), func=AF.Copy)
    k.dma("sp", out=S["OATTS"], in_=OATTS[:])


def phase_outproj(nc, k, kb, I, O, S):
    kb.mk_stage()
    w_out = kb.load_w("w_out", I["w_out_even"], D, D)
    w_o64 = k.sb("w_o64", [64, 8, D], BF16, bufs=1)
    src = I["w_out_even"][0:512, :].rearrange("(h d) n -> d h n", d=64)
    for h0 in range(0, 8, 2):
        st = kb.stage[(h0 // 2) % 2]
        sv = st[:64, 0:2 * D].rearrange("p (c n) -> p c n", c=2)
        k.dma("sp", out=sv, in_=src[:, h0:h0 + 2, :])
        k.op("dve", "tensor_copy", out=w_o64[:, h0:h0 + 2, :], in_=sv)
    OATTS = k.sb("OATTS2", [64, 8, NST], BF16, bufs=1)
    if not NOP["v"]:
        k.dma("sp", out=OATTS[:], in_=S["OATTS"])
    for (i, t0, R) in tiles():
        xt = k.sb("oxt", [128, D], F32)
        k.dma("sp", out=xt[:R], in_=(I["xp"][t0:t0 + R, :] if i < NPT else I["xs"]))
        os5 = k.sb("os5", [128, 512], BF16); s5T = k.sb("s5T", [128, 4, 128], BF16)
        k.dma("sp", out=os5[:R], in_=S["OS5"][t0:t0 + R, :])
        kb.transpose_chunks(s5T, os5, 4, R)
        if i < NPT:
            oat = k.sb("oat", [128, 512], BF16); oT = k.sb("oT", [128, 4, 128], BF16)
            k.dma("sp", out=oat[:R], in_=S["OATT"][t0:t0 + R, :])
            kb.transpose_chunks(oT, oat, 4, R)
        res = k.sb("ores", [128, D], F32)
        for half in range(2):
            n0 = half * 512
            pso = k.ps("pp", [128, 512], F32, bufs=4)
            if i < NPT:
                for c in range(4):
                    k.op("pe", "matmul", out=pso[:R, :], lhsT=oT[:, c, :R], rhs=w_out[:, c, n0:n0 + 512], start=(c == 0), stop=False)
            elif NOP["v"]:
                pass
            else:
                for h in range(8):
                    k.op("pe", "matmul", out=pso[:R, :], lhsT=OATTS[:64, h, :R], rhs=w_o64[:64, h, n0:n0 + 512], start=(h == 0), stop=False)
            for c in range(4):
                k.op("pe", "matmul", out=pso[:R, :], lhsT=s5T[:, c, :R], rhs=w_out[:, 4 + c, n0:n0 + 512], start=(c == 0 and i >= NPT and NOP["v"]), stop=(c == 3))
            k.op("dve", "tensor_tensor", out=res[:R, n0:n0 + 512], in0=pso[:R, :], in1=xt[:R, n0:n0 + 512], op=ALU.add)
        k.dma("sp", out=S["R1"][t0:t0 + R, :], in_=res[:R])
        if "dbg_r1" in O:
            k.dma("sp", out=O["dbg_r1"][t0:t0 + R, :], in_=res[:R])


def phase_mem(nc, k, kb, I, O, S, l, Rin, Rout):
    kb.mk_stage()
    g_mem = kb.bcast_load("g_mem", I["norm_mem"][l], D); g_q = kb.bcast_load("g_mq", I["mem_q_gain"][l], 128)
    wq = kb.load_w("wmq", I["w_mem_q"][l], D, 512); wo = kb.load_w("wmo", I["w_mem_o"][l], 512, D)
    MKT = k.sb("MKT", [128, 4, 256], BF16, bufs=1); MV = k.sb("MV", [128, 2, 4, 130], BF16, bufs=1)
    k.dma("sp", out=MKT[:], in_=S["MKT"][l].rearrange("h d m -> d h m"))
    k.dma("sp", out=MV[:], in_=S["MV"][l].rearrange("(mt p) h e -> p mt h e", p=128))
    ones = k.sb("ones", [128, 128], BF16, bufs=1)
    k.op("dve", "memset", ap=ones[:], constant=1.0, _w=[ones[:]])
    sc = 128 ** -0.5
    for (i, t0, R) in tiles():
        xt = k.sb("mxt", [128, D], F32); scr = k.sb("mscr", [128, D], F32, bufs=1)
        hn = k.sb("mhn", [128, D], BF16); hnT = k.sb("mhnT", [128, 8, 128], BF16)
        k.dma("sp", out=xt[:R], in_=Rin[t0:t0 + R, :])
        kb.rmsnorm(hn[:R], xt[:R], g_mem[:R], D, R, scr[:R])
        kb.transpose_chunks(hnT, hn, 8, R)
        pq = k.ps("pp", [128, 512], F32, bufs=2)
        kb.proj(pq[:R, :], hnT, wq, 8, R, 0, 512)
        qf = k.sb("mqf", [128, 4, 128], F32)
        k.op("act", "activation", out=qf[:R].rearrange("p h d -> p (h d)"), in_=pq[:R, :], func=AF.Copy)
        kb.head_norm(qf[:R], 4, 128, g_q, R)
        qb = k.sb("mqb", [128, 512], BF16); qT = k.sb("mqT", [128, 4, 128], BF16)
        k.op("act", "activation", out=qb[:R], in_=qf[:R].rearrange("p h d -> p (h d)"), func=AF.Copy)
        kb.transpose_chunks(qT, qb, 4, R)
        obT = k.sb("mobT", [128, 4, 128], BF16)
        if i < NPT:
            ob = k.sb("mob", [128, 512], BF16)
            for h in range(4):
                pss = k.ps("pp", [128, 512], F32, bufs=2)
                for mt in range(2):
                    k.op("pe", "matmul", out=pss[:, mt * 128:mt * 128 + R], lhsT=MKT[:, h, mt * 128:(mt + 1) * 128], rhs=qT[:, h, :R], start=True, stop=True)
                pT = k.sb("mpT", [128, 2, 128], BF16)
                k.op("act", "activation", out=pT[:, :, :R], in_=pss[:, 0:256].rearrange("p (a n) -> p a n", a=2)[:, :, :R], func=AF.Exp, scale=sc)
                po = k.ps("pp", [128, 512], F32, bufs=2)
                for mt in range(2):
                    k.op("pe", "matmul", out=po[:R, 0:130], lhsT=pT[:, mt, :R], rhs=MV[:, mt, h, :], start=(mt == 0), stop=(mt == 1))
                rinv = k.sb("mrinv", [128, 1], F32)
                k.op("dve", "reciprocal", out=rinv[:R], in_=po[:R, 128:129])
                k.op("dve", "tensor_scalar", out=ob[:R, h * 128:(h + 1) * 128], in0=po[:R, 0:128], scalar1=rinv[:R, 0:1], scalar2=None, op0=ALU.mult)
            kb.transpose_chunks(obT, ob, 4, R)
        else:
            for bq in range(NSB):
                Kb = k.sb("mKb", [128, 2, 512], F32); Vb = k.sb("mVb", [128, 2, 512], F32)
                k.dma("sp", out=Kb[:], in_=I["cmk"][l, bq].rearrange("(mt p) x -> p mt x", p=128))
                k.dma("sp", out=Vb[:], in_=I["cmv"][l, bq].rearrange("(mt p) x -> p mt x", p=128))
                Vbb = k.sb("mVbb", [128, 2, 512], BF16)
                k.op("pool", "tensor_copy", out=Vbb[:], in_=Vb[:])
                KbT = k.sb("mKbT", [128, 4, 256], BF16)
                for mt in range(2):
                    ptf = k.ps("ptf", [128, 4, 128], F32, bufs=2)
                    for h in range(4):
                        k.op("pe", "transpose", out=ptf[:, h, :], in_=Kb[:, mt, h * 128:(h + 1) * 128], identity=kb.idf[:])
                    k.op("act", "activation", out=KbT[:, :, mt * 128:(mt + 1) * 128], in_=ptf[:, :, :], func=AF.Copy)
                pss = k.ps("psS", [128, 32], F32, bufs=2)
                for mt in range(2):
                    for h in range(4):
                        c0 = (mt * 4 + h) * 4
                        k.op("pe", "matmul", out=pss[:, c0:c0 + 4], lhsT=KbT[:, h, mt * 128:(mt + 1) * 128], rhs=qT[:, h, bq * 4:(bq + 1) * 4], start=True, stop=True)
                pTb = k.sb("mpTb", [128, 32], BF16)
                k.op("act", "activation", out=pTb[:], in_=pss[:, :], func=AF.Exp, scale=sc)
                psl = k.ps("psS", [128, 32], F32, bufs=2)
                for mt in range(2):
                    k.op("pe", "matmul", out=psl[:, 0:16], lhsT=ones[:, :], rhs=pTb[:, mt * 16:(mt + 1) * 16], start=(mt == 0), stop=(mt == 1))
                rl = k.sb("mrl", [128, 16], F32)
                k.op("dve", "reciprocal", out=rl[:], in_=psl[:, 0:16])
                pso = k.ps("psS", [128, 32], F32, bufs=2)
                for h in range(4):
                    for mt in range(2):
                        c0 = (mt * 4 + h) * 4
                        k.op("pe", "matmul", out=pso[:, h * 4:(h + 1) * 4], lhsT=Vbb[:, mt, h * 128:(h + 1) * 128], rhs=pTb[:, c0:c0 + 4], start=(mt == 0), stop=(mt == 1))
                k.op("dve", "tensor_tensor", out=obT[:, :, bq * 4:(bq + 1) * 4], in0=pso[:, 0:16].rearrange("p (h t) -> p h t", t=4),
                     in1=rl[:].rearrange("p (h t) -> p h t", t=4), op=ALU.mult)
        res = k.sb("mres", [128, D], F32)
        for half in range(2):
            n0 = half * 512
            pso2 = k.ps("pp", [128, 512], F32, bufs=2)
            for h in range(4):
                k.op("pe", "matmul", out=pso2[:R, :], lhsT=obT[:, h, :R], rhs=wo[:, h, n0:n0 + 512], start=(h == 0), stop=(h == 3))
            k.op("dve", "tensor_tensor", out=res[:R, n0:n0 + 512], in0=pso2[:R, :], in1=xt[:R, n0:n0 + 512], op=ALU.add)
        k.dma("sp", out=Rout(i, t0, R), in_=res[:R])
        if i >= NPT and DBG.get("ap") is not None:
            k.dma("sp", out=DBG["ap"][DBG["n"]], in_=res[:R])
            DBG["n"] += 1


def phase_mlp(nc, k, kb, I, O, S, l, Rin, Rout):
    kb.mk_stage(512)
    g_mlp = kb.bcast_load("g_mlp", I["norm_mlp"][l], D)
    wup = kb.load_w("wup", I["w_mlp_up"][l], D, 4096, stage_cols=512)
    wdn = kb.load_w("wdn", I["w_mlp_down"][l], 4096, D, stage_cols=512)
    groups = [(g * 4, [(g * 4 + s, (g * 4 + s) * 128, 128) for s in range(4)]) for g in range(NPT // 4)] + [(NPT, [(NPT, SEQ, NST)])]
    for (_, subs) in groups:
        NT = sum(R for (_, _, R) in subs)
        xt4 = k.sb("fxt4", [128, 4, D], F32, bufs=1)
        hnT4 = k.sb("fhnT4", [128, 8, 512], BF16, bufs=1)
        for s, (i, t0, R) in enumerate(subs):
            res = k.sb("fres", [128, D], F32, bufs=1)
            hn = k.sb("fhn", [128, D], BF16)
            k.dma("sp", out=xt4[:R, s, :], in_=Rin[t0:t0 + R, :])
            kb.rmsnorm(hn[:R], xt4[:R, s, :], g_mlp[:R], D, R, res[:R])
            kb.transpose_chunks(hnT4[:, :, s * 128:(s + 1) * 128], hn, 8, R)
        hT = k.sb("fhT", [128, 32, 512], BF16, bufs=1)
        for fc in range(32):
            pu = k.ps("pp", [128, 512], F32, bufs=4)
            for kc in range(8):
                k.op("pe", "matmul", out=pu[:, :NT], lhsT=wup[:, kc, fc * 128:(fc + 1) * 128], rhs=hnT4[:, kc, :NT], start=(kc == 0), stop=(kc == 7))
            rl = k.sb("frl", [128, 512], F32)
            k.op("act", "activation", out=rl[:, :NT], in_=pu[:, :NT], func=AF.Relu)
            k.op("pool" if fc % 2 else "dve", "tensor_tensor", out=hT[:, fc, :NT], in0=rl[:, :NT], in1=rl[:, :NT], op=ALU.mult)
        for s, (i, t0, R) in enumerate(subs):
            res = k.sb("fres", [128, D], F32, bufs=1)
            for half in range(2):
                n0 = half * 512
                pd = k.ps("pp", [128, 512], F32, bufs=4)
                for fc in range(32):
                    k.op("pe", "matmul", out=pd[:R, :], lhsT=hT[:, fc, s * 128:s * 128 + R], rhs=wdn[:, fc, n0:n0 + 512], start=(fc == 0), stop=(fc == 31))
                k.op("dve", "tensor_tensor", out=res[:R, n0:n0 + 512], in0=pd[:R, :], in1=xt4[:R, s, n0:n0 + 512], op=ALU.add)
            k.dma("sp", out=Rout(i, t0, R), in_=res[:R])
            if i >= NPT and DBG.get("ap") is not None:
                k.dma("sp", out=DBG["ap"][DBG["n"]], in_=res[:R])
                DBG["n"] += 1


def phase_hgrn(nc, k, kb, I, O, S, Rin, Rout):
    kb.mk_stage(1024)
    g_mix = kb.bcast_load("g_mix1", I["norm_mix"][1], D); g_on = kb.bcast_load("g_on", I["hgrn_on"], D)
    w_in = kb.load_w("w_ino", I["w_in_odd"], D, 4096, stage_cols=1024)
    w_out = kb.load_w("w_outo", I["w_out_odd"], D, D, stage_cols=1024)
    a0 = k.sb("lba0", [128, 8], F32, bufs=1); a1 = k.sb("lba1", [128, 8], F32, bufs=1)
    k.dma("sp", out=a0[:], in_=I["hgrn_lb"][0].rearrange("(h d) -> d h", d=128), allow_slow_non_contiguous=True)
    k.dma("sp", out=a1[:], in_=I["hgrn_lb"][1].rearrange("(h d) -> d h", d=128), allow_slow_non_contiguous=True)
    k.op("act", "activation", out=a0[:], in_=a0[:], func=AF.Exp)
    k.op("act", "activation", out=a1[:], in_=a1[:], func=AF.Exp)
    lb = k.sb("lb", [128, 8], F32, bufs=1); oml = k.sb("oml", [128, 8], F32, bufs=1)
    k.op("dve", "tensor_tensor", out=a0[:], in0=a0[:], in1=a1[:], op=ALU.add)
    k.op("dve", "reciprocal", out=a0[:], in_=a0[:])
    k.op("dve", "tensor_tensor", out=lb[:], in0=a1[:], in1=a0[:], op=ALU.mult)
    k.op("dve", "tensor_scalar", out=oml[:], in0=lb[:], scalar1=-1.0, scalar2=1.0, op0=ALU.mult, op1=ALU.add)
    def cload(nm, shape, dt=BF16):
        f_ = k.sb(nm + "f", shape, F32, bufs=1)
        k.dma("sp", out=f_[:], in_=I[nm])
        if dt == F32:
            return f_
        t_ = k.sb(nm + "b", shape, BF16, bufs=1)
        k.op("dve", "tensor_copy", out=t_[:], in_=f_[:])
        return t_
    rmask512 = k.sb("rmask512", [128, 512], F32, bufs=1)
    k.dma("sp", out=rmask512[:].rearrange("p (a n) -> p a n", a=4), in_=I["rmask64"].unsqueeze(1).to_broadcast([128, 4, 128]))
    rmask64 = cload("rmask64", [128, 128], F32); rmask4 = cload("rmask4", [128, 64], F32)
    bd64 = cload("bd64", [128, 128]); bd4 = cload("bd4", [64, 64])
    qmask = cload("qmask", [128, 256]); bmask = cload("bmask", [128, 1024]); rowmask = cload("rowmask", [64, 16], F32)
    Sf = k.sb("Sf", [128, 8, 128], F32, bufs=1); Sb = k.sb("Sb", [128, 8, 128], BF16, bufs=1)
    k.op("dve", "memset", ap=Sf[:], constant=0.0, _w=[Sf[:]])
    k.op("dve", "memset", ap=Sb[:], constant=0.0, _w=[Sb[:]])
    for (i, t0, R) in tiles():
        samp = i >= NPT
        xt = k.sb("hxt", [128, D], F32, bufs=1); scr = k.sb("hscr", [128, D], F32, bufs=1)
        hn = k.sb("hhn", [128, D], BF16, bufs=1); hnT = k.sb("hhnT", [128, 8, 128], BF16, bufs=1)
        k.dma("sp", out=xt[:R], in_=Rin[t0:t0 + R, :])
        kb.rmsnorm(hn[:R], xt[:R], g_mix[:R], D, R, scr[:R])
        kb.transpose_chunks(hnT, hn, 8, R)
        vb = k.sb("hvb", [128, D], BF16, bufs=1); gs = k.sb("hgs", [128, D], F32, bufs=1)
        for half in range(2):
            pv = k.ps("pp", [128, 512], F32, bufs=3)
            kb.proj(pv[:R, :], hnT, w_in, 8, R, 2048 + half * 512, 2048 + half * 512 + 512)
            k.op("act", "activation", out=vb[:R, half * 512:(half + 1) * 512], in_=pv[:R, :], func=AF.Copy)
            pg = k.ps("pp", [128, 512], F32, bufs=3)
            kb.proj(pg[:R, :], hnT, w_in, 8, R, 3072 + half * 512, 3072 + half * 512 + 512)
            k.op("act", "activation", out=gs[:R, half * 512:(half + 1) * 512], in_=pg[:R, :], func=AF.Silu)
        otile = k.sb("hot", [128, D], F32, bufs=1)
        rmask = rmask4 if samp else rmask64
        for h in range(8):
            if samp:
                S0 = k.sb("hS0", [128, NSB, 128], F32, bufs=1); S0b = k.sb("hS0b", [128, NSB, 128], BF16, bufs=1)
                k.dma("sp", out=S0[:], in_=I["hg0"][:, h].rearrange("b d e -> d b e"))
                k.op("pool", "tensor_copy", out=S0b[:], in_=S0[:])
            hh = h % 4
            if hh == 0:
                pq4 = k.ps("pq4", [128, 2, 512], F32, bufs=1)
                for h2 in range(4):
                    hx = h + h2
                    for kc in range(8):
                        k.op("pe", "matmul", out=pq4[:, 0, h2 * 128:h2 * 128 + R], lhsT=w_in[:, kc, hx * 128:(hx + 1) * 128], rhs=hnT[:, kc, :R], start=(kc == 0), stop=(kc == 7))
                    for kc in range(8):
                        k.op("pe", "matmul", out=pq4[:, 1, h2 * 128:h2 * 128 + R], lhsT=w_in[:, kc, 1024 + hx * 128:1024 + (hx + 1) * 128], rhs=hnT[:, kc, :R], start=(kc == 0), stop=(kc == 7))
                q4 = pq4[:, 0, :].rearrange("p (a n) -> p a n", a=4)[:, :, :R]
                f4 = pq4[:, 1, :].rearrange("p (a n) -> p a n", a=4)[:, :, :R]
                sig4 = k.sb("hsig4", [128, 4, 128], F32, bufs=1); fg4 = k.sb("hfg4", [128, 4, 128], F32, bufs=1); logf4 = k.sb("hlogf4", [128, 4, 128], F32, bufs=1)
                kk4 = k.sb("hkk4", [128, 4, 128], F32, bufs=1); bc4 = k.sb("hbc4", [128, 4, 128], F32, bufs=1); eb4 = k.sb("heb4", [128, 4, 128], F32); enb4 = k.sb("henb4", [128, 4, 128], F32, bufs=1)
                qs4 = k.sb("hqs4", [128, 4, 128], F32, bufs=1); qt4 = k.sb("hqt4", [128, 4, 128], BF16); kt4 = k.sb("hkt4", [128, 4, 128], BF16)
                V = lambda t_: t_[:, :, :R]
                k.op("act", "activation", out=V(sig4), in_=f4, func=AF.Sigmoid)
                k.op("dve", "tensor_tensor", out=V(fg4), in0=V(sig4), in1=oml[:, h:h + 4].unsqueeze(2).to_broadcast([128, 4, R]), op=ALU.mult)
                k.op("dve", "tensor_tensor", out=V(fg4), in0=V(fg4), in1=lb[:, h:h + 4].unsqueeze(2).to_broadcast([128, 4, R]), op=ALU.add)
                k.op("act", "activation", out=V(logf4), in_=V(fg4), func=AF.Ln)
                k.op("pool", "tensor_scalar", out=V(kk4), in0=V(fg4), scalar1=-1.0, scalar2=1.0, op0=ALU.mult, op1=ALU.add)
                if not samp:
                    k.op("dve", "tensor_tensor_scan", out=bc4[:].rearrange("p a n -> p (a n)"), data0=rmask512[:, :], data1=logf4[:].rearrange("p a n -> p (a n)"), initial=0.0, op0=ALU.mult, op1=ALU.add)
                else:
                    for h2 in range(4):
                        k.op("dve", "tensor_tensor_scan", out=bc4[:, h2, :R], data0=rmask[:, :R], data1=logf4[:, h2, :R], initial=0.0, op0=ALU.mult, op1=ALU.add)
                k.op("act", "activation", out=V(eb4), in_=V(bc4), func=AF.Exp)
                k.op("act", "activation", out=V(enb4), in_=V(bc4), func=AF.Exp, scale=-1.0)
                k.op("act", "activation", out=V(qs4), in_=q4, func=AF.Silu)
                k.op("dve", "tensor_tensor", out=V(qt4), in0=V(qs4), in1=V(eb4), op=ALU.mult)
                k.op("pool", "tensor_tensor", out=V(kt4), in0=V(kk4), in1=V(enb4), op=ALU.mult)
            qt_ = qt4[:, hh, :]; kt_ = kt4[:, hh, :]; eb = eb4[:, hh, :]
            ptk = k.ps("ptb", [128, 8, 128], BF16, bufs=2)
            k.op("pe", "transpose", out=ptk[:R, 0, :], in_=kt_[:, :R], identity=kb.idb[:])
            Ktok = k.sb("hKtok", [128, 128], BF16)
            k.op("act", "activation", out=Ktok[:R, :], in_=ptk[:R, 0, :], func=AF.Copy)
            pat = k.ps("pp", [128, 512], F32, bufs=3)
            k.op("pe", "matmul", out=pat[:R, 0:R], lhsT=kt_[:, :R], rhs=qt_[:, :R], start=True, stop=True)
            attb = k.sb("hattb", [128, 128], BF16)
            k.op("dve", "tensor_tensor", out=attb[:R, :R], in0=pat[:R, 0:R], in1=(bd4[:, :] if samp else bd64[:, :]), op=ALU.mult)
            po = k.ps("pp", [128, 512], F32, bufs=3)
            k.op("pe", "matmul", out=po[:R, 0:128], lhsT=attb[:R, :R], rhs=vb[:R, h * 128:(h + 1) * 128], start=True, stop=False)
            if not samp:
                qz = k.sb("hqz", [128, 2, 128], BF16)
                k.op("dve", "tensor_tensor", out=qz[:], in0=qt_[:, :].unsqueeze(1).to_broadcast([128, 2, 128]), in1=qmask[:, :].rearrange("p (c t) -> p c t", c=2), op=ALU.mult)
                for c in range(2):
                    k.op("pe", "matmul", out=po[:, 0:128], lhsT=qz[:, c, :], rhs=Sb[:, h, :], start=False, stop=(c == 1))
                    pP = k.ps("pp", [128, 512], F32, bufs=3)
                    k.op("pe", "matmul", out=pP[:, 0:128], lhsT=Ktok[c * 64:(c + 1) * 64, :], rhs=vb[c * 64:(c + 1) * 64, h * 128:(h + 1) * 128], start=True, stop=True)
                    tmp = k.sb("htmp", [128, 128], F32)
                    k.op("dve", "tensor_tensor", out=tmp[:], in0=pP[:, 0:128], in1=Sf[:, h, :], op=ALU.add)
                    k.op("act", "activation", out=Sf[:, h, :], in_=tmp[:], func=AF.Copy, scale=eb[:, c * 64 + 63:c * 64 + 64])
                    k.op("pool", "tensor_copy", out=Sb[:, h, :], in_=Sf[:, h, :])
            else:
                QB = k.sb("hQB", [128, NSB, 64], BF16, bufs=1)
                k.op("dve", "tensor_tensor", out=QB[:], in0=qt_[:, :64].unsqueeze(1).to_broadcast([128, NSB, 64]), in1=bmask[:, :].rearrange("p (b t) -> p b t", b=NSB), op=ALU.mult)
                KB = k.sb("hKB", [64, NSB, 128], BF16, bufs=1)
                k.op("dve", "tensor_tensor", out=KB[:], in0=Ktok[:64, :].unsqueeze(1).to_broadcast([64, NSB, 128]), in1=rowmask[:, :].unsqueeze(2).to_broadcast([64, NSB, 128]), op=ALU.mult)
                for bq in range(NSB):
                    k.op("pe", "matmul", out=po[:R, 0:128], lhsT=QB[:, bq, :], rhs=S0b[:, bq, :], start=False, stop=(bq == NSB - 1))
                ebl = k.sb("hebl", [128, NSB], F32)
                k.op("act", "activation", out=ebl[:], in_=eb[:, 3:64:4], func=AF.Copy)
                Sn = k.sb("hSn", [128, NSB, 128], F32, bufs=1)
                for b0 in range(0, NSB, 4):
                    pP = k.ps("pPs", [128, 512], F32, bufs=1)
                    for bq in range(b0, b0 + 4):
                        k.op("pe", "matmul", out=pP[:, (bq - b0) * 128:(bq - b0 + 1) * 128], lhsT=KB[:64, bq, :], rhs=vb[:64, h * 128:(h + 1) * 128], start=True, stop=True)
                    tmp4 = k.sb("htmp4", [128, 4, 128], F32, bufs=1)
                    k.op("dve", "tensor_tensor", out=tmp4[:], in0=pP[:, :].rearrange("p (a n) -> p a n", a=4), in1=S0[:, b0:b0 + 4, :], op=ALU.add)
                    k.op("pool", "tensor_tensor", out=Sn[:, b0:b0 + 4, :], in0=tmp4[:], in1=ebl[:, b0:b0 + 4].unsqueeze(2).to_broadcast([128, 4, 128]), op=ALU.mult)
                k.dma("sp", out=O["hgs"][:, h].rearrange("b d e -> d b e"), in_=Sn[:])
            k.op("act", "activation", out=otile[:R, h * 128:(h + 1) * 128], in_=po[:R, 0:128], func=AF.Copy)
        on = k.sb("hon", [128, D], F32, bufs=1); ob = k.sb("hob", [128, D], BF16, bufs=1); obT = k.sb("hobT", [128, 8, 128], BF16, bufs=1)
        kb.rmsnorm(on[:R], otile[:R], g_on[:R], D, R, scr[:R])
        k.op("dve", "tensor_tensor", out=ob[:R], in0=on[:R], in1=gs[:R], op=ALU.mult)
        kb.transpose_chunks(obT, ob, 8, R)
        res = k.sb("hres", [128, D], F32, bufs=1)
        for half in range(2):
            n0 = half * 512
            pso = k.ps("pp", [128, 512], F32, bufs=3)
            kb.proj(pso[:R, :], obT, w_out, 8, R, n0, n0 + 512)
            k.op("dve", "tensor_tensor", out=res[:R, n0:n0 + 512], in0=pso[:R, :], in1=xt[:R, n0:n0 + 512], op=ALU.add)
        k.dma("sp", out=Rout(i, t0, R), in_=res[:R])
        if i >= NPT and DBG.get("ap") is not None:
            k.dma("sp", out=DBG["ap"][DBG["n"]], in_=res[:R])
            DBG["n"] += 1
    k.dma("sp", out=O["hgp"].rearrange("h d e -> d h e"), in_=Sf[:])


def build(stages="A"):
    nc = bass.Bass("TRN2", target_bir_lowering=False)
    NOP["v"] = "P" not in stages

    def din(name, shape, dt=F32):
        return nc.dram_tensor(name, list(shape), dt, kind="ExternalInput").ap()

    def dout(name, shape, dt=F32):
        return nc.dram_tensor(name, list(shape), dt, kind="ExternalOutput").ap()

    def dscr(name, shape, dt=F32):
        return nc.dram_tensor(name, list(shape), dt, kind="Internal").ap()

    I = {}
    I["xp"] = din("xp", [SEQ, D]); I["xs"] = din("xs", [NST, D])
    I["mem"] = din("mem", [256, D])
    I["ident"] = din("ident", [128, 128]); I["rope"] = din("rope", [NTOK + 64, 32])
    for nm, shp in [("norm_mix", [2, D]), ("norm_mem", [2, D]), ("norm_memsrc", [2, D]), ("norm_mlp", [2, D]),
                    ("w_mem_q", [2, D, 512]), ("w_mem_k", [2, D, 512]), ("w_mem_v", [2, D, 512]), ("w_mem_o", [2, 512, D]),
                    ("mem_q_gain", [2, 128]), ("mem_k_gain", [2, 128]),
                    ("w_mlp_up", [2, D, 4096]), ("w_mlp_down", [2, 4096, D]),
                    ("w_in_even", [D, 1568]), ("mla_cq_norm", [768]), ("mla_ckv_norm", [256]),
                    ("w_mla_uq", [768, 768]), ("w_mla_ukv", [256, 1024]),
                    ("q_gain", [96]), ("k_gain", [96])]:
        I[nm] = din(nm, shp)
    for nm, shp in [("s5_lre", [32, 64]), ("s5_lim", [32, 64]), ("s5_ls", [32]), ("s5_bre", [32, 64, 16]), ("s5_bim", [32, 64, 16]),
                    ("s5_cre", [32, 16, 64]), ("s5_cim", [32, 16, 64]), ("s5_d", [512]), ("s5_wglu", [512, 512]), ("s5_bglu", [512]),
                    ("s5_h0re", [NSB, 2048]), ("s5_h0im", [NSB, 2048]), ("ramp", [128, 128])]:
        I[nm] = din(nm, shp)
    O = {}
    O["yp"] = dout("o_yp", [SEQ, D]); O["ys"] = dout("o_ys", [NST, D])
    O["hgp"] = dout("o_hgp", [8, 128, 128]); O["hgs"] = dout("o_hgs", [NSB, 8, 128, 128])
    O["s5p_re"] = dout("o_s5p_re", [16, 128]); O["s5p_im"] = dout("o_s5p_im", [16, 128])
    O["s5s_re"] = dout("o_s5s_re", [NSB, 2048]); O["s5s_im"] = dout("o_s5s_im", [NSB, 2048])
    O["lat"] = dout("o_lat", [NTOK, 256]); O["kr"] = dout("o_kr", [NTOK, 32])
    O["mk"] = dout("o_mk", [2, 256, 512]); O["mv"] = dout("o_mv", [2, 256, 512])
    S = {}
    S["QT"] = dscr("s_qt", [8, 96, NTOK], BF16); S["KT"] = dscr("s_kt", [8, 96, NTOK], BF16)
    S["V"] = dscr("s_v", [NTOK, 8, 66], BF16)
    S["U"] = dscr("s_u", [NTOK, 512], F32); S["UT"] = dscr("s_ut", [4, 128, NTOK], BF16)
    S["MKT"] = dscr("s_mkt", [2, 4, 128, 256], BF16); S["MV"] = dscr("s_mva", [2, 256, 4, 130], BF16)

    S["OS5"] = dscr("s_os5", [NTOK, 512], BF16)
    S["OATT"] = dscr("s_oatt", [NTOK, 512], BF16); S["OATTS"] = dscr("s_oatts", [64, 8, NST], BF16)
    S["R1"] = dscr("s_r1", [NTOK, D], F32)
    for nm in ("R2", "R3", "R4", "R5"):
        S[nm] = dscr("s_" + nm, [NTOK, D], F32)
    I["cmk"] = din("cmk", [2, NSB, 256, 512]); I["cmv"] = din("cmv", [2, NSB, 256, 512])
    I["hgrn_on"] = din("hgrn_on", [D]); I["w_in_odd"] = din("w_in_odd", [D, 4096]); I["w_out_odd"] = din("w_out_odd", [D, D])
    I["hgrn_lb"] = din("hgrn_lb", [2, D]); I["hg0"] = din("hg0", [NSB, 8, 128, 128])
    for nm, shp in (("rmask64", [128, 128]), ("rmask4", [128, 64]), ("bd64", [128, 128]), ("bd4", [64, 64]), ("qmask", [128, 256]),
                    ("bmask", [128, 1024]), ("rowmask", [64, 16])):
        I[nm] = din(nm, shp)
    I["tri"] = din("tri", [128, 128]); I["maskS"] = din("maskS", [64, NSB * 32])
    I["w_out_even"] = din("w_out_even", [D, D])
    if "P" in stages:
        I["cache_lat"] = din("cache_lat", [NPHYS, 128, 256]); I["cache_kr"] = din("cache_kr", [NPHYS, 128, 32])
        I["ptab"] = din("ptab", [NSB, NPAGES], I32)
    if "d" in stages:
        O["dbg_r1"] = dout("dbg_r1", [NTOK, D])
    DBG["ap"] = dout("dbg_s", [5, NST, D]) if "d" in stages else None
    DBG["n"] = 0
    global DECLARED
    DECLARED = set(I.keys())
    with ExitStack() as gst:
        k = Sched(nc, gst)
        kb = KB(nc, k)
        k.begin_phase()
        kb.idf = k.gsb("idf", [128, 128], F32); kb.idb = k.gsb("idb", [128, 128], BF16)
        k.dma("sp", out=kb.idf[:], in_=I["ident"])
        k.op("dve", "tensor_copy", out=kb.idb[:], in_=kb.idf[:])
        k.end_phase()

        k.begin_phase()
        kb.mk_stage()
        for l in (range(2) if "M" in stages else []):
            g_src = kb.bcast_load("g_src", I["norm_memsrc"][l], D)
            g_k = kb.bcast_load("g_mk", I["mem_k_gain"][l], 128)
            wk = kb.load_w("wmk", I["w_mem_k"][l], D, 512)
            wv = kb.load_w("wmv", I["w_mem_v"][l], D, 512)
            for mt in range(2):
                R = 128
                xt = k.sb("mx", [128, D], F32); scr = k.sb("mscr", [128, D], F32)
                mn = k.sb("mn", [128, D], BF16); mnT = k.sb("mnT", [128, 8, 128], BF16)
                k.dma("sp", out=xt[:], in_=I["mem"][mt * 128:(mt + 1) * 128, :])
                kb.rmsnorm(mn[:], xt[:], g_src[:], D, R, scr[:])
                kb.transpose_chunks(mnT, mn, 8, R)
                pk = k.ps("pp", [128, 512], bufs=6); pv = k.ps("pp", [128, 512], bufs=6)
                kb.proj(pk[:], mnT, wk, 8, R, 0, 512)
                kb.proj(pv[:], mnT, wv, 8, R, 0, 512)
                kf = k.sb("mkf", [128, 4, 128], F32); vf = k.sb("mvf", [128, 4, 128], F32)
                k.op("act", "activation", out=kf[:], in_=pk[:].rearrange("p (h d) -> p h d", h=4), func=AF.Copy)
                k.op("act", "activation", out=vf[:], in_=pv[:].rearrange("p (h d) -> p h d", h=4), func=AF.Copy)
                kb.head_norm(kf[:], 4, 128, g_k, R)
                k.dma("sp", out=O["mk"][l, mt * 128:(mt + 1) * 128, :], in_=kf[:].rearrange("p h d -> p (h d)"))
                k.dma("sp", out=O["mv"][l, mt * 128:(mt + 1) * 128, :], in_=vf[:].rearrange("p h d -> p (h d)"))
                kbf = k.sb("mkb", [128, 512], BF16)
                k.op("dve", "tensor_copy", out=kbf[:], in_=kf[:].rearrange("p h d -> p (h d)"))
                kT = k.sb("mkT", [128, 4, 128], BF16)
                kb.transpose_chunks(kT, kbf, 4, R)
                k.dma("sp", out=S["MKT"][l, :, :, mt * 128:(mt + 1) * 128].rearrange("h d m -> d h m"), in_=kT[:])
                va = k.sb("mva", [128, 4, 130], BF16)
                k.op("dve", "memset", ap=va[:, :, 128:130], constant=1.0, _w=[va[:]])
                k.op("dve", "tensor_copy", out=va[:, :, 0:128], in_=vf[:])
                k.dma("sp", out=S["MV"][l, mt * 128:(mt + 1) * 128], in_=va[:])
        k.end_phase()

        k.begin_phase()
        kb.mk_stage()
        g_mix = kb.bcast_load("g_mix", I["norm_mix"][0], D)
        g_cq = kb.bcast_load("g_cq", I["mla_cq_norm"], 768)
        g_ckv = kb.bcast_load("g_ckv", I["mla_ckv_norm"], 256)
        g_q = kb.bcast_load("g_q", I["q_gain"], 96)
        g_kk = kb.bcast_load("g_kk", I["k_gain"], 96)
        w_in = kb.load_w("w_in", I["w_in_even"], D, 1568, stage_cols=1568)
        w_uq = kb.load_w("w_uq", I["w_mla_uq"], 768, 768)
        w_ukv = kb.load_w("w_ukv", I["w_mla_ukv"], 256, 1024)
        for (i, t0, R) in (tiles() if "1" in stages else []):
            if "x" in stages and i not in (0, 32):
                continue
            xsrc = I["xp"][t0:t0 + R, :] if i < NPT else I["xs"]
            xt = k.sb("xt", [128, D], F32); scr = k.sb("scr", [128, D], F32)
            hn = k.sb("hn", [128, D], BF16); hnT = k.sb("hnT", [128, 8, 128], BF16)
            k.dma("sp", out=xt[:R], in_=xsrc)
            cs = k.sb("cs", [128, 32], F32)
            k.dma("sp", out=cs[:R], in_=I["rope"][t0:t0 + R, :])
            kb.rmsnorm(hn[:R], xt[:R], g_mix[:R], D, R, scr[:R])
            kb.transpose_chunks(hnT, hn, 8, R)
            pA = k.ps("pp", [128, 512], bufs=6); pB = k.ps("pp", [128, 512], bufs=6); pC = k.ps("pp", [128, 512], bufs=6); pD = k.ps("pp", [128, 512], bufs=6)
            kb.proj(pA[:R, :], hnT, w_in, 8, R, 0, 512)
            kb.proj(pB[:R, 0:256], hnT, w_in, 8, R, 512, 768)
            kb.proj(pC[:R, 0:288], hnT, w_in, 8, R, 768, 1056)
            kb.proj(pD[:R, :], hnT, w_in, 8, R, 1056, 1568)
            if CUT <= 1:
                continue
            ssa = k.sb("ssa", [128, 1], F32); ssb = k.sb("ssb", [128, 1], F32)
            k.op("act", "activation", out=scr[:R, 0:512], in_=pA[:R, :], func=AF.Square, accum_out=ssa[:R])
            k.op("act", "activation", out=scr[:R, 512:768], in_=pB[:R, 0:256], func=AF.Square, accum_out=ssb[:R])
            k.op("dve", "tensor_tensor", out=ssa[:R], in0=ssa[:R], in1=ssb[:R], op=ALU.add)
            kb.rstd_from_ss(ssa[:R], 768)
            cq = k.sb("cq", [128, 768], BF16); cqT = k.sb("cqT", [128, 6, 128], BF16)
            k.op("dve", "scalar_tensor_tensor", out=cq[:R, 0:512], in0=pA[:R, :], scalar=ssa[:R, 0:1], in1=g_cq[:R, 0:512], op0=ALU.mult, op1=ALU.mult)
            k.op("dve", "scalar_tensor_tensor", out=cq[:R, 512:768], in0=pB[:R, 0:256], scalar=ssa[:R, 0:1], in1=g_cq[:R, 512:768], op0=ALU.mult, op1=ALU.mult)
            if CUT <= 2:
                continue
            kb.transpose_chunks(cqT, cq, 6, R)
            pQ1 = k.ps("pp", [128, 512], bufs=6); pQ2 = k.ps("pp", [128, 512], bufs=6)
            kb.proj(pQ1[:R, :], cqT, w_uq, 6, R, 0, 512)
            kb.proj(pQ2[:R, 0:256], cqT, w_uq, 6, R, 512, 768)
            qf = k.sb("qf", [128, 8, 96], F32)
            qff = qf[:].rearrange("p h d -> p (h d)")
            k.op("act", "activation", out=qff[:R, 0:512], in_=pQ1[:R, :], func=AF.Copy)
            k.op("act", "activation", out=qff[:R, 512:768], in_=pQ2[:R, 0:256], func=AF.Copy)
            if CUT <= 3:
                continue
            kb.head_norm(qf[:R], 8, 96, g_q, R)
            if CUT <= 4:
                continue
            qb = k.sb("qb", [128, 8, 96], BF16)
            k.op("act", "activation", out=qb[:R, :, 0:64], in_=qf[:R, :, 0:64], func=AF.Copy)
            csb = cs[:R, 0:16].unsqueeze(1).to_broadcast([R, 8, 16]); snb = cs[:R, 16:32].unsqueeze(1).to_broadcast([R, 8, 16])
            kb.rope(qb[:R, :, 64:80], qb[:R, :, 80:96], qf[:R, :, 64:80], qf[:R, :, 80:96], csb, snb, R, [8, 16])
            if CUT <= 5:
                continue
            qT = k.sb("qT", [128, 8, 128], BF16)
            kb.transpose_chunks(qT, qb[:].rearrange("p h d -> p (h d)"), 8, R, width=96)
            k.dma("sp", out=S["QT"][:, :, t0:t0 + R].rearrange("h d t -> d h t"), in_=qT[:96, :, :R])
            if CUT <= 6:
                continue
            lat = k.sb("lat", [128, 256], F32)
            kb.rmsnorm(lat[:R], pC[:R, 0:256], g_ckv[:R], 256, R, scr[:R, 0:256])
            k.dma("sp", out=O["lat"][t0:t0 + R, :], in_=lat[:R])
            krf = k.sb("krf", [128, 32], F32); kro = k.sb("kro", [128, 32], F32)
            k.op("act", "activation", out=krf[:R], in_=pC[:R, 256:288], func=AF.Copy)
            kb.rope(kro[:R, 0:16], kro[:R, 16:32], krf[:R, 0:16], krf[:R, 16:32], cs[:R, 0:16], cs[:R, 16:32], R, [16])
            k.dma("sp", out=O["kr"][t0:t0 + R, :], in_=kro[:R])
            if CUT <= 7:
                continue
            latb = k.sb("latb", [128, 256], BF16); latT = k.sb("latT", [128, 2, 128], BF16)
            k.op("pool", "tensor_copy", out=latb[:R], in_=lat[:R])
            kb.transpose_chunks(latT, latb, 2, R)
            pK1 = k.ps("pp", [128, 512], bufs=6); pK2 = k.ps("pp", [128, 512], bufs=6)
            kb.proj(pK1[:R, :], latT, w_ukv, 2, R, 0, 512)
            kb.proj(pK2[:R, :], latT, w_ukv, 2, R, 512, 1024)
            if CUT <= 8:
                continue
            kf = k.sb("kf", [128, 8, 96], F32)
            va = k.sb("va", [128, 8, 66], BF16)
            kvf = k.sb("kvf", [128, 8, 128], F32)
            k.op("act", "activation", out=kvf[:R, 0:4, :].rearrange("p h d -> p (h d)"), in_=pK1[:R, :], func=AF.Copy)
            k.op("act", "activation", out=kvf[:R, 4:8, :].rearrange("p h d -> p (h d)"), in_=pK2[:R, :], func=AF.Copy)
            k.op("dve", "tensor_copy", out=kf[:R, :, 0:64], in_=kvf[:R, :, 0:64])
            k.op("dve", "tensor_copy", out=va[:R, :, 0:64], in_=kvf[:R, :, 64:128])
            k.op("dve", "memset", ap=va[:R, :, 64:66], constant=1.0, _w=[va[:]])
            k.op("dve", "tensor_copy", out=kf[:R, :, 64:96], in_=kro[:R].unsqueeze(1).to_broadcast([R, 8, 32]))
            if CUT <= 9:
                continue
            kb.head_norm(kf[:R], 8, 96, g_kk, R)
            kbb = k.sb("kbb", [128, 768], BF16)
            k.op("act", "activation", out=kbb[:R], in_=kf[:R].rearrange("p h d -> p (h d)"), func=AF.Copy)
            if CUT <= 10:
                continue
            kT = k.sb("kT", [128, 8, 128], BF16)
            kb.transpose_chunks(kT, kbb, 8, R, width=96)
            k.dma("sp", out=S["KT"][:, :, t0:t0 + R].rearrange("h d t -> d h t"), in_=kT[:96, :, :R])
            k.dma("sp", out=S["V"][t0:t0 + R], in_=va[:R])
            if CUT <= 11:
                continue
            uf = k.sb("uf", [128, 512], F32); ub = k.sb("ub", [128, 512], BF16); uT = k.sb("uT", [128, 4, 128], BF16)
            k.op("act", "activation", out=uf[:R], in_=pD[:R, :], func=AF.Copy)
            k.op("dve", "tensor_copy", out=ub[:R], in_=pD[:R, :])
            k.dma("sp", out=S["U"][t0:t0 + R, :], in_=uf[:R])
            kb.transpose_chunks(uT, ub, 4, R)
            k.dma("sp", out=S["UT"][:, :, t0:t0 + R].rearrange("c p t -> p c t"), in_=uT[:, :, :R])
        k.end_phase()

        k.begin_phase()
        if "3" in stages:
            phase_attn_prompt(nc, k, kb, I, O, S)
        k.end_phase()
        k.begin_phase()
        if "P" in stages:
            phase_attn_sample(nc, k, kb, I, O, S)
        k.end_phase()
        k.begin_phase()
        if "2" in stages:
            phase_s5(nc, k, kb, I, O, S)
        k.end_phase()
        k.begin_phase()
        if "4" in stages:
            phase_outproj(nc, k, kb, I, O, S)
        k.end_phase()
        def to_scr(name):
            return lambda i, t0, R: S[name][t0:t0 + R, :]
        def to_out(i, t0, R):
            return O["yp"][t0:t0 + R, :] if i < NPT else O["ys"]
        plan = [("5", lambda: phase_mem(nc, k, kb, I, O, S, 0, S["R1"], to_scr("R2"))),
                ("6", lambda: phase_mlp(nc, k, kb, I, O, S, 0, S["R2"], to_scr("R3"))),
                ("7", lambda: phase_hgrn(nc, k, kb, I, O, S, S["R3"], to_scr("R4"))),
                ("8", lambda: phase_mem(nc, k, kb, I, O, S, 1, S["R4"], to_scr("R5"))),
                ("9", lambda: phase_mlp(nc, k, kb, I, O, S, 1, S["R5"], to_out))]
        for flag, fnc in plan:
            k.begin_phase()
            if flag in stages:
                fnc()
            k.end_phase()
    return nc


def _consts():
    ident = np.eye(128, dtype=np.float32)
    half = 16
    inv = (10000.0 ** (-np.arange(half, dtype=np.float32) / half)).astype(np.float32)
    pos = np.concatenate([np.arange(SEQ), np.tile(16384 + np.arange(4), NSB), np.zeros(64)]).astype(np.float32)
    ang = pos[:, None] * inv[None, :]
    rope = np.concatenate([np.cos(ang), np.sin(ang)], axis=1).astype(np.float32)
    return ident, rope


def make_in_maps(inp):
    ident, rope = _consts()
    f = lambda a: np.ascontiguousarray(np.asarray(a, dtype=np.float32))
    shared = {"ident": ident, "rope": rope}
    for nm in ["norm_mix", "norm_mem", "norm_memsrc", "norm_mlp", "w_mem_q", "w_mem_k", "w_mem_v", "w_mem_o",
               "mem_q_gain", "mem_k_gain", "w_mlp_up", "w_mlp_down"]:
        shared[nm] = f(inp[nm])
    shared["w_in_even"] = f(inp["w_in_even"][0]); shared["mla_cq_norm"] = f(inp["mla_cq_norm"][0])
    shared["mla_ckv_norm"] = f(inp["mla_ckv_norm"][0]); shared["w_mla_uq"] = f(inp["w_mla_uq"][0])
    shared["w_mla_ukv"] = f(inp["w_mla_ukv"][0])
    shared["q_gain"] = f(np.concatenate([inp["mla_qn_nope"][0], inp["mla_qn_rope"][0], inp["mla_qn_rope"][0]]))
    shared["k_gain"] = f(np.concatenate([inp["mla_kn_nope"][0], inp["mla_kn_rope"][0], inp["mla_kn_rope"][0]]))
    shared["s5_lre"] = f(inp["s5_lambda_re"][0]); shared["s5_lim"] = f(inp["s5_lambda_im"][0]); shared["s5_ls"] = f(inp["s5_log_step"][0])
    shared["s5_bre"] = f(inp["s5_b_re"][0]); shared["s5_bim"] = f(inp["s5_b_im"][0]); shared["s5_cre"] = f(inp["s5_c_re"][0]); shared["s5_cim"] = f(inp["s5_c_im"][0])
    shared["s5_d"] = f(inp["s5_d"][0]); shared["s5_wglu"] = f(inp["s5_w_glu"][0]); shared["s5_bglu"] = f(inp["s5_b_glu"][0])
    shared["tri"] = np.ascontiguousarray(np.triu(np.ones((128, 128), np.float32)))
    ms = np.zeros((64, NSB, 8, 4), np.float32)
    for bb in range(NSB):
        for tk in range(4):
            for tq in range(tk, 4):
                ms[bb * 4 + tk, bb, :, tq] = 1.0
    shared["maskS"] = np.ascontiguousarray(ms.reshape(64, NSB * 32))
    shared["w_out_even"] = f(inp["w_out_even"][0])
    if "cache_lat" in DECLARED:
        shared["cache_lat"] = np.ascontiguousarray(np.asarray(inp["cache_mla_latent"], np.float32).reshape(NPHYS, 128, 256))
        shared["cache_kr"] = np.ascontiguousarray(np.asarray(inp["cache_mla_krope"], np.float32).reshape(NPHYS, 128, 32))
    shared["hgrn_on"] = f(inp["hgrn_out_norm"][0]); shared["w_in_odd"] = f(inp["w_in_odd"][0]); shared["w_out_odd"] = f(inp["w_out_odd"][0])
    shared["hgrn_lb"] = f(inp["hgrn_lower_bounds"])
    t_ = np.arange(128)
    shared["rmask64"] = np.ascontiguousarray(np.tile((t_ % 64 != 0).astype(np.float32)[None, :], (128, 1)))
    shared["rmask4"] = np.ascontiguousarray(np.tile((np.arange(64) % 4 != 0).astype(np.float32)[None, :], (128, 1)))
    shared["bd64"] = np.ascontiguousarray(((t_[:, None] // 64 == t_[None, :] // 64) & (t_[:, None] <= t_[None, :])).astype(np.float32))
    t4 = np.arange(64)
    shared["bd4"] = np.ascontiguousarray(((t4[:, None] // 4 == t4[None, :] // 4) & (t4[:, None] <= t4[None, :])).astype(np.float32))
    shared["qmask"] = np.ascontiguousarray(np.tile((np.arange(2)[:, None] == (t_[None, :] // 64)).astype(np.float32).reshape(1, 256), (128, 1)))
    shared["bmask"] = np.ascontiguousarray(np.tile((np.arange(NSB)[:, None] == (t4[None, :] // 4)).astype(np.float32).reshape(1, 1024), (128, 1)))
    shared["rowmask"] = np.ascontiguousarray((t4[:, None] // 4 == np.arange(NSB)[None, :]).astype(np.float32))
    shared["ramp"] = np.ascontiguousarray(np.tile(np.arange(1, 129, dtype=np.float32)[None, :], (128, 1)))
    maps = []
    for c in range(NCORES):
        m = dict(shared)
        m["s5_h0re"] = f(inp["state_s5_re"][0, c * NSB:(c + 1) * NSB]).reshape(NSB, 2048)
        m["s5_h0im"] = f(inp["state_s5_im"][0, c * NSB:(c + 1) * NSB]).reshape(NSB, 2048)
        m["xp"] = f(inp["x_prompt"][c]); m["xs"] = f(inp["x_sample"][c * NSB:(c + 1) * NSB]).reshape(NST, D)
        m["mem"] = f(inp["mem_prompt"][c])
        m["cmk"] = f(inp["cache_mem_k"][:, c * NSB:(c + 1) * NSB]).reshape(2, NSB, 256, 512)
        m["cmv"] = f(inp["cache_mem_v"][:, c * NSB:(c + 1) * NSB]).reshape(2, NSB, 256, 512)
        m["hg0"] = f(inp["state_hgrn"][0, c * NSB:(c + 1) * NSB])
        if "ptab" in DECLARED:
            m["ptab"] = np.ascontiguousarray(np.asarray(inp["page_table"][c * NSB:(c + 1) * NSB], np.int32))
        maps.append(m)
    return maps


STAGES = "M123P456789"


def kernel(**inputs):
    nc = build(STAGES)
    maps = make_in_maps(inputs)
    res = run_bass_kernel_spmd(nc, maps, core_ids=list(range(NCORES)))
    R = res.results
    f32 = np.float32
    st = lambda nm: np.stack([np.asarray(R[c][nm], f32) for c in range(NCORES)])
    lat = st("o_lat"); kr = st("o_kr")
    y_p = st("o_yp")
    y_s = st("o_ys").reshape(128, 4, D)
    lat_p = lat[:, :SEQ].reshape(8, 1, SEQ, 256); kr_p = kr[:, :SEQ].reshape(8, 1, SEQ, 32)
    lat_s = lat[:, SEQ:SEQ + NST].reshape(128, 1, 4, 256); kr_s = kr[:, SEQ:SEQ + NST].reshape(128, 1, 4, 32)
    s5pr = st("o_s5p_re").reshape(1, 8, 32, 64); s5pi = st("o_s5p_im").reshape(1, 8, 32, 64)
    s5sr = st("o_s5s_re").reshape(1, 128, 32, 64); s5si = st("o_s5s_im").reshape(1, 128, 32, 64)
    hgp = st("o_hgp").reshape(1, 8, 8, 128, 128)
    hgs = st("o_hgs").reshape(1, 128, 8, 128, 128)
    mk = np.stack([np.asarray(R[c]["o_mk"], f32) for c in range(NCORES)], 1).reshape(2, 8, 256, 4, 128)
    mv = np.stack([np.asarray(R[c]["o_mv"], f32) for c in range(NCORES)], 1).reshape(2, 8, 256, 4, 128)
    return (y_p, y_s, lat_p, kr_p, s5pr, s5pi, hgp, mk, mv, lat_s, kr_s, s5sr, s5si, hgs)
```

```python
import concourse.bass as bass
import concourse.mybir as mybir

F32 = mybir.dt.float32
BF16 = mybir.dt.bfloat16
I32 = mybir.dt.int32
U32 = mybir.dt.uint32
ALU = mybir.AluOpType
AF = mybir.ActivationFunctionType
AX = mybir.AxisListType

_AP_T = None


class _Buf:
    __slots__ = ("name", "w", "r", "dsem", "dcnt", "space")

    def __init__(self, name, space):
        self.name = name
        self.w = None
        self.r = []
        self.dsem = None
        self.space = space


class Sched:
    EPOCH = 30000

    def __init__(self, nc, stack):
        self.nc = nc
        self.stack = stack
        self.engs = {"pe": nc.tensor, "act": nc.scalar, "dve": nc.vector,
                     "pool": nc.gpsimd, "sp": nc.sync}
        self.streams = {e: [] for e in self.engs}
        self.sems = {}
        self.semcnt = {}
        self.cur = {}
        self.waited = {e: {} for e in self.engs}
        self.bufs = {}
        self.nsem = 0
        self.uid = 0
        self.free_dma = []
        self.pstack = None
        for e in self.engs:
            self._new_eng_sem(e)

    def _alloc_sem(self, key):
        h = self.stack.enter_context(self.nc.semaphore("s_%s_%d" % (key if isinstance(key, str) else "x", self.nsem)))
        self.nsem += 1
        self.sems[key] = h
        self.semcnt[key] = 0
        return key

    def _new_eng_sem(self, e):
        key = ("eng", e, self.nsem)
        self._alloc_sem(key)
        self.cur[e] = key

    def _buf(self, ap):
        t = ap.tensor
        name = t.name
        b = self.bufs.get(name)
        if b is None:
            tn = type(t).__name__
            space = "dram" if "DRam" in tn else ("psum" if "PSum" in tn else "sbuf")
            b = _Buf(name, space)
            self.bufs[name] = b
        return b

    def _ring(self, kind, name, shape, dtype, bufs):
        key = (kind, name)
        r = self.rings.get(key)
        if r is None:
            r = {"tiles": [], "i": 0, "shape": list(shape), "dtype": dtype}
            self.rings[key] = r
        assert r["shape"] == list(shape) and r["dtype"] == dtype, (name, shape, r["shape"])
        if len(r["tiles"]) < bufs:
            self.uid += 1
            mk = self.nc.sbuf_tensor if kind == "sb" else self.nc.psum_tensor
            t = self.pstack.enter_context(mk("%s_%d" % (name, self.uid), list(shape), dtype))
            r["tiles"].append(t)
            return t
        t = r["tiles"][r["i"] % len(r["tiles"])]
        r["i"] += 1
        return t

    def sb(self, name, shape, dtype, bufs=2):
        return self._ring("sb", name, shape, dtype, bufs)

    def gsb(self, name, shape, dtype):
        self.uid += 1
        return self.stack.enter_context(self.nc.sbuf_tensor("%s_%d" % (name, self.uid), list(shape), dtype))

    def ps(self, name, shape, dtype=F32, bufs=2):
        return self._ring("ps", name, shape, dtype, bufs)

    def begin_phase(self):
        from contextlib import ExitStack
        self.pstack = ExitStack()
        self.pstack.__enter__()
        self.rings = {}

    def end_phase(self):
        allv = [(kk, vv) for kk, vv in self.semcnt.items() if vv > 0]
        self.phase_final = allv
        self.emit()
        for e in self.engs:
            self.streams[e] = []
            for kk, vv in allv:
                self.waited[e][kk] = vv
        for b in self.bufs.values():
            if b.dsem is not None:
                self.free_dma.append(b.dsem)
        self.bufs = {}
        self.pstack.__exit__(None, None, None)
        self.pstack = None

    def dram(self, name, shape, dtype, kind="Internal"):
        return self.nc.dram_tensor(name, list(shape), dtype, kind=kind)

    def _deps(self, eng, reads, writes):
        deps = {}
        def add(tok):
            if tok is None:
                return
            k, v = tok
            if k[0] == "dma":
                v = self.semcnt[k]
            if deps.get(k, 0) < v:
                deps[k] = v
        for b in reads:
            add(b.w)
        for b in writes:
            add(b.w)
            for t in b.r:
                add(t)
        out = []
        wd = self.waited[eng]
        for k, v in deps.items():
            if eng == "pe" and k == self.cur["pe"]:
                continue
            if wd.get(k, 0) >= v:
                continue
            wd[k] = v
            out.append((k, v))
        return out

    def _classify(self, kw, extra_r, extra_w):
        global _AP_T
        reads, writes = [], []
        for name, v in kw.items():
            if name in ("identity",):
                continue
            if hasattr(v, "tensor") and hasattr(v, "partition_size"):
                b = self._buf(v)
                if name in ("out", "accum_out"):
                    writes.append(b)
                else:
                    reads.append(b)
        for v in extra_r:
            reads.append(self._buf(v))
        for v in extra_w:
            writes.append(self._buf(v))
        return reads, writes

    def op(self, eng, method, _r=(), _w=(), _rw=(), **kw):
        reads, writes = self._classify(kw, list(_r) + list(_rw), list(_w) + list(_rw))
        waits = self._deps(eng, reads, writes)
        if self.semcnt[self.cur[eng]] >= self.EPOCH:
            self._new_eng_sem(eng)
        key = self.cur[eng]
        self.semcnt[key] += 1
        tok = (key, self.semcnt[key])
        self.streams[eng].append((waits, method, kw, key, 1))
        for b in reads:
            b.r.append(tok)
        for b in writes:
            b.w = tok
            b.r = []
        return tok

    def dma(self, queue, out, in_, **kw):
        bo, bi = self._buf(out), self._buf(in_)
        sbside = bo if bo.space != "dram" else bi
        assert sbside.space == "sbuf", "dma needs an sbuf side: %s %s" % (bo.name, bi.name)
        if sbside.dsem is None:
            if self.free_dma:
                sbside.dsem = self.free_dma.pop()
            else:
                sbside.dsem = self._alloc_sem(("dma", sbside.name))
        key = sbside.dsem
        waits = self._deps(queue, [bi] if bi.space != "dram" else [], [bo] if bo.space != "dram" else [])
        self.semcnt[key] += 16
        tok = (key, self.semcnt[key])
        meth = kw.pop("_method", "dma_start")
        for xr in kw.pop("_r", ()):
            xb = self._buf(xr)
            if xb.space != "dram":
                for kk, vv in self._deps(queue, [xb], []):
                    waits.append((kk, vv))
                xb.r.append((key, self.semcnt[key]))
        d = dict(out=out, in_=in_)
        d.update(kw)
        self.streams[queue].append((waits, meth, d, key, 16))
        if bi.space != "dram":
            bi.r.append(tok)
        if bo.space != "dram":
            bo.w = tok
            bo.r = []
        return tok

    def finish(self, final_bufs_aps=()):
        deps = {}
        for b in self.bufs.values():
            if b.space == "dram" and b.w is not None:
                k, v = b.w
                if k[0] == "dma":
                    v = self.semcnt[k]
                deps[k] = max(deps.get(k, 0), v)
        self.final_waits = list(deps.items())

    def emit(self):
        nc = self.nc
        blk = nc.Block()
        blk.__enter__()
        names = {"pe": "tensor", "act": "scalar", "dve": "vector", "pool": "gpsimd", "sp": "sync"}
        for e, attr in names.items():
            stream = self.streams[e]
            final = list(self.phase_final)

            def body(engine, stream=stream, final=final):
                for waits, meth, kw, key, inc in stream:
                    for k, v in waits:
                        engine.wait_ge(self.sems[k], v)
                    if isinstance(meth, tuple):
                        ins = meth[1](engine, kw["out"], kw["in_"])
                    else:
                        ins = getattr(engine, meth)(**kw)
                    ins.then_inc(self.sems[key], inc)
                for k, v in final:
                    engine.wait_ge(self.sems[k], v)
            getattr(blk, attr)(body)
        blk.__exit__(None, None, None)

import numpy as np
from contextlib import ExitStack
from concourse.bass_utils import run_bass_kernel_spmd

NCORES = 8
D = 1024
SEQ = 4096
NPT = 32
NSB = 16
NST = 64
NTOK = SEQ + NST
EPS = 1e-6
NPHYS = 20480
NPAGES = 128


def tiles():
    return [(i, i * 128, 128) for i in range(NPT)] + [(NPT, SEQ, NST)]


class KB:
    def __init__(self, nc, k):
        self.nc = nc
        self.k = k

    def bcast_load(self, name, vec_ap, n, glob=False):
        t = (self.k.gsb if glob else self.k.sb)(name, [128, n], F32)
        self.k.dma("sp", out=t[:], in_=vec_ap.partition_broadcast(128))
        return t

    def load_w(self, name, w_ap, K, N, stage_cols=2048):
        k = self.k
        kc = K // 128
        wb = k.sb(name, [128, kc, N], BF16)
        src = w_ap.rearrange("(c p) n -> p c n", p=128)
        step = max(1, stage_cols // N) if N <= stage_cols else 1
        idx = 0
        for c0 in range(0, kc, step):
            c1 = min(kc, c0 + step)
            for n0 in range(0, N, stage_cols):
                n1 = min(N, n0 + stage_cols)
                st = self.stage[idx % 2]
                idx += 1
                sv = st[:, 0:(c1 - c0) * (n1 - n0)].rearrange("p (c n) -> p c n", c=c1 - c0)
                k.dma("sp", out=sv, in_=src[:, c0:c1, n0:n1])
                eng = "pool" if idx % 2 else "dve"
                k.op(eng, "tensor_copy", out=wb[:, c0:c1, n0:n1], in_=sv)
        return wb

    def mk_stage(self, cols=2048):
        self.stage = [self.k.sb("wstage", [128, cols], F32) for _ in range(2)]

    def rstd_from_ss(self, ss_ap, n_feat):
        k = self.k
        k.op("dve", "tensor_scalar", out=ss_ap, in0=ss_ap, scalar1=1.0 / n_feat, scalar2=EPS,
             op0=ALU.mult, op1=ALU.add)
        k.op("act", "activation", out=ss_ap, in_=ss_ap, func=AF.Ln)
        k.op("act", "activation", out=ss_ap, in_=ss_ap, func=AF.Exp, scale=-0.5)

    def rmsnorm(self, out_ap, x_ap, g_ap, n_feat, R, scr_ap):
        k = self.k
        ss = k.sb("ss", [128, 1], F32)
        k.op("act", "activation", out=scr_ap, in_=x_ap, func=AF.Square, accum_out=ss[:R])
        self.rstd_from_ss(ss[:R], n_feat)
        k.op("dve", "scalar_tensor_tensor", out=out_ap, in0=x_ap, scalar=ss[:R, 0:1], in1=g_ap,
             op0=ALU.mult, op1=ALU.mult)

    def transpose_chunks(self, dst, src, nch, R, width=128, dt=BF16):
        k = self.k
        per = 8 if dt == BF16 else 4
        for c0 in range(0, nch, per):
            c1 = min(nch, c0 + per)
            pt = k.ps("pt" + ("b" if dt == BF16 else "f"), [128, per, 128], dt, bufs=2)
            for c in range(c0, c1):
                k.op("pe", "transpose", out=pt[:width, c - c0, :R], in_=src[:R, c * width:(c + 1) * width],
                     identity=(self.idb if dt == BF16 else self.idf)[:R, :R])
            k.op("act", "activation", out=dst[:width, c0:c1, :R], in_=pt[:width, 0:c1 - c0, :R], func=AF.Copy)

    def proj(self, ps_ap, xT, w, kc, R, n0, n1, kw=128):
        for c in range(kc):
            self.k.op("pe", "matmul", out=ps_ap, lhsT=xT[:kw, c, :R], rhs=w[:kw, c, n0:n1],
                      start=(c == 0), stop=(c == kc - 1))

    def head_norm(self, x3, nh, hd, gain_t, R):
        k = self.k
        sq = k.sb("hn_sq%d_%d" % (nh, hd), [128, nh, hd], F32)
        s8 = k.sb("hn_s8%d" % nh, [128, nh], F32)
        k.op("act", "activation", out=sq[:R], in_=x3, func=AF.Square)
        k.op("dve", "tensor_reduce", out=s8[:R], in_=sq[:R], axis=AX.X, op=ALU.add)
        self.rstd_from_ss(s8[:R], hd)
        k.op("dve", "tensor_tensor", out=x3, in0=x3, in1=s8[:R].unsqueeze(2).to_broadcast([R, nh, hd]), op=ALU.mult)
        k.op("pool", "tensor_tensor", out=x3, in0=x3, in1=gain_t[:R].unsqueeze(1).to_broadcast([R, nh, hd]), op=ALU.mult)

    def rope(self, out1, out2, x1, x2, cs, sn, R, shape):
        k = self.k
        sfx = "_".join(str(x) for x in shape)
        t1 = k.sb("rp1" + sfx, [128] + shape, F32)
        t2 = k.sb("rp2" + sfx, [128] + shape, F32)
        t3 = k.sb("rp3" + sfx, [128] + shape, F32)
        t4 = k.sb("rp4" + sfx, [128] + shape, F32)
        k.op("dve", "tensor_tensor", out=t1[:R], in0=x1, in1=cs, op=ALU.mult)
        k.op("pool", "tensor_tensor", out=t2[:R], in0=x2, in1=sn, op=ALU.mult)
        k.op("dve", "tensor_tensor", out=t3[:R], in0=x2, in1=cs, op=ALU.mult)
        k.op("pool", "tensor_tensor", out=t4[:R], in0=x1, in1=sn, op=ALU.mult)
        k.op("dve", "tensor_tensor", out=out1, in0=t1[:R], in1=t2[:R], op=ALU.subtract)
        k.op("dve", "tensor_tensor", out=out2, in0=t3[:R], in1=t4[:R], op=ALU.add)


CUT = 99
DBG = {}
NOP = {"v": False}
DECLARED = set()


TWO_PI = 6.283185307179586


def phase_s5(nc, k, kb, I, O, S):
    kb.mk_stage()
    NJ = 16
    def par(name, src):
        t = k.sb(name, [128, NJ], F32, bufs=1)
        for g2 in range(2):
            k.dma("sp", out=t[g2 * 64:(g2 + 1) * 64, :], in_=src.rearrange("(j g) n -> g n j", g=2)[g2], allow_slow_non_contiguous=True)
        return t
    lre = par("lre", I["s5_lre"]); lim = par("lim", I["s5_lim"])
    dt = k.sb("dt", [128, NJ], F32, bufs=1)
    ls2 = I["s5_ls"].rearrange("(j g) -> g j", g=2)
    for g2 in range(2):
        k.dma("sp", out=dt[g2 * 64:(g2 + 1) * 64, :], in_=ls2[g2].partition_broadcast(64), allow_slow_non_contiguous=True)
    k.op("act", "activation", out=dt[:], in_=dt[:], func=AF.Exp)
    k.op("dve", "tensor_scalar", out=lre[:], in0=lre[:], scalar1=-1e-4, scalar2=None, op0=ALU.min)
    mag = k.sb("mag", [128, NJ], F32, bufs=1); th = k.sb("th", [128, NJ], F32, bufs=1)
    k.op("dve", "tensor_tensor", out=mag[:], in0=lre[:], in1=dt[:], op=ALU.mult)
    k.op("act", "activation", out=mag[:], in_=mag[:], func=AF.Exp)
    k.op("dve", "tensor_tensor", out=th[:], in0=lim[:], in1=dt[:], op=ALU.mult)
    def sincos(dst, ang_ap, shape, shift):
        sfx = "%d" % shape[1]
        tmp = k.sb("sc_tmp" + sfx, shape, F32); ki = k.sb("sc_ki" + sfx, shape, I32); kf = k.sb("sc_kf" + sfx, shape, F32)
        m = k.sb("sc_m" + sfx, shape, F32)
        k.op("dve", "tensor_scalar", out=tmp[:], in0=ang_ap, scalar1=1.0 / TWO_PI, scalar2=shift / TWO_PI, op0=ALU.mult, op1=ALU.add)
        k.op("dve", "tensor_copy", out=ki[:], in_=tmp[:])
        k.op("dve", "tensor_copy", out=kf[:], in_=ki[:])
        k.op("dve", "tensor_tensor", out=tmp[:], in0=tmp[:], in1=kf[:], op=ALU.subtract)
        k.op("dve", "tensor_scalar", out=m[:], in0=tmp[:], scalar1=0.5, scalar2=-1.0, op0=ALU.is_gt, op1=ALU.mult)
        k.op("dve", "tensor_tensor", out=tmp[:], in0=tmp[:], in1=m[:], op=ALU.add)
        k.op("dve", "tensor_scalar", out=m[:], in0=tmp[:], scalar1=-0.5, scalar2=1.0, op0=ALU.is_lt, op1=ALU.mult)
        k.op("dve", "tensor_tensor", out=tmp[:], in0=tmp[:], in1=m[:], op=ALU.add)
        k.op("act", "activation", out=dst, in_=tmp[:], func=AF.Sin, scale=TWO_PI)
    abre = k.sb("abre", [128, NJ], F32, bufs=1); abim = k.sb("abim", [128, NJ], F32, bufs=1)
    sincos(abre[:], th[:], [128, NJ], np.pi / 2); sincos(abim[:], th[:], [128, NJ], 0.0)
    k.op("dve", "tensor_tensor", out=abre[:], in0=abre[:], in1=mag[:], op=ALU.mult)
    k.op("dve", "tensor_tensor", out=abim[:], in0=abim[:], in1=mag[:], op=ALU.mult)
    den = k.sb("den", [128, NJ], F32, bufs=1); t1 = k.sb("t1", [128, NJ], F32, bufs=1); t2 = k.sb("t2", [128, NJ], F32, bufs=1)
    core_ = k.sb("core", [128, NJ], F32, bufs=1); coim = k.sb("coim", [128, NJ], F32, bufs=1); am1 = k.sb("am1", [128, NJ], F32, bufs=1)
    k.op("dve", "tensor_tensor", out=den[:], in0=lre[:], in1=lre[:], op=ALU.mult)
    k.op("dve", "tensor_tensor", out=t1[:], in0=lim[:], in1=lim[:], op=ALU.mult)
    k.op("dve", "tensor_tensor", out=den[:], in0=den[:], in1=t1[:], op=ALU.add)
    k.op("dve", "reciprocal", out=den[:], in_=den[:])
    k.op("dve", "tensor_scalar", out=am1[:], in0=abre[:], scalar1=-1.0, scalar2=None, op0=ALU.add)
    k.op("dve", "tensor_tensor", out=t1[:], in0=am1[:], in1=lre[:], op=ALU.mult)
    k.op("dve", "tensor_tensor", out=t2[:], in0=abim[:], in1=lim[:], op=ALU.mult)
    k.op("dve", "tensor_tensor", out=t1[:], in0=t1[:], in1=t2[:], op=ALU.add)
    k.op("dve", "tensor_tensor", out=core_[:], in0=t1[:], in1=den[:], op=ALU.mult)
    k.op("dve", "tensor_tensor", out=t1[:], in0=abim[:], in1=lre[:], op=ALU.mult)
    k.op("dve", "tensor_tensor", out=t2[:], in0=am1[:], in1=lim[:], op=ALU.mult)
    k.op("dve", "tensor_tensor", out=t1[:], in0=t1[:], in1=t2[:], op=ALU.subtract)
    k.op("dve", "tensor_tensor", out=coim[:], in0=t1[:], in1=den[:], op=ALU.mult)
    bre = k.sb("bre", [128, NJ, 16], F32, bufs=1); bim = k.sb("bim", [128, NJ, 16], F32, bufs=1)
    for g2 in range(2):
        k.dma("sp", out=bre[g2 * 64:(g2 + 1) * 64], in_=I["s5_bre"].rearrange("(j g) n c -> g n j c", g=2)[g2])
        k.dma("sp", out=bim[g2 * 64:(g2 + 1) * 64], in_=I["s5_bim"].rearrange("(j g) n c -> g n j c", g=2)[g2])
    bbre = k.sb("bbre", [128, NJ, 16], F32, bufs=1); bbim = k.sb("bbim", [128, NJ, 16], F32, bufs=1); tb = k.sb("tb", [128, NJ, 16], F32, bufs=1)
    cr3 = core_[:].unsqueeze(2).to_broadcast([128, NJ, 16]); ci3 = coim[:].unsqueeze(2).to_broadcast([128, NJ, 16])
    k.op("dve", "tensor_tensor", out=bbre[:], in0=bre[:], in1=cr3, op=ALU.mult)
    k.op("dve", "tensor_tensor", out=tb[:], in0=bim[:], in1=ci3, op=ALU.mult)
    k.op("dve", "tensor_tensor", out=bbre[:], in0=bbre[:], in1=tb[:], op=ALU.subtract)
    k.op("dve", "tensor_tensor", out=bbim[:], in0=bim[:], in1=cr3, op=ALU.mult)
    k.op("dve", "tensor_tensor", out=tb[:], in0=bre[:], in1=ci3, op=ALU.mult)
    k.op("dve", "tensor_tensor", out=bbim[:], in0=bbim[:], in1=tb[:], op=ALU.add)
    BBT = [k.sb("BBTre", [128, NJ, 128], BF16, bufs=1), k.sb("BBTim", [128, NJ, 128], BF16, bufs=1)]
    for ri, bbx in enumerate((bbre, bbim)):
        for j in range(NJ):
            blk = k.sb("bblk", [128, 128], F32)
            k.op("dve", "memset", ap=blk[:], constant=0.0, _w=[blk[:]])
            for g2 in range(2):
                c0 = (j % 4) * 32 + g2 * 16
                k.op("dve", "tensor_copy", out=blk[g2 * 64:(g2 + 1) * 64, c0:c0 + 16], in_=bbx[g2 * 64:(g2 + 1) * 64, j, :])
            pt = k.ps("px4", [128, 4, 2, 128], F32, bufs=2)[:, :, 0, :]
            k.op("pe", "transpose", out=pt[:, 0, :], in_=blk[:], identity=kb.idf[:])
            k.op("act", "activation", out=BBT[ri][:, j, :], in_=pt[:, 0, :], func=AF.Copy)
    CM = []
    for ri, nm in enumerate(("s5_cre", "s5_cim")):
        cf = k.sb("cmf%d" % ri, [128, NJ, 32], F32, bufs=1)
        k.op("dve", "memset", ap=cf[:], constant=0.0, _w=[cf[:]])
        for j in range(NJ):
            for g2 in range(2):
                k.dma("sp", out=cf[g2 * 64:(g2 + 1) * 64, j, g2 * 16:(g2 + 1) * 16], in_=I[nm][2 * j + g2].rearrange("c n -> n c"), allow_slow_non_contiguous=True)
        cb = k.sb("cmb%d" % ri, [128, NJ, 32], BF16, bufs=1)
        k.op("dve", "tensor_scalar", out=cb[:], in0=cf[:], scalar1=(1.0 if ri == 0 else -1.0), scalar2=None, op0=ALU.mult)
        CM.append(cb)
    ramp = k.sb("ramp", [128, 128], F32, bufs=1)
    k.dma("sp", out=ramp[:], in_=I["ramp"])
    TC = k.sb("TC", [128, NJ, 128], F32, bufs=1); TS = k.sb("TS", [128, NJ, 128], F32, bufs=1)
    for j in range(NJ):
        ph = k.sb("ph", [128, 128], F32)
        k.op("dve", "tensor_scalar", out=ph[:], in0=ramp[:], scalar1=th[:, j:j + 1], scalar2=None, op0=ALU.mult)
        sincos(TC[:, j, :], ph[:], [128, 128], np.pi / 2); sincos(TS[:, j, :], ph[:], [128, 128], 0.0)
    dvec = kb.bcast_load("s5d", I["s5_d"], 512); bglu = kb.bcast_load("s5bg", I["s5_bglu"], 512)
    wglu = kb.load_w("wglu", I["s5_wglu"], 512, 512)
    car = [k.sb("car_re", [128, NJ], F32, bufs=1), k.sb("car_im", [128, NJ], F32, bufs=1)]
    k.op("dve", "memset", ap=car[0][:], constant=0.0, _w=[car[0][:]])
    k.op("dve", "memset", ap=car[1][:], constant=0.0, _w=[car[1][:]])
    H0 = [k.sb("h0re", [128, NJ, NSB], F32, bufs=1), k.sb("h0im", [128, NJ, NSB], F32, bufs=1)]
    for ri, nm in enumerate(("s5_h0re", "s5_h0im")):
        hn_ = k.sb("h0nat", [NSB, 2048], F32)
        k.dma("sp", out=hn_[:], in_=I[nm])
        for j0 in range(0, NJ, 4):
            pt = k.ps("px4", [128, 4, 2, 128], F32, bufs=2)[:, :, 0, :]
            for j in range(j0, j0 + 4):
                k.op("pe", "transpose", out=pt[:, j - j0, :NSB], in_=hn_[:NSB, j * 128:(j + 1) * 128], identity=kb.idf[:NSB, :NSB])
            k.op("act", "activation", out=H0[ri][:, j0:j0 + 4, :], in_=pt[:, :, :NSB], func=AF.Copy)
    HS = [k.sb("hsre", [128, NJ, NSB], F32, bufs=1), k.sb("hsim", [128, NJ, NSB], F32, bufs=1)]

    for (i, t0, R) in tiles():
        T = R
        uT = k.sb("s5uT", [128, 4, 128], BF16)
        k.dma("sp", out=uT[:, :, :T], in_=S["UT"][:, :, t0:t0 + T].rearrange("c p t -> p c t"))
        uf = k.sb("s5uf", [128, 512], F32)
        k.dma("sp", out=uf[:T], in_=S["U"][t0:t0 + T, :])
        py = k.ps("py", [128, 512], F32, bufs=2)
        for j0 in (range(0, NJ, 4) if i < NPT else []):
            px4 = k.ps("px4", [128, 4, 2, 128], F32, bufs=2)
            for jj in range(4):
                j = j0 + jj
                k.op("pe", "matmul", out=px4[:, jj, 0, :T], lhsT=BBT[0][:, j, :], rhs=uT[:, j // 4, :T], start=True, stop=True)
                k.op("pe", "matmul", out=px4[:, jj, 1, :T], lhsT=BBT[1][:, j, :], rhs=uT[:, j // 4, :T], start=True, stop=True)
            xs4 = k.sb("s5x4", [128, 4, 2, 128], F32)
            k.op("act", "activation", out=xs4[:], in_=px4[:], func=AF.Copy)
            c4 = TC[:, j0:j0 + 4, :]; s4 = TS[:, j0:j0 + 4, :]
            xre = xs4[:, :, 0, :]; xim = xs4[:, :, 1, :]
            b1 = k.sb("s5b1", [128, 4, 128], F32); b2 = k.sb("s5b2", [128, 4, 128], F32); b3 = k.sb("s5b3", [128, 4, 128], F32); b4 = k.sb("s5b4", [128, 4, 128], F32)
            k.op("dve", "tensor_tensor", out=b1[:], in0=xre, in1=c4, op=ALU.mult)
            k.op("pool", "tensor_tensor", out=b2[:], in0=xim, in1=s4, op=ALU.mult)
            k.op("dve", "tensor_tensor", out=b3[:], in0=xim, in1=c4, op=ALU.mult)
            k.op("pool", "tensor_tensor", out=b4[:], in0=xre, in1=s4, op=ALU.mult)
            k.op("dve", "tensor_tensor", out=b1[:], in0=b1[:], in1=b2[:], op=ALU.add)
            k.op("pool", "tensor_tensor", out=b3[:], in0=b3[:], in1=b4[:], op=ALU.subtract)
            g4 = k.sb("s5g4", [128, 4, 2, 128], F32)
            for jj in range(4):
                j = j0 + jj
                rb = mag[:, j:j + 1].to_broadcast([128, T])
                k.op("dve", "tensor_tensor_scan", out=g4[:, jj, 0, :], data0=rb, data1=b1[:, jj, :], initial=car[0][:, j:j + 1], op0=ALU.mult, op1=ALU.add)
                k.op("dve", "tensor_tensor_scan", out=g4[:, jj, 1, :], data0=rb, data1=b3[:, jj, :], initial=car[1][:, j:j + 1], op0=ALU.mult, op1=ALU.add)
            gre = g4[:, :, 0, :]; gim = g4[:, :, 1, :]
            h4 = k.sb("s5h4", [128, 4, 2, 128], F32)
            k.op("dve", "tensor_tensor", out=b1[:], in0=gre, in1=c4, op=ALU.mult)
            k.op("pool", "tensor_tensor", out=b2[:], in0=gim, in1=s4, op=ALU.mult)
            k.op("dve", "tensor_tensor", out=b3[:], in0=gre, in1=s4, op=ALU.mult)
            k.op("pool", "tensor_tensor", out=b4[:], in0=gim, in1=c4, op=ALU.mult)
            k.op("dve", "tensor_tensor", out=h4[:, :, 0, :], in0=b1[:], in1=b2[:], op=ALU.subtract)
            k.op("pool", "tensor_tensor", out=h4[:, :, 1, :], in0=b3[:], in1=b4[:], op=ALU.add)
            k.op("act", "activation", out=car[0][:, j0:j0 + 4], in_=h4[:, :, 0, T - 1], func=AF.Copy)
            k.op("act", "activation", out=car[1][:, j0:j0 + 4], in_=h4[:, :, 1, T - 1], func=AF.Copy)
            hb4 = k.sb("s5hb4", [128, 4, 2, 128], BF16)
            k.op("act", "activation", out=hb4[:], in_=h4[:], func=AF.Copy)
            for jj in range(4):
                j = j0 + jj
                k.op("pe", "matmul", out=py[:T, j * 32:(j + 1) * 32], lhsT=hb4[:, jj, 0, :T], rhs=CM[0][:, j, :], start=True, stop=False)
                k.op("pe", "matmul", out=py[:T, j * 32:(j + 1) * 32], lhsT=hb4[:, jj, 1, :T], rhs=CM[1][:, j, :], start=False, stop=True)
        for j in (range(NJ) if i >= NPT else []):
            px = k.ps("px4", [128, 4, 2, 128], F32, bufs=2)[:, 0, :, :]
            k.op("pe", "matmul", out=px[:, 0, :T], lhsT=BBT[0][:, j, :], rhs=uT[:, j // 4, :T], start=True, stop=True)
            k.op("pe", "matmul", out=px[:, 1, :T], lhsT=BBT[1][:, j, :], rhs=uT[:, j // 4, :T], start=True, stop=True)
            xs_ = k.sb("s5x", [128, 2, 128], F32)
            k.op("act", "activation", out=xs_[:, :, :T], in_=px[:, :, :T], func=AF.Copy)
            if i < NPT:
                c_ = TC[:, j, :T]; s_ = TS[:, j, :T]
                shp = [128, T]
                v = lambda a: a
            else:
                c_ = TC[:, j, 0:4].unsqueeze(1).to_broadcast([128, NSB, 4]); s_ = TS[:, j, 0:4].unsqueeze(1).to_broadcast([128, NSB, 4])
                v = lambda a: a.rearrange("p (b t) -> p b t", t=4)
            xre = v(xs_[:, 0, :T]); xim = v(xs_[:, 1, :T])
            a1 = k.sb("s5a1", [128, 128], F32); a2 = k.sb("s5a2", [128, 128], F32); a3 = k.sb("s5a3", [128, 128], F32); a4 = k.sb("s5a4", [128, 128], F32)
            k.op("dve", "tensor_tensor", out=v(a1[:, :T]), in0=xre, in1=c_, op=ALU.mult)
            k.op("pool", "tensor_tensor", out=v(a2[:, :T]), in0=xim, in1=s_, op=ALU.mult)
            k.op("dve", "tensor_tensor", out=v(a3[:, :T]), in0=xim, in1=c_, op=ALU.mult)
            k.op("pool", "tensor_tensor", out=v(a4[:, :T]), in0=xre, in1=s_, op=ALU.mult)
            k.op("dve", "tensor_tensor", out=a1[:, :T], in0=a1[:, :T], in1=a2[:, :T], op=ALU.add)
            k.op("pool", "tensor_tensor", out=a3[:, :T], in0=a3[:, :T], in1=a4[:, :T], op=ALU.subtract)
            gre = k.sb("s5gre", [128, 128], F32); gim = k.sb("s5gim", [128, 128], F32)
            if i < NPT:
                rb = mag[:, j:j + 1].to_broadcast([128, T])
                k.op("dve", "tensor_tensor_scan", out=gre[:, :T], data0=rb, data1=a1[:, :T], initial=car[0][:, j:j + 1], op0=ALU.mult, op1=ALU.add)
                k.op("dve", "tensor_tensor_scan", out=gim[:, :T], data0=rb, data1=a3[:, :T], initial=car[1][:, j:j + 1], op0=ALU.mult, op1=ALU.add)
            else:
                rb = mag[:, j:j + 1].to_broadcast([128, 4])
                for bb_ in range(NSB):
                    sl = slice(bb_ * 4, bb_ * 4 + 4)
                    k.op("dve", "tensor_tensor_scan", out=gre[:, sl], data0=rb, data1=a1[:, sl], initial=H0[0][:, j, bb_:bb_ + 1], op0=ALU.mult, op1=ALU.add)
                    k.op("dve", "tensor_tensor_scan", out=gim[:, sl], data0=rb, data1=a3[:, sl], initial=H0[1][:, j, bb_:bb_ + 1], op0=ALU.mult, op1=ALU.add)
            hre = k.sb("s5hre", [128, 128], F32); him = k.sb("s5him", [128, 128], F32)
            k.op("dve", "tensor_tensor", out=v(a1[:, :T]), in0=v(gre[:, :T]), in1=c_, op=ALU.mult)
            k.op("pool", "tensor_tensor", out=v(a2[:, :T]), in0=v(gim[:, :T]), in1=s_, op=ALU.mult)
            k.op("dve", "tensor_tensor", out=v(a3[:, :T]), in0=v(gre[:, :T]), in1=s_, op=ALU.mult)
            k.op("pool", "tensor_tensor", out=v(a4[:, :T]), in0=v(gim[:, :T]), in1=c_, op=ALU.mult)
            k.op("dve", "tensor_tensor", out=hre[:, :T], in0=a1[:, :T], in1=a2[:, :T], op=ALU.subtract)
            k.op("pool", "tensor_tensor", out=him[:, :T], in0=a3[:, :T], in1=a4[:, :T], op=ALU.add)
            if i < NPT:
                k.op("act", "activation", out=car[0][:, j:j + 1], in_=hre[:, T - 1:T], func=AF.Copy)
                k.op("act", "activation", out=car[1][:, j:j + 1], in_=him[:, T - 1:T], func=AF.Copy)
            else:
                k.op("act", "activation", out=HS[0][:, j, :], in_=hre[:, 3:T:4], func=AF.Copy)
                k.op("act", "activation", out=HS[1][:, j, :], in_=him[:, 3:T:4], func=AF.Copy)
            hb = k.sb("s5hb", [128, 2, 128], BF16)
            k.op("act", "activation", out=hb[:, 0, :T], in_=hre[:, :T], func=AF.Copy)
            k.op("act", "activation", out=hb[:, 1, :T], in_=him[:, :T], func=AF.Copy)
            k.op("pe", "matmul", out=py[:T, j * 32:(j + 1) * 32], lhsT=hb[:, 0, :T], rhs=CM[0][:, j, :], start=True, stop=False)
            k.op("pe", "matmul", out=py[:T, j * 32:(j + 1) * 32], lhsT=hb[:, 1, :T], rhs=CM[1][:, j, :], start=False, stop=True)
        y = k.sb("s5y", [128, 512], F32); y2 = k.sb("s5y2", [128, 512], F32); z = k.sb("s5z", [128, 512], F32)
        k.op("dve", "tensor_tensor", out=y[:T], in0=uf[:T], in1=dvec[:T], op=ALU.mult)
        k.op("dve", "tensor_tensor", out=y[:T], in0=y[:T], in1=py[:T, :], op=ALU.add)
        k.op("act", "activation", out=y2[:T], in_=y[:T], func=AF.Square)
        k.op("dve", "tensor_scalar", out=y2[:T], in0=y2[:T], scalar1=0.044715, scalar2=1.0, op0=ALU.mult, op1=ALU.add)
        k.op("dve", "tensor_tensor", out=y2[:T], in0=y2[:T], in1=y[:T], op=ALU.mult)
        k.op("act", "activation", out=y2[:T], in_=y2[:T], func=AF.Sigmoid, scale=1.5957691216057308)
        k.op("dve", "tensor_tensor", out=z[:T], in0=y2[:T], in1=y[:T], op=ALU.mult)
        zb = k.sb("s5zb", [128, 512], BF16); zT = k.sb("s5zT", [128, 4, 128], BF16)
        k.op("act", "activation", out=zb[:T], in_=z[:T], func=AF.Copy)
        kb.transpose_chunks(zT, zb, 4, T)
        pg = k.ps("py", [128, 512], F32, bufs=2)
        kb.proj(pg[:T, :], zT, wglu, 4, T, 0, 512)
        gt_ = k.sb("s5g", [128, 512], F32)
        k.op("dve", "tensor_tensor", out=gt_[:T], in0=pg[:T, :], in1=bglu[:T], op=ALU.add)
        k.op("act", "activation", out=gt_[:T], in_=gt_[:T], func=AF.Sigmoid)
        ob = k.sb("s5ob", [128, 512], BF16)
        k.op("dve", "tensor_tensor", out=ob[:T], in0=gt_[:T], in1=z[:T], op=ALU.mult)
        k.dma("sp", out=S["OS5"][t0:t0 + T, :], in_=ob[:T])
    for ri, nm in enumerate(("s5p_re", "s5p_im")):
        pt = k.ps("px4", [128, 4, 2, 128], F32, bufs=2)[:, :, 0, :]
        k.op("pe", "transpose", out=pt[:NJ, 0, :], in_=car[ri][:], identity=kb.idf[:])
        o_ = k.sb("s5po", [NJ, 128], F32)
        k.op("act", "activation", out=o_[:], in_=pt[:NJ, 0, :], func=AF.Copy)
        k.dma("sp", out=O[nm], in_=o_[:])
    for ri, nm in enumerate(("s5s_re", "s5s_im")):
        o_ = k.sb("s5so", [NSB, 2048], F32)
        for j0 in range(0, NJ, 4):
            pt = k.ps("px4", [128, 4, 2, 128], F32, bufs=2)[:, :, 0, :]
            for j in range(j0, j0 + 4):
                k.op("pe", "transpose", out=pt[:NSB, j - j0, :], in_=HS[ri][:, j, :], identity=kb.idf[:])
            k.op("act", "activation", out=o_[:, j0 * 128:(j0 + 4) * 128].rearrange("p (a b) -> p a b", a=4), in_=pt[:NSB, :, :], func=AF.Copy)
        k.dma("sp", out=O[nm], in_=o_[:])


def phase_attn_prompt(nc, k, kb, I, O, S):
    trif = k.sb("trif", [128, 128], F32, bufs=1); trib = k.sb("trib", [128, 128], BF16, bufs=1)
    k.dma("sp", out=trif[:], in_=I["tri"])
    k.op("dve", "tensor_copy", out=trib[:], in_=trif[:])
    OATT = k.sb("OATT", [128, NPT, 512], BF16, bufs=1)
    sc = 96 ** -0.5
    for h in range(8):
        qT = k.sb("aqT", [128, SEQ], BF16); kT = k.sb("akT", [128, SEQ], BF16); V = k.sb("aV", [128, NPT, 66], BF16)
        k.dma("sp", out=qT[:96, :], in_=S["QT"][h, :, 0:SEQ])
        k.dma("sp", out=kT[:96, :], in_=S["KT"][h, :, 0:SEQ])
        k.dma("sp", out=V[:], in_=S["V"][0:SEQ, h, :].rearrange("(j p) e -> p j e", p=128))
        for g in range(8):
            po = [k.ps("po", [128, 512], F32, bufs=4) for _ in range(4)]
            for j in range(4 * g + 4):
                t_lo = max(0, j - 4 * g)
                q0 = (4 * g + t_lo) * 128; nq = (4 - t_lo) * 128
                ps = k.ps("psc", [128, 512], F32, bufs=4)
                k.op("pe", "matmul", out=ps[:, :nq], lhsT=kT[:96, j * 128:(j + 1) * 128], rhs=qT[:96, q0:q0 + nq], start=True, stop=True)
                pT = k.sb("apT", [128, 512], BF16, bufs=4)
                k.op("act", "activation", out=pT[:, :nq], in_=ps[:, :nq], func=AF.Exp, scale=sc)
                if j >= 4 * g:
                    k.op("dve", "tensor_tensor", out=pT[:, 0:128], in0=pT[:, 0:128], in1=trib[:], op=ALU.mult)
                for t in range(t_lo, 4):
                    col = (t - t_lo) * 128
                    k.op("pe", "matmul", out=po[t][:, 0:66], lhsT=pT[:, col:col + 128], rhs=V[:, j, :], start=(j == 0), stop=(j == 4 * g + t))
            for t in range(4):
                rinv = k.sb("arinv", [128, 1], F32)
                k.op("dve", "reciprocal", out=rinv[:], in_=po[t][:, 64:65])
                k.op("dve", "tensor_scalar", out=OATT[:, 4 * g + t, h * 64:(h + 1) * 64], in0=po[t][:, 0:64], scalar1=rinv[:, 0:1], scalar2=None, op0=ALU.mult)
    for i in range(NPT):
        k.dma("sp", out=S["OATT"][i * 128:(i + 1) * 128, :], in_=OATT[:, i, :])


def phase_attn_sample(nc, k, kb, I, O, S):
    wf = k.sb("pwf", [128, 2, 1024], F32, bufs=1)
    k.dma("sp", out=wf[:], in_=I["w_mla_ukv"].rearrange("(c p) n -> p c n", p=128))
    WukC = k.sb("WukC", [128, 2, 512], BF16, bufs=1); WuvC = k.sb("WuvC", [128, 2, 512], BF16, bufs=1)
    for c in range(2):
        w3 = wf[:, c, :].rearrange("p (h x) -> p h x", x=128)
        k.op("dve", "tensor_copy", out=WukC[:, c, :].rearrange("p (h d) -> p h d", d=64), in_=w3[:, :, 0:64])
        k.op("dve", "tensor_copy", out=WuvC[:, c, :].rearrange("p (h d) -> p h d", d=64), in_=w3[:, :, 64:128])
    WukT = k.sb("WukT", [64, 8, 256], BF16, bufs=1)
    for h in range(8):
        ptw = k.ps("pk", [128, 2, 512], F32, bufs=2)
        pt = ptw[:, 0, :].rearrange("p (a n) -> p a n", a=4)
        for c in range(2):
            k.op("pe", "transpose", out=pt[:64, c, :], in_=wf[:, c, h * 128:h * 128 + 64], identity=kb.idf[:])
        k.op("act", "activation", out=WukT[:, h, :].rearrange("p (c n) -> p c n", c=2), in_=pt[:64, 0:2, :], func=AF.Copy)
    gkn = k.sb("gkn", [64, 1], F32, bufs=1); gkr = k.sb("gkr", [32, 1], F32, bufs=1)
    k.dma("sp", out=gkn[:], in_=I["k_gain"][0:64].rearrange("(d o) -> d o", o=1))
    k.dma("sp", out=gkr[:], in_=I["k_gain"][64:96].rearrange("(d o) -> d o", o=1))
    qn = k.sb("pqn", [64, 8, NST], BF16, bufs=1); qr = k.sb("pqr", [32, 8, NST], BF16, bufs=1)
    k.dma("sp", out=qn[:], in_=S["QT"][:, 0:64, SEQ:NTOK].rearrange("h d t -> d h t"))
    k.dma("sp", out=qr[:], in_=S["QT"][:, 64:96, SEQ:NTOK].rearrange("h d t -> d h t"))
    qng = k.sb("pqng", [64, 8, NST], BF16, bufs=1); qrg = k.sb("pqrg", [32, 8, NST], BF16, bufs=1)
    k.op("dve", "tensor_scalar", out=qng[:], in0=qn[:], scalar1=gkn[:, 0:1], scalar2=None, op0=ALU.mult)
    k.op("dve", "tensor_scalar", out=qrg[:], in0=qr[:], scalar1=gkr[:, 0:1], scalar2=None, op0=ALU.mult)
    qabsT = k.sb("qabsT", [128, 2, 8, NST], BF16, bufs=1)
    for c in range(2):
        pqw = k.ps("pk", [128, 2, 512], F32, bufs=2)
        pq = pqw[:, 0, :]
        for h in range(8):
            k.op("pe", "matmul", out=pq[:, h * 64:(h + 1) * 64], lhsT=WukT[:64, h, c * 128:(c + 1) * 128], rhs=qng[:64, h, :], start=True, stop=True)
        k.op("act", "activation", out=qabsT[:, c, :, :].rearrange("p h t -> p (h t)"), in_=pq, func=AF.Copy)
    msf = k.sb("msf", [64, NSB * 32], F32, bufs=1); msb = k.sb("msb", [64, NSB, 32], BF16, bufs=1)
    k.dma("sp", out=msf[:], in_=I["maskS"])
    k.op("dve", "tensor_copy", out=msb[:].rearrange("p b x -> p (b x)"), in_=msf[:])
    OATTS = k.sb("OATTS", [64, 8, NST], BF16, bufs=1)
    ptab = k.sb("ptab", [128, NSB], I32, bufs=1)
    k.dma("sp", out=ptab[:], in_=I["ptab"].rearrange("b p -> p b"), allow_slow_non_contiguous=True)
    idx8 = k.sb("idx8", [128, NSB, 8], I32, bufs=1)
    for c in range(8):
        k.op("dve", "tensor_scalar", out=idx8[:, :, c], in0=ptab[:], scalar1=8.0, scalar2=float(c), op0=ALU.mult, op1=ALU.add)
    latv = I["cache_lat"].rearrange("n (a r) c -> (n a) (r c)", a=8)
    krv = I["cache_kr"].rearrange("n (a r) c -> (n a) (r c)", a=8)

    def block_prep(c_ap, kr_ap, R, tag=""):
        cb = k.sb("cb" + tag, [128, 258], BF16, bufs=3)
        k.op("pool", "tensor_copy", out=cb[:R, 0:256], in_=c_ap)
        k.op("dve", "memset", ap=cb[:R, 256:258], constant=1.0, _w=[cb[:]])
        krb = k.sb("krb", [128, 32], BF16)
        k.op("dve", "tensor_copy", out=krb[:R], in_=kr_ap)
        krss = k.sb("krss", [128, 1], F32); krsq = k.sb("krsq", [128, 32], F32)
        k.op("act", "activation", out=krsq[:R], in_=kr_ap, func=AF.Square, accum_out=krss[:R])
        pt = k.ps("ptb", [128, 8, 128], BF16, bufs=2)
        k.op("pe", "transpose", out=pt[:, 0, :R], in_=cb[:R, 0:128], identity=kb.idb[:R, :R])
        k.op("pe", "transpose", out=pt[:, 1, :R], in_=cb[:R, 128:256], identity=kb.idb[:R, :R])
        k.op("pe", "transpose", out=pt[:32, 2, :R], in_=krb[:R, :], identity=kb.idb[:R, :R])
        cT = k.sb("cT" + tag, [128, 3, 128], BF16, bufs=3)
        k.op("act", "activation", out=cT[:, 0:2, :R], in_=pt[:, 0:2, :R], func=AF.Copy)
        k.op("act", "activation", out=cT[:32, 2, :R], in_=pt[:32, 2, :R], func=AF.Copy)
        pkw = k.ps("pk", [128, 2, 512], F32, bufs=2)
        pk = pkw[:, 0, :]
        for c in range(2):
            k.op("pe", "matmul", out=pk[:R, :], lhsT=cT[:, c, :R], rhs=WukC[:, c, :], start=(c == 0), stop=(c == 1))
        sq = k.sb("psq1", [128, 512], F32, bufs=1)
        k.op("act", "activation", out=sq[:R], in_=pk[:R, :], func=AF.Square)
        ss = k.sb("pss" + tag, [128, 8], F32, bufs=3)
        k.op("dve", "tensor_reduce", out=ss[:R], in_=sq[:R].rearrange("p (h d) -> p h d", d=64), axis=AX.X, op=ALU.add)
        k.op("dve", "tensor_scalar", out=ss[:R], in0=ss[:R], scalar1=krss[:R, 0:1], scalar2=96 * EPS, op0=ALU.add, op1=ALU.add)
        k.op("act", "activation", out=ss[:R], in_=ss[:R], func=AF.Ln)
        k.op("act", "activation", out=ss[:R], in_=ss[:R], func=AF.Exp, scale=-0.5)
        return cb, cT, ss

    def score_acc(b, acc, cb, cT, rstd, R, first, last, mask=None):
        pscw = k.ps("psS", [128, 4, 32], F32, bufs=1)
        o3 = pscw[:R, 0, :].rearrange("p (h t) -> p h t", t=4)
        k.op("pe", "matmul", out=o3, lhsT=cT[:, 0, :R], rhs=qabsT[:, 0, :, b * 4:(b + 1) * 4], start=True, stop=False)
        k.op("pe", "matmul", out=o3, lhsT=cT[:, 1, :R], rhs=qabsT[:, 1, :, b * 4:(b + 1) * 4], start=False, stop=False)
        k.op("pe", "matmul", out=o3, lhsT=cT[:32, 2, :R], rhs=qrg[:32, :, b * 4:(b + 1) * 4], start=False, stop=True)
        sc_ = k.sb("psc_", [128, 8, 4], F32)
        k.op("dve", "tensor_tensor", out=sc_[:R], in0=o3, in1=rstd[:R].unsqueeze(2).to_broadcast([R, 8, 4]), op=ALU.mult)
        pT = k.sb("ppT", [128, 32], BF16)
        k.op("act", "activation", out=pT[:R], in_=sc_[:R].rearrange("p h t -> p (h t)"), func=AF.Exp)
        if mask is not None:
            k.op("dve", "tensor_tensor", out=pT[:R], in0=pT[:R], in1=mask, op=ALU.mult)
        k.op("pe", "matmul", out=acc[:32, 0:258], lhsT=pT[:R, :], rhs=cb[:R, :], start=first, stop=last)

    def group4(b, acc, Xg, XRg, first):
        cb4 = k.sb("cb4", [128, 4, 258], BF16, bufs=2)
        k.op("pool", "tensor_copy", out=cb4[:, :, 0:256], in_=Xg)
        k.op("dve", "memset", ap=cb4[:, :, 256:258], constant=1.0, _w=[cb4[:]])
        krb4 = k.sb("krb4", [128, 4, 32], BF16)
        k.op("dve", "tensor_copy", out=krb4[:], in_=XRg)
        krsq = k.sb("krsq4", [128, 4, 32], F32); krss4 = k.sb("krss4", [128, 4], F32)
        k.op("act", "activation", out=krsq[:], in_=XRg, func=AF.Square)
        k.op("dve", "tensor_reduce", out=krss4[:], in_=krsq[:], axis=AX.X, op=ALU.add)
        pt = k.ps("ptb", [128, 8, 128], BF16, bufs=2)
        for g in range(4):
            for c in range(2):
                k.op("pe", "transpose", out=pt[:, g * 2 + c, :], in_=cb4[:, g, c * 128:(c + 1) * 128], identity=kb.idb[:])
        cT4 = k.sb("cT4", [128, 8, 128], BF16, bufs=2)
        k.op("act", "activation", out=cT4[:], in_=pt[:], func=AF.Copy)
        ptk = k.ps("ptb", [128, 8, 128], BF16, bufs=2)
        for g in range(4):
            k.op("pe", "transpose", out=ptk[:32, g, :], in_=krb4[:, g, :], identity=kb.idb[:])
        krT4 = k.sb("krT4", [32, 4, 128], BF16, bufs=2)
        k.op("act", "activation", out=krT4[:], in_=ptk[:32, 0:4, :], func=AF.Copy)
        ss4 = k.sb("ss4", [128, 4, 8], F32, bufs=2)
        for pair in range(2):
            pk = k.ps("pk", [128, 2, 512], F32, bufs=2)
            for gg in range(2):
                g = pair * 2 + gg
                for c in range(2):
                    k.op("pe", "matmul", out=pk[:, gg, :], lhsT=cT4[:, g * 2 + c, :], rhs=WukC[:, c, :], start=(c == 0), stop=(c == 1))
            sq = k.sb("psq2", [128, 2, 512], F32)
            k.op("act", "activation", out=sq[:], in_=pk[:], func=AF.Square)
            k.op("dve", "tensor_reduce", out=ss4[:, pair * 2:(pair + 1) * 2, :], in_=sq[:].rearrange("p g (h d) -> p g h d", d=64), axis=AX.X, op=ALU.add)
        k.op("dve", "scalar_tensor_tensor", out=ss4[:], in0=ss4[:], scalar=96 * EPS, in1=krss4[:].unsqueeze(2).to_broadcast([128, 4, 8]), op0=ALU.add, op1=ALU.add)
        ssf = ss4[:].rearrange("p g h -> p (g h)")
        k.op("act", "activation", out=ssf, in_=ssf, func=AF.Ln)
        k.op("act", "activation", out=ssf, in_=ssf, func=AF.Exp, scale=-0.5)
        psc = k.ps("psS", [128, 4, 32], F32, bufs=1)
        for g in range(4):
            o3 = psc[:, g, :].rearrange("p (h t) -> p h t", t=4)
            k.op("pe", "matmul", out=o3, lhsT=cT4[:, g * 2, :], rhs=qabsT[:, 0, :, b * 4:(b + 1) * 4], start=True, stop=False)
            k.op("pe", "matmul", out=o3, lhsT=cT4[:, g * 2 + 1, :], rhs=qabsT[:, 1, :, b * 4:(b + 1) * 4], start=False, stop=False)
            k.op("pe", "matmul", out=o3, lhsT=krT4[:32, g, :], rhs=qrg[:32, :, b * 4:(b + 1) * 4], start=False, stop=True)
        sc4 = k.sb("sc4", [128, 32, 4], F32)
        k.op("dve", "tensor_tensor", out=sc4[:], in0=psc[:].rearrange("p g (h t) -> p (g h) t", t=4), in1=ssf.unsqueeze(2).to_broadcast([128, 32, 4]), op=ALU.mult)
        pT4 = k.sb("pT4", [128, 4, 32], BF16)
        k.op("act", "activation", out=pT4[:].rearrange("p g x -> p (g x)"), in_=sc4[:].rearrange("p a t -> p (a t)"), func=AF.Exp)
        for g in range(4):
            k.op("pe", "matmul", out=acc[:32, 0:258], lhsT=pT4[:, g, :], rhs=cb4[:, g, :], start=(first and g == 0), stop=False)

    latn = k.sb("latn", [64, 256], F32, bufs=1); krn = k.sb("krn", [64, 32], F32, bufs=1)
    k.dma("sp", out=latn[:], in_=O["lat"][SEQ:NTOK, :])
    k.dma("sp", out=krn[:], in_=O["kr"][SEQ:NTOK, :])
    cbn, cTn, rstdn = block_prep(latn[:], krn[:], 64, tag="n")

    for b in range(NSB):
        acc = k.ps("pacc", [128, 512], F32, bufs=1)
        for c in range(8):
            X = k.sb("gX", [128, 16, 256], F32); XR = k.sb("gXR", [128, 16, 32], F32)
            k.dma("pool", out=X[:].rearrange("p r c -> p (r c)"), in_=latv, _method="indirect_dma_start", out_offset=None,
                  in_offset=bass.IndirectOffsetOnAxis(ap=idx8[:, b, c:c + 1], axis=0), _r=[idx8[:]])
            k.dma("pool", out=XR[:].rearrange("p r c -> p (r c)"), in_=krv, _method="indirect_dma_start", out_offset=None,
                  in_offset=bass.IndirectOffsetOnAxis(ap=idx8[:, b, c:c + 1], axis=0), _r=[idx8[:]])
            for r0 in range(0, 16, 4):
                group4(b, acc, X[:, r0:r0 + 4, :], XR[:, r0:r0 + 4, :], first=(c == 0 and r0 == 0))
        score_acc(b, acc, cbn, cTn, rstdn, 64, first=False, last=True, mask=msb[:, b, :])
        rinv = k.sb("prinv", [128, 1], F32)
        k.op("dve", "reciprocal", out=rinv[:32], in_=acc[:32, 256:257])
        olat = k.sb("olat", [128, 256], BF16)
        k.op("dve", "tensor_scalar", out=olat[:32], in0=acc[:32, 0:256], scalar1=rinv[:32, 0:1], scalar2=None, op0=ALU.mult)
        pt = k.ps("ptb", [128, 8, 128], BF16, bufs=2)
        for c in range(2):
            k.op("pe", "transpose", out=pt[:, c, :32], in_=olat[:32, c * 128:(c + 1) * 128], identity=kb.idb[:32, :32])
        olT = k.sb("olT", [128, 2, 32], BF16)
        k.op("act", "activation", out=olT[:], in_=pt[:, 0:2, :32], func=AF.Copy)
        povw = k.ps("psS", [128, 4, 32], F32, bufs=1)
        pov = povw[:, 0, :]
        for h in range(8):
            for c in range(2):
                k.op("pe", "matmul", out=pov[:64, h * 4:(h + 1) * 4], lhsT=WuvC[:, c, h * 64:(h + 1) * 64], rhs=olT[:, c, h * 4:(h + 1) * 4], start=(c == 0), stop=(c == 1))
        k.op("act", "activation", out=OATTS[:, :, b * 4:(b + 1) * 4], in_=pov[:64, :].rearrange("p (h t) -> p h t", t=4), func=AF.Copy)
    k.dma("sp", out=S["OATTS"], in_=OATTS[:])


def phase_outproj(nc, k, kb, I, O, S):
    kb.mk_stage()
    w_out = kb.load_w("w_out", I["w_out_even"], D, D)
    w_o64 = k.sb("w_o64", [64, 8, D], BF16, bufs=1)
    src = I["w_out_even"][0:512, :].rearrange("(h d) n -> d h n", d=64)
    for h0 in range(0, 8, 2):
        st = kb.stage[(h0 // 2) % 2]
        sv = st[:64, 0:2 * D].rearrange("p (c n) -> p c n", c=2)
        k.dma("sp", out=sv, in_=src[:, h0:h0 + 2, :])
        k.op("dve", "tensor_copy", out=w_o64[:, h0:h0 + 2, :], in_=sv)
    OATTS = k.sb("OATTS2", [64, 8, NST], BF16, bufs=1)
    if not NOP["v"]:
        k.dma("sp", out=OATTS[:], in_=S["OATTS"])
    for (i, t0, R) in tiles():
        xt = k.sb("oxt", [128, D], F32)
        k.dma("sp", out=xt[:R], in_=(I["xp"][t0:t0 + R, :] if i < NPT else I["xs"]))
        os5 = k.sb("os5", [128, 512], BF16); s5T = k.sb("s5T", [128, 4, 128], BF16)
        k.dma("sp", out=os5[:R], in_=S["OS5"][t0:t0 + R, :])
        kb.transpose_chunks(s5T, os5, 4, R)
        if i < NPT:
            oat = k.sb("oat", [128, 512], BF16); oT = k.sb("oT", [128, 4, 128], BF16)
            k.dma("sp", out=oat[:R], in_=S["OATT"][t0:t0 + R, :])
            kb.transpose_chunks(oT, oat, 4, R)
        res = k.sb("ores", [128, D], F32)
        for half in range(2):
            n0 = half * 512
            pso = k.ps("pp", [128, 512], F32, bufs=4)
            if i < NPT:
                for c in range(4):
                    k.op("pe", "matmul", out=pso[:R, :], lhsT=oT[:, c, :R], rhs=w_out[:, c, n0:n0 + 512], start=(c == 0), stop=False)
            elif NOP["v"]:
                pass
            else:
                for h in range(8):
                    k.op("pe", "matmul", out=pso[:R, :], lhsT=OATTS[:64, h, :R], rhs=w_o64[:64, h, n0:n0 + 512], start=(h == 0), stop=False)
            for c in range(4):
                k.op("pe", "matmul", out=pso[:R, :], lhsT=s5T[:, c, :R], rhs=w_out[:, 4 + c, n0:n0 + 512], start=(c == 0 and i >= NPT and NOP["v"]), stop=(c == 3))
            k.op("dve", "tensor_tensor", out=res[:R, n0:n0 + 512], in0=pso[:R, :], in1=xt[:R, n0:n0 + 512], op=ALU.add)
        k.dma("sp", out=S["R1"][t0:t0 + R, :], in_=res[:R])
        if "dbg_r1" in O:
            k.dma("sp", out=O["dbg_r1"][t0:t0 + R, :], in_=res[:R])


def phase_mem(nc, k, kb, I, O, S, l, Rin, Rout):
    kb.mk_stage()
    g_mem = kb.bcast_load("g_mem", I["norm_mem"][l], D); g_q = kb.bcast_load("g_mq", I["mem_q_gain"][l], 128)
    wq = kb.load_w("wmq", I["w_mem_q"][l], D, 512); wo = kb.load_w("wmo", I["w_mem_o"][l], 512, D)
    MKT = k.sb("MKT", [128, 4, 256], BF16, bufs=1); MV = k.sb("MV", [128, 2, 4, 130], BF16, bufs=1)
    k.dma("sp", out=MKT[:], in_=S["MKT"][l].rearrange("h d m -> d h m"))
    k.dma("sp", out=MV[:], in_=S["MV"][l].rearrange("(mt p) h e -> p mt h e", p=128))
    ones = k.sb("ones", [128, 128], BF16, bufs=1)
    k.op("dve", "memset", ap=ones[:], constant=1.0, _w=[ones[:]])
    sc = 128 ** -0.5
    for (i, t0, R) in tiles():
        xt = k.sb("mxt", [128, D], F32); scr = k.sb("mscr", [128, D], F32, bufs=1)
        hn = k.sb("mhn", [128, D], BF16); hnT = k.sb("mhnT", [128, 8, 128], BF16)
        k.dma("sp", out=xt[:R], in_=Rin[t0:t0 + R, :])
        kb.rmsnorm(hn[:R], xt[:R], g_mem[:R], D, R, scr[:R])
        kb.transpose_chunks(hnT, hn, 8, R)
        pq = k.ps("pp", [128, 512], F32, bufs=2)
        kb.proj(pq[:R, :], hnT, wq, 8, R, 0, 512)
        qf = k.sb("mqf", [128, 4, 128], F32)
        k.op("act", "activation", out=qf[:R].rearrange("p h d -> p (h d)"), in_=pq[:R, :], func=AF.Copy)
        kb.head_norm(qf[:R], 4, 128, g_q, R)
        qb = k.sb("mqb", [128, 512], BF16); qT = k.sb("mqT", [128, 4, 128], BF16)
        k.op("act", "activation", out=qb[:R], in_=qf[:R].rearrange("p h d -> p (h d)"), func=AF.Copy)
        kb.transpose_chunks(qT, qb, 4, R)
        obT = k.sb("mobT", [128, 4, 128], BF16)
        if i < NPT:
            ob = k.sb("mob", [128, 512], BF16)
            for h in range(4):
                pss = k.ps("pp", [128, 512], F32, bufs=2)
                for mt in range(2):
                    k.op("pe", "matmul", out=pss[:, mt * 128:mt * 128 + R], lhsT=MKT[:, h, mt * 128:(mt + 1) * 128], rhs=qT[:, h, :R], start=True, stop=True)
                pT = k.sb("mpT", [128, 2, 128], BF16)
                k.op("act", "activation", out=pT[:, :, :R], in_=pss[:, 0:256].rearrange("p (a n) -> p a n", a=2)[:, :, :R], func=AF.Exp, scale=sc)
                po = k.ps("pp", [128, 512], F32, bufs=2)
                for mt in range(2):
                    k.op("pe", "matmul", out=po[:R, 0:130], lhsT=pT[:, mt, :R], rhs=MV[:, mt, h, :], start=(mt == 0), stop=(mt == 1))
                rinv = k.sb("mrinv", [128, 1], F32)
                k.op("dve", "reciprocal", out=rinv[:R], in_=po[:R, 128:129])
                k.op("dve", "tensor_scalar", out=ob[:R, h * 128:(h + 1) * 128], in0=po[:R, 0:128], scalar1=rinv[:R, 0:1], scalar2=None, op0=ALU.mult)
            kb.transpose_chunks(obT, ob, 4, R)
        else:
            for bq in range(NSB):
                Kb = k.sb("mKb", [128, 2, 512], F32); Vb = k.sb("mVb", [128, 2, 512], F32)
                k.dma("sp", out=Kb[:], in_=I["cmk"][l, bq].rearrange("(mt p) x -> p mt x", p=128))
                k.dma("sp", out=Vb[:], in_=I["cmv"][l, bq].rearrange("(mt p) x -> p mt x", p=128))
                Vbb = k.sb("mVbb", [128, 2, 512], BF16)
                k.op("pool", "tensor_copy", out=Vbb[:], in_=Vb[:])
                KbT = k.sb("mKbT", [128, 4, 256], BF16)
                for mt in range(2):
                    ptf = k.ps("ptf", [128, 4, 128], F32, bufs=2)
                    for h in range(4):
                        k.op("pe", "transpose", out=ptf[:, h, :], in_=Kb[:, mt, h * 128:(h + 1) * 128], identity=kb.idf[:])
                    k.op("act", "activation", out=KbT[:, :, mt * 128:(mt + 1) * 128], in_=ptf[:, :, :], func=AF.Copy)
                pss = k.ps("psS", [128, 32], F32, bufs=2)
                for mt in range(2):
                    for h in range(4):
                        c0 = (mt * 4 + h) * 4
                        k.op("pe", "matmul", out=pss[:, c0:c0 + 4], lhsT=KbT[:, h, mt * 128:(mt + 1) * 128], rhs=qT[:, h, bq * 4:(bq + 1) * 4], start=True, stop=True)
                pTb = k.sb("mpTb", [128, 32], BF16)
                k.op("act", "activation", out=pTb[:], in_=pss[:, :], func=AF.Exp, scale=sc)
                psl = k.ps("psS", [128, 32], F32, bufs=2)
                for mt in range(2):
                    k.op("pe", "matmul", out=psl[:, 0:16], lhsT=ones[:, :], rhs=pTb[:, mt * 16:(mt + 1) * 16], start=(mt == 0), stop=(mt == 1))
                rl = k.sb("mrl", [128, 16], F32)
                k.op("dve", "reciprocal", out=rl[:], in_=psl[:, 0:16])
                pso = k.ps("psS", [128, 32], F32, bufs=2)
                for h in range(4):
                    for mt in range(2):
                        c0 = (mt * 4 + h) * 4
                        k.op("pe", "matmul", out=pso[:, h * 4:(h + 1) * 4], lhsT=Vbb[:, mt, h * 128:(h + 1) * 128], rhs=pTb[:, c0:c0 + 4], start=(mt == 0), stop=(mt == 1))
                k.op("dve", "tensor_tensor", out=obT[:, :, bq * 4:(bq + 1) * 4], in0=pso[:, 0:16].rearrange("p (h t) -> p h t", t=4),
                     in1=rl[:].rearrange("p (h t) -> p h t", t=4), op=ALU.mult)
        res = k.sb("mres", [128, D], F32)
        for half in range(2):
            n0 = half * 512
            pso2 = k.ps("pp", [128, 512], F32, bufs=2)
            for h in range(4):
                k.op("pe", "matmul", out=pso2[:R, :], lhsT=obT[:, h, :R], rhs=wo[:, h, n0:n0 + 512], start=(h == 0), stop=(h == 3))
            k.op("dve", "tensor_tensor", out=res[:R, n0:n0 + 512], in0=pso2[:R, :], in1=xt[:R, n0:n0 + 512], op=ALU.add)
        k.dma("sp", out=Rout(i, t0, R), in_=res[:R])
        if i >= NPT and DBG.get("ap") is not None:
            k.dma("sp", out=DBG["ap"][DBG["n"]], in_=res[:R])
            DBG["n"] += 1


def phase_mlp(nc, k, kb, I, O, S, l, Rin, Rout):
    kb.mk_stage(512)
    g_mlp = kb.bcast_load("g_mlp", I["norm_mlp"][l], D)
    wup = kb.load_w("wup", I["w_mlp_up"][l], D, 4096, stage_cols=512)
    wdn = kb.load_w("wdn", I["w_mlp_down"][l], 4096, D, stage_cols=512)
    groups = [(g * 4, [(g * 4 + s, (g * 4 + s) * 128, 128) for s in range(4)]) for g in range(NPT // 4)] + [(NPT, [(NPT, SEQ, NST)])]
    for (_, subs) in groups:
        NT = sum(R for (_, _, R) in subs)
        xt4 = k.sb("fxt4", [128, 4, D], F32, bufs=1)
        hnT4 = k.sb("fhnT4", [128, 8, 512], BF16, bufs=1)
        for s, (i, t0, R) in enumerate(subs):
            res = k.sb("fres", [128, D], F32, bufs=1)
            hn = k.sb("fhn", [128, D], BF16)
            k.dma("sp", out=xt4[:R, s, :], in_=Rin[t0:t0 + R, :])
            kb.rmsnorm(hn[:R], xt4[:R, s, :], g_mlp[:R], D, R, res[:R])
            kb.transpose_chunks(hnT4[:, :, s * 128:(s + 1) * 128], hn, 8, R)
        hT = k.sb("fhT", [128, 32, 512], BF16, bufs=1)
        for fc in range(32):
            pu = k.ps("pp", [128, 512], F32, bufs=4)
            for kc in range(8):
                k.op("pe", "matmul", out=pu[:, :NT], lhsT=wup[:, kc, fc * 128:(fc + 1) * 128], rhs=hnT4[:, kc, :NT], start=(kc == 0), stop=(kc == 7))
            rl = k.sb("frl", [128, 512], F32)
            k.op("act", "activation", out=rl[:, :NT], in_=pu[:, :NT], func=AF.Relu)
            k.op("pool" if fc % 2 else "dve", "tensor_tensor", out=hT[:, fc, :NT], in0=rl[:, :NT], in1=rl[:, :NT], op=ALU.mult)
        for s, (i, t0, R) in enumerate(subs):
            res = k.sb("fres", [128, D], F32, bufs=1)
            for half in range(2):
                n0 = half * 512
                pd = k.ps("pp", [128, 512], F32, bufs=4)
                for fc in range(32):
                    k.op("pe", "matmul", out=pd[:R, :], lhsT=hT[:, fc, s * 128:s * 128 + R], rhs=wdn[:, fc, n0:n0 + 512], start=(fc == 0), stop=(fc == 31))
                k.op("dve", "tensor_tensor", out=res[:R, n0:n0 + 512], in0=pd[:R, :], in1=xt4[:R, s, n0:n0 + 512], op=ALU.add)
            k.dma("sp", out=Rout(i, t0, R), in_=res[:R])
            if i >= NPT and DBG.get("ap") is not None:
                k.dma("sp", out=DBG["ap"][DBG["n"]], in_=res[:R])
                DBG["n"] += 1


def phase_hgrn(nc, k, kb, I, O, S, Rin, Rout):
    kb.mk_stage(1024)
    g_mix = kb.bcast_load("g_mix1", I["norm_mix"][1], D); g_on = kb.bcast_load("g_on", I["hgrn_on"], D)
    w_in = kb.load_w("w_ino", I["w_in_odd"], D, 4096, stage_cols=1024)
    w_out = kb.load_w("w_outo", I["w_out_odd"], D, D, stage_cols=1024)
    a0 = k.sb("lba0", [128, 8], F32, bufs=1); a1 = k.sb("lba1", [128, 8], F32, bufs=1)
    k.dma("sp", out=a0[:], in_=I["hgrn_lb"][0].rearrange("(h d) -> d h", d=128), allow_slow_non_contiguous=True)
    k.dma("sp", out=a1[:], in_=I["hgrn_lb"][1].rearrange("(h d) -> d h", d=128), allow_slow_non_contiguous=True)
    k.op("act", "activation", out=a0[:], in_=a0[:], func=AF.Exp)
    k.op("act", "activation", out=a1[:], in_=a1[:], func=AF.Exp)
    lb = k.sb("lb", [128, 8], F32, bufs=1); oml = k.sb("oml", [128, 8], F32, bufs=1)
    k.op("dve", "tensor_tensor", out=a0[:], in0=a0[:], in1=a1[:], op=ALU.add)
    k.op("dve", "reciprocal", out=a0[:], in_=a0[:])
    k.op("dve", "tensor_tensor", out=lb[:], in0=a1[:], in1=a0[:], op=ALU.mult)
    k.op("dve", "tensor_scalar", out=oml[:], in0=lb[:], scalar1=-1.0, scalar2=1.0, op0=ALU.mult, op1=ALU.add)
    def cload(nm, shape, dt=BF16):
        f_ = k.sb(nm + "f", shape, F32, bufs=1)
        k.dma("sp", out=f_[:], in_=I[nm])
        if dt == F32:
            return f_
        t_ = k.sb(nm + "b", shape, BF16, bufs=1)
        k.op("dve", "tensor_copy", out=t_[:], in_=f_[:])
        return t_
    rmask512 = k.sb("rmask512", [128, 512], F32, bufs=1)
    k.dma("sp", out=rmask512[:].rearrange("p (a n) -> p a n", a=4), in_=I["rmask64"].unsqueeze(1).to_broadcast([128, 4, 128]))
    rmask64 = cload("rmask64", [128, 128], F32); rmask4 = cload("rmask4", [128, 64], F32)
    bd64 = cload("bd64", [128, 128]); bd4 = cload("bd4", [64, 64])
    qmask = cload("qmask", [128, 256]); bmask = cload("bmask", [128, 1024]); rowmask = cload("rowmask", [64, 16], F32)
    Sf = k.sb("Sf", [128, 8, 128], F32, bufs=1); Sb = k.sb("Sb", [128, 8, 128], BF16, bufs=1)
    k.op("dve", "memset", ap=Sf[:], constant=0.0, _w=[Sf[:]])
    k.op("dve", "memset", ap=Sb[:], constant=0.0, _w=[Sb[:]])
    for (i, t0, R) in tiles():
        samp = i >= NPT
        xt = k.sb("hxt", [128, D], F32, bufs=1); scr = k.sb("hscr", [128, D], F32, bufs=1)
        hn = k.sb("hhn", [128, D], BF16, bufs=1); hnT = k.sb("hhnT", [128, 8, 128], BF16, bufs=1)
        k.dma("sp", out=xt[:R], in_=Rin[t0:t0 + R, :])
        kb.rmsnorm(hn[:R], xt[:R], g_mix[:R], D, R, scr[:R])
        kb.transpose_chunks(hnT, hn, 8, R)
        vb = k.sb("hvb", [128, D], BF16, bufs=1); gs = k.sb("hgs", [128, D], F32, bufs=1)
        for half in range(2):
            pv = k.ps("pp", [128, 512], F32, bufs=3)
            kb.proj(pv[:R, :], hnT, w_in, 8, R, 2048 + half * 512, 2048 + half * 512 + 512)
            k.op("act", "activation", out=vb[:R, half * 512:(half + 1) * 512], in_=pv[:R, :], func=AF.Copy)
            pg = k.ps("pp", [128, 512], F32, bufs=3)
            kb.proj(pg[:R, :], hnT, w_in, 8, R, 3072 + half * 512, 3072 + half * 512 + 512)
            k.op("act", "activation", out=gs[:R, half * 512:(half + 1) * 512], in_=pg[:R, :], func=AF.Silu)
        otile = k.sb("hot", [128, D], F32, bufs=1)
        rmask = rmask4 if samp else rmask64
        for h in range(8):
            if samp:
                S0 = k.sb("hS0", [128, NSB, 128], F32, bufs=1); S0b = k.sb("hS0b", [128, NSB, 128], BF16, bufs=1)
                k.dma("sp", out=S0[:], in_=I["hg0"][:, h].rearrange("b d e -> d b e"))
                k.op("pool", "tensor_copy", out=S0b[:], in_=S0[:])
            hh = h % 4
            if hh == 0:
                pq4 = k.ps("pq4", [128, 2, 512], F32, bufs=1)
                for h2 in range(4):
                    hx = h + h2
                    for kc in range(8):
                        k.op("pe", "matmul", out=pq4[:, 0, h2 * 128:h2 * 128 + R], lhsT=w_in[:, kc, hx * 128:(hx + 1) * 128], rhs=hnT[:, kc, :R], start=(kc == 0), stop=(kc == 7))
                    for kc in range(8):
                        k.op("pe", "matmul", out=pq4[:, 1, h2 * 128:h2 * 128 + R], lhsT=w_in[:, kc, 1024 + hx * 128:1024 + (hx + 1) * 128], rhs=hnT[:, kc, :R], start=(kc == 0), stop=(kc == 7))
                q4 = pq4[:, 0, :].rearrange("p (a n) -> p a n", a=4)[:, :, :R]
                f4 = pq4[:, 1, :].rearrange("p (a n) -> p a n", a=4)[:, :, :R]
                sig4 = k.sb("hsig4", [128, 4, 128], F32, bufs=1); fg4 = k.sb("hfg4", [128, 4, 128], F32, bufs=1); logf4 = k.sb("hlogf4", [128, 4, 128], F32, bufs=1)
                kk4 = k.sb("hkk4", [128, 4, 128], F32, bufs=1); bc4 = k.sb("hbc4", [128, 4, 128], F32, bufs=1); eb4 = k.sb("heb4", [128, 4, 128], F32); enb4 = k.sb("henb4", [128, 4, 128], F32, bufs=1)
                qs4 = k.sb("hqs4", [128, 4, 128], F32, bufs=1); qt4 = k.sb("hqt4", [128, 4, 128], BF16); kt4 = k.sb("hkt4", [128, 4, 128], BF16)
                V = lambda t_: t_[:, :, :R]
                k.op("act", "activation", out=V(sig4), in_=f4, func=AF.Sigmoid)
                k.op("dve", "tensor_tensor", out=V(fg4), in0=V(sig4), in1=oml[:, h:h + 4].unsqueeze(2).to_broadcast([128, 4, R]), op=ALU.mult)
                k.op("dve", "tensor_tensor", out=V(fg4), in0=V(fg4), in1=lb[:, h:h + 4].unsqueeze(2).to_broadcast([128, 4, R]), op=ALU.add)
                k.op("act", "activation", out=V(logf4), in_=V(fg4), func=AF.Ln)
                k.op("pool", "tensor_scalar", out=V(kk4), in0=V(fg4), scalar1=-1.0, scalar2=1.0, op0=ALU.mult, op1=ALU.add)
                if not samp:
                    k.op("dve", "tensor_tensor_scan", out=bc4[:].rearrange("p a n -> p (a n)"), data0=rmask512[:, :], data1=logf4[:].rearrange("p a n -> p (a n)"), initial=0.0, op0=ALU.mult, op1=ALU.add)
                else:
                    for h2 in range(4):
                        k.op("dve", "tensor_tensor_scan", out=bc4[:, h2, :R], data0=rmask[:, :R], data1=logf4[:, h2, :R], initial=0.0, op0=ALU.mult, op1=ALU.add)
                k.op("act", "activation", out=V(eb4), in_=V(bc4), func=AF.Exp)
                k.op("act", "activation", out=V(enb4), in_=V(bc4), func=AF.Exp, scale=-1.0)
                k.op("act", "activation", out=V(qs4), in_=q4, func=AF.Silu)
                k.op("dve", "tensor_tensor", out=V(qt4), in0=V(qs4), in1=V(eb4), op=ALU.mult)
                k.op("pool", "tensor_tensor", out=V(kt4), in0=V(kk4), in1=V(enb4), op=ALU.mult)
            qt_ = qt4[:, hh, :]; kt_ = kt4[:, hh, :]; eb = eb4[:, hh, :]
            ptk = k.ps("ptb", [128, 8, 128], BF16, bufs=2)
            k.op("pe", "transpose", out=ptk[:R, 0, :], in_=kt_[:, :R], identity=kb.idb[:])
            Ktok = k.sb("hKtok", [128, 128], BF16)
            k.op("act", "activation", out=Ktok[:R, :], in_=ptk[:R, 0, :], func=AF.Copy)
            pat = k.ps("pp", [128, 512], F32, bufs=3)
            k.op("pe", "matmul", out=pat[:R, 0:R], lhsT=kt_[:, :R], rhs=qt_[:, :R], start=True, stop=True)
            attb = k.sb("hattb", [128, 128], BF16)
            k.op("dve", "tensor_tensor", out=attb[:R, :R], in0=pat[:R, 0:R], in1=(bd4[:, :] if samp else bd64[:, :]), op=ALU.mult)
            po = k.ps("pp", [128, 512], F32, bufs=3)
            k.op("pe", "matmul", out=po[:R, 0:128], lhsT=attb[:R, :R], rhs=vb[:R, h * 128:(h + 1) * 128], start=True, stop=False)
            if not samp:
                qz = k.sb("hqz", [128, 2, 128], BF16)
                k.op("dve", "tensor_tensor", out=qz[:], in0=qt_[:, :].unsqueeze(1).to_broadcast([128, 2, 128]), in1=qmask[:, :].rearrange("p (c t) -> p c t", c=2), op=ALU.mult)
                for c in range(2):
                    k.op("pe", "matmul", out=po[:, 0:128], lhsT=qz[:, c, :], rhs=Sb[:, h, :], start=False, stop=(c == 1))
                    pP = k.ps("pp", [128, 512], F32, bufs=3)
                    k.op("pe", "matmul", out=pP[:, 0:128], lhsT=Ktok[c * 64:(c + 1) * 64, :], rhs=vb[c * 64:(c + 1) * 64, h * 128:(h + 1) * 128], start=True, stop=True)
                    tmp = k.sb("htmp", [128, 128], F32)
                    k.op("dve", "tensor_tensor", out=tmp[:], in0=pP[:, 0:128], in1=Sf[:, h, :], op=ALU.add)
                    k.op("act", "activation", out=Sf[:, h, :], in_=tmp[:], func=AF.Copy, scale=eb[:, c * 64 + 63:c * 64 + 64])
                    k.op("pool", "tensor_copy", out=Sb[:, h, :], in_=Sf[:, h, :])
            else:
                QB = k.sb("hQB", [128, NSB, 64], BF16, bufs=1)
                k.op("dve", "tensor_tensor", out=QB[:], in0=qt_[:, :64].unsqueeze(1).to_broadcast([128, NSB, 64]), in1=bmask[:, :].rearrange("p (b t) -> p b t", b=NSB), op=ALU.mult)
                KB = k.sb("hKB", [64, NSB, 128], BF16, bufs=1)
                k.op("dve", "tensor_tensor", out=KB[:], in0=Ktok[:64, :].unsqueeze(1).to_broadcast([64, NSB, 128]), in1=rowmask[:, :].unsqueeze(2).to_broadcast([64, NSB, 128]), op=ALU.mult)
                for bq in range(NSB):
                    k.op("pe", "matmul", out=po[:R, 0:128], lhsT=QB[:, bq, :], rhs=S0b[:, bq, :], start=False, stop=(bq == NSB - 1))
                ebl = k.sb("hebl", [128, NSB], F32)
                k.op("act", "activation", out=ebl[:], in_=eb[:, 3:64:4], func=AF.Copy)
                Sn = k.sb("hSn", [128, NSB, 128], F32, bufs=1)
                for b0 in range(0, NSB, 4):
                    pP = k.ps("pPs", [128, 512], F32, bufs=1)
                    for bq in range(b0, b0 + 4):
                        k.op("pe", "matmul", out=pP[:, (bq - b0) * 128:(bq - b0 + 1) * 128], lhsT=KB[:64, bq, :], rhs=vb[:64, h * 128:(h + 1) * 128], start=True, stop=True)
                    tmp4 = k.sb("htmp4", [128, 4, 128], F32, bufs=1)
                    k.op("dve", "tensor_tensor", out=tmp4[:], in0=pP[:, :].rearrange("p (a n) -> p a n", a=4), in1=S0[:, b0:b0 + 4, :], op=ALU.add)
                    k.op("pool", "tensor_tensor", out=Sn[:, b0:b0 + 4, :], in0=tmp4[:], in1=ebl[:, b0:b0 + 4].unsqueeze(2).to_broadcast([128, 4, 128]), op=ALU.mult)
                k.dma("sp", out=O["hgs"][:, h].rearrange("b d e -> d b e"), in_=Sn[:])
            k.op("act", "activation", out=otile[:R, h * 128:(h + 1) * 128], in_=po[:R, 0:128], func=AF.Copy)
        on = k.sb("hon", [128, D], F32, bufs=1); ob = k.sb("hob", [128, D], BF16, bufs=1); obT = k.sb("hobT", [128, 8, 128], BF16, bufs=1)
        kb.rmsnorm(on[:R], otile[:R], g_on[:R], D, R, scr[:R])
        k.op("dve", "tensor_tensor", out=ob[:R], in0=on[:R], in1=gs[:R], op=ALU.mult)
        kb.transpose_chunks(obT, ob, 8, R)
        res = k.sb("hres", [128, D], F32, bufs=1)
        for half in range(2):
            n0 = half * 512
            pso = k.ps("pp", [128, 512], F32, bufs=3)
            kb.proj(pso[:R, :], obT, w_out, 8, R, n0, n0 + 512)
            k.op("dve", "tensor_tensor", out=res[:R, n0:n0 + 512], in0=pso[:R, :], in1=xt[:R, n0:n0 + 512], op=ALU.add)
        k.dma("sp", out=Rout(i, t0, R), in_=res[:R])
        if i >= NPT and DBG.get("ap") is not None:
            k.dma("sp", out=DBG["ap"][DBG["n"]], in_=res[:R])
            DBG["n"] += 1
    k.dma("sp", out=O["hgp"].rearrange("h d e -> d h e"), in_=Sf[:])


def build(stages="A"):
    nc = bass.Bass("TRN2", target_bir_lowering=False)
    NOP["v"] = "P" not in stages

    def din(name, shape, dt=F32):
        return nc.dram_tensor(name, list(shape), dt, kind="ExternalInput").ap()

    def dout(name, shape, dt=F32):
        return nc.dram_tensor(name, list(shape), dt, kind="ExternalOutput").ap()

    def dscr(name, shape, dt=F32):
        return nc.dram_tensor(name, list(shape), dt, kind="Internal").ap()

    I = {}
    I["xp"] = din("xp", [SEQ, D]); I["xs"] = din("xs", [NST, D])
    I["mem"] = din("mem", [256, D])
    I["ident"] = din("ident", [128, 128]); I["rope"] = din("rope", [NTOK + 64, 32])
    for nm, shp in [("norm_mix", [2, D]), ("norm_mem", [2, D]), ("norm_memsrc", [2, D]), ("norm_mlp", [2, D]),
                    ("w_mem_q", [2, D, 512]), ("w_mem_k", [2, D, 512]), ("w_mem_v", [2, D, 512]), ("w_mem_o", [2, 512, D]),
                    ("mem_q_gain", [2, 128]), ("mem_k_gain", [2, 128]),
                    ("w_mlp_up", [2, D, 4096]), ("w_mlp_down", [2, 4096, D]),
                    ("w_in_even", [D, 1568]), ("mla_cq_norm", [768]), ("mla_ckv_norm", [256]),
                    ("w_mla_uq", [768, 768]), ("w_mla_ukv", [256, 1024]),
                    ("q_gain", [96]), ("k_gain", [96])]:
        I[nm] = din(nm, shp)
    for nm, shp in [("s5_lre", [32, 64]), ("s5_lim", [32, 64]), ("s5_ls", [32]), ("s5_bre", [32, 64, 16]), ("s5_bim", [32, 64, 16]),
                    ("s5_cre", [32, 16, 64]), ("s5_cim", [32, 16, 64]), ("s5_d", [512]), ("s5_wglu", [512, 512]), ("s5_bglu", [512]),
                    ("s5_h0re", [NSB, 2048]), ("s5_h0im", [NSB, 2048]), ("ramp", [128, 128])]:
        I[nm] = din(nm, shp)
    O = {}
    O["yp"] = dout("o_yp", [SEQ, D]); O["ys"] = dout("o_ys", [NST, D])
    O["hgp"] = dout("o_hgp", [8, 128, 128]); O["hgs"] = dout("o_hgs", [NSB, 8, 128, 128])
    O["s5p_re"] = dout("o_s5p_re", [16, 128]); O["s5p_im"] = dout("o_s5p_im", [16, 128])
    O["s5s_re"] = dout("o_s5s_re", [NSB, 2048]); O["s5s_im"] = dout("o_s5s_im", [NSB, 2048])
    O["lat"] = dout("o_lat", [NTOK, 256]); O["kr"] = dout("o_kr", [NTOK, 32])
    O["mk"] = dout("o_mk", [2, 256, 512]); O["mv"] = dout("o_mv", [2, 256, 512])
    S = {}
    S["QT"] = dscr("s_qt", [8, 96, NTOK], BF16); S["KT"] = dscr("s_kt", [8, 96, NTOK], BF16)
    S["V"] = dscr("s_v", [NTOK, 8, 66], BF16)
    S["U"] = dscr("s_u", [NTOK, 512], F32); S["UT"] = dscr("s_ut", [4, 128, NTOK], BF16)
    S["MKT"] = dscr("s_mkt", [2, 4, 128, 256], BF16); S["MV"] = dscr("s_mva", [2, 256, 4, 130], BF16)

    S["OS5"] = dscr("s_os5", [NTOK, 512], BF16)
    S["OATT"] = dscr("s_oatt", [NTOK, 512], BF16); S["OATTS"] = dscr("s_oatts", [64, 8, NST], BF16)
    S["R1"] = dscr("s_r1", [NTOK, D], F32)
    for nm in ("R2", "R3", "R4", "R5"):
        S[nm] = dscr("s_" + nm, [NTOK, D], F32)
    I["cmk"] = din("cmk", [2, NSB, 256, 512]); I["cmv"] = din("cmv", [2, NSB, 256, 512])
    I["hgrn_on"] = din("hgrn_on", [D]); I["w_in_odd"] = din("w_in_odd", [D, 4096]); I["w_out_odd"] = din("w_out_odd", [D, D])
    I["hgrn_lb"] = din("hgrn_lb", [2, D]); I["hg0"] = din("hg0", [NSB, 8, 128, 128])
    for nm, shp in (("rmask64", [128, 128]), ("rmask4", [128, 64]), ("bd64", [128, 128]), ("bd4", [64, 64]), ("qmask", [128, 256]),
                    ("bmask", [128, 1024]), ("rowmask", [64, 16])):
        I[nm] = din(nm, shp)
    I["tri"] = din("tri", [128, 128]); I["maskS"] = din("maskS", [64, NSB * 32])
    I["w_out_even"] = din("w_out_even", [D, D])
    if "P" in stages:
        I["cache_lat"] = din("cache_lat", [NPHYS, 128, 256]); I["cache_kr"] = din("cache_kr", [NPHYS, 128, 32])
        I["ptab"] = din("ptab", [NSB, NPAGES], I32)
    if "d" in stages:
        O["dbg_r1"] = dout("dbg_r1", [NTOK, D])
    DBG["ap"] = dout("dbg_s", [5, NST, D]) if "d" in stages else None
    DBG["n"] = 0
    global DECLARED
    DECLARED = set(I.keys())
    with ExitStack() as gst:
        k = Sched(nc, gst)
        kb = KB(nc, k)
        k.begin_phase()
        kb.idf = k.gsb("idf", [128, 128], F32); kb.idb = k.gsb("idb", [128, 128], BF16)
        k.dma("sp", out=kb.idf[:], in_=I["ident"])
        k.op("dve", "tensor_copy", out=kb.idb[:], in_=kb.idf[:])
        k.end_phase()

        k.begin_phase()
        kb.mk_stage()
        for l in (range(2) if "M" in stages else []):
            g_src = kb.bcast_load("g_src", I["norm_memsrc"][l], D)
            g_k = kb.bcast_load("g_mk", I["mem_k_gain"][l], 128)
            wk = kb.load_w("wmk", I["w_mem_k"][l], D, 512)
            wv = kb.load_w("wmv", I["w_mem_v"][l], D, 512)
            for mt in range(2):
                R = 128
                xt = k.sb("mx", [128, D], F32); scr = k.sb("mscr", [128, D], F32)
                mn = k.sb("mn", [128, D], BF16); mnT = k.sb("mnT", [128, 8, 128], BF16)
                k.dma("sp", out=xt[:], in_=I["mem"][mt * 128:(mt + 1) * 128, :])
                kb.rmsnorm(mn[:], xt[:], g_src[:], D, R, scr[:])
                kb.transpose_chunks(mnT, mn, 8, R)
                pk = k.ps("pp", [128, 512], bufs=6); pv = k.ps("pp", [128, 512], bufs=6)
                kb.proj(pk[:], mnT, wk, 8, R, 0, 512)
                kb.proj(pv[:], mnT, wv, 8, R, 0, 512)
                kf = k.sb("mkf", [128, 4, 128], F32); vf = k.sb("mvf", [128, 4, 128], F32)
                k.op("act", "activation", out=kf[:], in_=pk[:].rearrange("p (h d) -> p h d", h=4), func=AF.Copy)
                k.op("act", "activation", out=vf[:], in_=pv[:].rearrange("p (h d) -> p h d", h=4), func=AF.Copy)
                kb.head_norm(kf[:], 4, 128, g_k, R)
                k.dma("sp", out=O["mk"][l, mt * 128:(mt + 1) * 128, :], in_=kf[:].rearrange("p h d -> p (h d)"))
                k.dma("sp", out=O["mv"][l, mt * 128:(mt + 1) * 128, :], in_=vf[:].rearrange("p h d -> p (h d)"))
                kbf = k.sb("mkb", [128, 512], BF16)
                k.op("dve", "tensor_copy", out=kbf[:], in_=kf[:].rearrange("p h d -> p (h d)"))
                kT = k.sb("mkT", [128, 4, 128], BF16)
                kb.transpose_chunks(kT, kbf, 4, R)
                k.dma("sp", out=S["MKT"][l, :, :, mt * 128:(mt + 1) * 128].rearrange("h d m -> d h m"), in_=kT[:])
                va = k.sb("mva", [128, 4, 130], BF16)
                k.op("dve", "memset", ap=va[:, :, 128:130], constant=1.0, _w=[va[:]])
                k.op("dve", "tensor_copy", out=va[:, :, 0:128], in_=vf[:])
                k.dma("sp", out=S["MV"][l, mt * 128:(mt + 1) * 128], in_=va[:])
        k.end_phase()

        k.begin_phase()
        kb.mk_stage()
        g_mix = kb.bcast_load("g_mix", I["norm_mix"][0], D)
        g_cq = kb.bcast_load("g_cq", I["mla_cq_norm"], 768)
        g_ckv = kb.bcast_load("g_ckv", I["mla_ckv_norm"], 256)
        g_q = kb.bcast_load("g_q", I["q_gain"], 96)
        g_kk = kb.bcast_load("g_kk", I["k_gain"], 96)
        w_in = kb.load_w("w_in", I["w_in_even"], D, 1568, stage_cols=1568)
        w_uq = kb.load_w("w_uq", I["w_mla_uq"], 768, 768)
        w_ukv = kb.load_w("w_ukv", I["w_mla_ukv"], 256, 1024)
        for (i, t0, R) in (tiles() if "1" in stages else []):
            if "x" in stages and i not in (0, 32):
                continue
            xsrc = I["xp"][t0:t0 + R, :] if i < NPT else I["xs"]
            xt = k.sb("xt", [128, D], F32); scr = k.sb("scr", [128, D], F32)
            hn = k.sb("hn", [128, D], BF16); hnT = k.sb("hnT", [128, 8, 128], BF16)
            k.dma("sp", out=xt[:R], in_=xsrc)
            cs = k.sb("cs", [128, 32], F32)
            k.dma("sp", out=cs[:R], in_=I["rope"][t0:t0 + R, :])
            kb.rmsnorm(hn[:R], xt[:R], g_mix[:R], D, R, scr[:R])
            kb.transpose_chunks(hnT, hn, 8, R)
            pA = k.ps("pp", [128, 512], bufs=6); pB = k.ps("pp", [128, 512], bufs=6); pC = k.ps("pp", [128, 512], bufs=6); pD = k.ps("pp", [128, 512], bufs=6)
            kb.proj(pA[:R, :], hnT, w_in, 8, R, 0, 512)
            kb.proj(pB[:R, 0:256], hnT, w_in, 8, R, 512, 768)
            kb.proj(pC[:R, 0:288], hnT, w_in, 8, R, 768, 1056)
            kb.proj(pD[:R, :], hnT, w_in, 8, R, 1056, 1568)
            if CUT <= 1:
                continue
            ssa = k.sb("ssa", [128, 1], F32); ssb = k.sb("ssb", [128, 1], F32)
            k.op("act", "activation", out=scr[:R, 0:512], in_=pA[:R, :], func=AF.Square, accum_out=ssa[:R])
            k.op("act", "activation", out=scr[:R, 512:768], in_=pB[:R, 0:256], func=AF.Square, accum_out=ssb[:R])
            k.op("dve", "tensor_tensor", out=ssa[:R], in0=ssa[:R], in1=ssb[:R], op=ALU.add)
            kb.rstd_from_ss(ssa[:R], 768)
            cq = k.sb("cq", [128, 768], BF16); cqT = k.sb("cqT", [128, 6, 128], BF16)
            k.op("dve", "scalar_tensor_tensor", out=cq[:R, 0:512], in0=pA[:R, :], scalar=ssa[:R, 0:1], in1=g_cq[:R, 0:512], op0=ALU.mult, op1=ALU.mult)
            k.op("dve", "scalar_tensor_tensor", out=cq[:R, 512:768], in0=pB[:R, 0:256], scalar=ssa[:R, 0:1], in1=g_cq[:R, 512:768], op0=ALU.mult, op1=ALU.mult)
            if CUT <= 2:
                continue
            kb.transpose_chunks(cqT, cq, 6, R)
            pQ1 = k.ps("pp", [128, 512], bufs=6); pQ2 = k.ps("pp", [128, 512], bufs=6)
            kb.proj(pQ1[:R, :], cqT, w_uq, 6, R, 0, 512)
            kb.proj(pQ2[:R, 0:256], cqT, w_uq, 6, R, 512, 768)
            qf = k.sb("qf", [128, 8, 96], F32)
            qff = qf[:].rearrange("p h d -> p (h d)")
            k.op("act", "activation", out=qff[:R, 0:512], in_=pQ1[:R, :], func=AF.Copy)
            k.op("act", "activation", out=qff[:R, 512:768], in_=pQ2[:R, 0:256], func=AF.Copy)
            if CUT <= 3:
                continue
            kb.head_norm(qf[:R], 8, 96, g_q, R)
            if CUT <= 4:
                continue
            qb = k.sb("qb", [128, 8, 96], BF16)
            k.op("act", "activation", out=qb[:R, :, 0:64], in_=qf[:R, :, 0:64], func=AF.Copy)
            csb = cs[:R, 0:16].unsqueeze(1).to_broadcast([R, 8, 16]); snb = cs[:R, 16:32].unsqueeze(1).to_broadcast([R, 8, 16])
            kb.rope(qb[:R, :, 64:80], qb[:R, :, 80:96], qf[:R, :, 64:80], qf[:R, :, 80:96], csb, snb, R, [8, 16])
            if CUT <= 5:
                continue
            qT = k.sb("qT", [128, 8, 128], BF16)
            kb.transpose_chunks(qT, qb[:].rearrange("p h d -> p (h d)"), 8, R, width=96)
            k.dma("sp", out=S["QT"][:, :, t0:t0 + R].rearrange("h d t -> d h t"), in_=qT[:96, :, :R])
            if CUT <= 6:
                continue
            lat = k.sb("lat", [128, 256], F32)
            kb.rmsnorm(lat[:R], pC[:R, 0:256], g_ckv[:R], 256, R, scr[:R, 0:256])
            k.dma("sp", out=O["lat"][t0:t0 + R, :], in_=lat[:R])
            krf = k.sb("krf", [128, 32], F32); kro = k.sb("kro", [128, 32], F32)
            k.op("act", "activation", out=krf[:R], in_=pC[:R, 256:288], func=AF.Copy)
            kb.rope(kro[:R, 0:16], kro[:R, 16:32], krf[:R, 0:16], krf[:R, 16:32], cs[:R, 0:16], cs[:R, 16:32], R, [16])
            k.dma("sp", out=O["kr"][t0:t0 + R, :], in_=kro[:R])
            if CUT <= 7:
                continue
            latb = k.sb("latb", [128, 256], BF16); latT = k.sb("latT", [128, 2, 128], BF16)
            k.op("pool", "tensor_copy", out=latb[:R], in_=lat[:R])
            kb.transpose_chunks(latT, latb, 2, R)
            pK1 = k.ps("pp", [128, 512], bufs=6); pK2 = k.ps("pp", [128, 512], bufs=6)
            kb.proj(pK1[:R, :], latT, w_ukv, 2, R, 0, 512)
            kb.proj(pK2[:R, :], latT, w_ukv, 2, R, 512, 1024)
            if CUT <= 8:
                continue
            kf = k.sb("kf", [128, 8, 96], F32)
            va = k.sb("va", [128, 8, 66], BF16)
            kvf = k.sb("kvf", [128, 8, 128], F32)
            k.op("act", "activation", out=kvf[:R, 0:4, :].rearrange("p h d -> p (h d)"), in_=pK1[:R, :], func=AF.Copy)
            k.op("act", "activation", out=kvf[:R, 4:8, :].rearrange("p h d -> p (h d)"), in_=pK2[:R, :], func=AF.Copy)
            k.op("dve", "tensor_copy", out=kf[:R, :, 0:64], in_=kvf[:R, :, 0:64])
            k.op("dve", "tensor_copy", out=va[:R, :, 0:64], in_=kvf[:R, :, 64:128])
            k.op("dve", "memset", ap=va[:R, :, 64:66], constant=1.0, _w=[va[:]])
            k.op("dve", "tensor_copy", out=kf[:R, :, 64:96], in_=kro[:R].unsqueeze(1).to_broadcast([R, 8, 32]))
            if CUT <= 9:
                continue
            kb.head_norm(kf[:R], 8, 96, g_kk, R)
            kbb = k.sb("kbb", [128, 768], BF16)
            k.op("act", "activation", out=kbb[:R], in_=kf[:R].rearrange("p h d -> p (h d)"), func=AF.Copy)
            if CUT <= 10:
                continue
            kT = k.sb("kT", [128, 8, 128], BF16)
            kb.transpose_chunks(kT, kbb, 8, R, width=96)
            k.dma("sp", out=S["KT"][:, :, t0:t0 + R].rearrange("h d t -> d h t"), in_=kT[:96, :, :R])
            k.dma("sp", out=S["V"][t0:t0 + R], in_=va[:R])
            if CUT <= 11:
                continue
            uf = k.sb("uf", [128, 512], F32); ub = k.sb("ub", [128, 512], BF16); uT = k.sb("uT", [128, 4, 128], BF16)
            k.op("act", "activation", out=uf[:R], in_=pD[:R, :], func=AF.Copy)
            k.op("dve", "tensor_copy", out=ub[:R], in_=pD[:R, :])
            k.dma("sp", out=S["U"][t0:t0 + R, :], in_=uf[:R])
            kb.transpose_chunks(uT, ub, 4, R)
            k.dma("sp", out=S["UT"][:, :, t0:t0 + R].rearrange("c p t -> p c t"), in_=uT[:, :, :R])
        k.end_phase()

        k.begin_phase()
        if "3" in stages:
            phase_attn_prompt(nc, k, kb, I, O, S)
        k.end_phase()
        k.begin_phase()
        if "P" in stages:
            phase_attn_sample(nc, k, kb, I, O, S)
        k.end_phase()
        k.begin_phase()
        if "2" in stages:
            phase_s5(nc, k, kb, I, O, S)
        k.end_phase()
        k.begin_phase()
        if "4" in stages:
            phase_outproj(nc, k, kb, I, O, S)
        k.end_phase()
        def to_scr(name):
            return lambda i, t0, R: S[name][t0:t0 + R, :]
        def to_out(i, t0, R):
            return O["yp"][t0:t0 + R, :] if i < NPT else O["ys"]
        plan = [("5", lambda: phase_mem(nc, k, kb, I, O, S, 0, S["R1"], to_scr("R2"))),
                ("6", lambda: phase_mlp(nc, k, kb, I, O, S, 0, S["R2"], to_scr("R3"))),
                ("7", lambda: phase_hgrn(nc, k, kb, I, O, S, S["R3"], to_scr("R4"))),
                ("8", lambda: phase_mem(nc, k, kb, I, O, S, 1, S["R4"], to_scr("R5"))),
                ("9", lambda: phase_mlp(nc, k, kb, I, O, S, 1, S["R5"], to_out))]
        for flag, fnc in plan:
            k.begin_phase()
            if flag in stages:
                fnc()
            k.end_phase()
    return nc


def _consts():
    ident = np.eye(128, dtype=np.float32)
    half = 16
    inv = (10000.0 ** (-np.arange(half, dtype=np.float32) / half)).astype(np.float32)
    pos = np.concatenate([np.arange(SEQ), np.tile(16384 + np.arange(4), NSB), np.zeros(64)]).astype(np.float32)
    ang = pos[:, None] * inv[None, :]
    rope = np.concatenate([np.cos(ang), np.sin(ang)], axis=1).astype(np.float32)
    return ident, rope


def make_in_maps(inp):
    ident, rope = _consts()
    f = lambda a: np.ascontiguousarray(np.asarray(a, dtype=np.float32))
    shared = {"ident": ident, "rope": rope}
    for nm in ["norm_mix", "norm_mem", "norm_memsrc", "norm_mlp", "w_mem_q", "w_mem_k", "w_mem_v", "w_mem_o",
               "mem_q_gain", "mem_k_gain", "w_mlp_up", "w_mlp_down"]:
        shared[nm] = f(inp[nm])
    shared["w_in_even"] = f(inp["w_in_even"][0]); shared["mla_cq_norm"] = f(inp["mla_cq_norm"][0])
    shared["mla_ckv_norm"] = f(inp["mla_ckv_norm"][0]); shared["w_mla_uq"] = f(inp["w_mla_uq"][0])
    shared["w_mla_ukv"] = f(inp["w_mla_ukv"][0])
    shared["q_gain"] = f(np.concatenate([inp["mla_qn_nope"][0], inp["mla_qn_rope"][0], inp["mla_qn_rope"][0]]))
    shared["k_gain"] = f(np.concatenate([inp["mla_kn_nope"][0], inp["mla_kn_rope"][0], inp["mla_kn_rope"][0]]))
    shared["s5_lre"] = f(inp["s5_lambda_re"][0]); shared["s5_lim"] = f(inp["s5_lambda_im"][0]); shared["s5_ls"] = f(inp["s5_log_step"][0])
    shared["s5_bre"] = f(inp["s5_b_re"][0]); shared["s5_bim"] = f(inp["s5_b_im"][0]); shared["s5_cre"] = f(inp["s5_c_re"][0]); shared["s5_cim"] = f(inp["s5_c_im"][0])
    shared["s5_d"] = f(inp["s5_d"][0]); shared["s5_wglu"] = f(inp["s5_w_glu"][0]); shared["s5_bglu"] = f(inp["s5_b_glu"][0])
    shared["tri"] = np.ascontiguousarray(np.triu(np.ones((128, 128), np.float32)))
    ms = np.zeros((64, NSB, 8, 4), np.float32)
    for bb in range(NSB):
        for tk in range(4):
            for tq in range(tk, 4):
                ms[bb * 4 + tk, bb, :, tq] = 1.0
    shared["maskS"] = np.ascontiguousarray(ms.reshape(64, NSB * 32))
    shared["w_out_even"] = f(inp["w_out_even"][0])
    if "cache_lat" in DECLARED:
        shared["cache_lat"] = np.ascontiguousarray(np.asarray(inp["cache_mla_latent"], np.float32).reshape(NPHYS, 128, 256))
        shared["cache_kr"] = np.ascontiguousarray(np.asarray(inp["cache_mla_krope"], np.float32).reshape(NPHYS, 128, 32))
    shared["hgrn_on"] = f(inp["hgrn_out_norm"][0]); shared["w_in_odd"] = f(inp["w_in_odd"][0]); shared["w_out_odd"] = f(inp["w_out_odd"][0])
    shared["hgrn_lb"] = f(inp["hgrn_lower_bounds"])
    t_ = np.arange(128)
    shared["rmask64"] = np.ascontiguousarray(np.tile((t_ % 64 != 0).astype(np.float32)[None, :], (128, 1)))
    shared["rmask4"] = np.ascontiguousarray(np.tile((np.arange(64) % 4 != 0).astype(np.float32)[None, :], (128, 1)))
    shared["bd64"] = np.ascontiguousarray(((t_[:, None] // 64 == t_[None, :] // 64) & (t_[:, None] <= t_[None, :])).astype(np.float32))
    t4 = np.arange(64)
    shared["bd4"] = np.ascontiguousarray(((t4[:, None] // 4 == t4[None, :] // 4) & (t4[:, None] <= t4[None, :])).astype(np.float32))
    shared["qmask"] = np.ascontiguousarray(np.tile((np.arange(2)[:, None] == (t_[None, :] // 64)).astype(np.float32).reshape(1, 256), (128, 1)))
    shared["bmask"] = np.ascontiguousarray(np.tile((np.arange(NSB)[:, None] == (t4[None, :] // 4)).astype(np.float32).reshape(1, 1024), (128, 1)))
    shared["rowmask"] = np.ascontiguousarray((t4[:, None] // 4 == np.arange(NSB)[None, :]).astype(np.float32))
    shared["ramp"] = np.ascontiguousarray(np.tile(np.arange(1, 129, dtype=np.float32)[None, :], (128, 1)))
    maps = []
    for c in range(NCORES):
        m = dict(shared)
        m["s5_h0re"] = f(inp["state_s5_re"][0, c * NSB:(c + 1) * NSB]).reshape(NSB, 2048)
        m["s5_h0im"] = f(inp["state_s5_im"][0, c * NSB:(c + 1) * NSB]).reshape(NSB, 2048)
        m["xp"] = f(inp["x_prompt"][c]); m["xs"] = f(inp["x_sample"][c * NSB:(c + 1) * NSB]).reshape(NST, D)
        m["mem"] = f(inp["mem_prompt"][c])
        m["cmk"] = f(inp["cache_mem_k"][:, c * NSB:(c + 1) * NSB]).reshape(2, NSB, 256, 512)
        m["cmv"] = f(inp["cache_mem_v"][:, c * NSB:(c + 1) * NSB]).reshape(2, NSB, 256, 512)
        m["hg0"] = f(inp["state_hgrn"][0, c * NSB:(c + 1) * NSB])
        if "ptab" in DECLARED:
            m["ptab"] = np.ascontiguousarray(np.asarray(inp["page_table"][c * NSB:(c + 1) * NSB], np.int32))
        maps.append(m)
    return maps


STAGES = "M123P456789"


def kernel(**inputs):
    nc = build(STAGES)
    maps = make_in_maps(inputs)
    res = run_bass_kernel_spmd(nc, maps, core_ids=list(range(NCORES)))
    R = res.results
    f32 = np.float32
    st = lambda nm: np.stack([np.asarray(R[c][nm], f32) for c in range(NCORES)])
    lat = st("o_lat"); kr = st("o_kr")
    y_p = st("o_yp")
    y_s = st("o_ys").reshape(128, 4, D)
    lat_p = lat[:, :SEQ].reshape(8, 1, SEQ, 256); kr_p = kr[:, :SEQ].reshape(8, 1, SEQ, 32)
    lat_s = lat[:, SEQ:SEQ + NST].reshape(128, 1, 4, 256); kr_s = kr[:, SEQ:SEQ + NST].reshape(128, 1, 4, 32)
    s5pr = st("o_s5p_re").reshape(1, 8, 32, 64); s5pi = st("o_s5p_im").reshape(1, 8, 32, 64)
    s5sr = st("o_s5s_re").reshape(1, 128, 32, 64); s5si = st("o_s5s_im").reshape(1, 128, 32, 64)
    hgp = st("o_hgp").reshape(1, 8, 8, 128, 128)
    hgs = st("o_hgs").reshape(1, 128, 8, 128, 128)
    mk = np.stack([np.asarray(R[c]["o_mk"], f32) for c in range(NCORES)], 1).reshape(2, 8, 256, 4, 128)
    mv = np.stack([np.asarray(R[c]["o_mv"], f32) for c in range(NCORES)], 1).reshape(2, 8, 256, 4, 128)
    return (y_p, y_s, lat_p, kr_p, s5pr, s5pi, hgp, mk, mv, lat_s, kr_s, s5sr, s5si, hgs)
```
